# Optimizing a Trainium2 kernel written in Bass

```python
import math
import jax, jax.numpy as jnp
from jax import lax
import numpy as np

D_MODEL = 2048
BATCH = 32
SEQ = 256
DEPTH = 4
DEC_BATCH = 4
DEC_SEQ = 4096
PAST_LEN = 512

GRID_W = 64
BLOCK = 128
HEAD_DIM = 64
N_FREQ = HEAD_DIM // 4
ROPE_BASE = 10000.0
ATTN_SCALE = HEAD_DIM ** -0.5
RMS_EPS = 1e-6
N_MOD = 6
N_BRANCH = 4
BRANCH_W = D_MODEL // 4
LRU_W = BRANCH_W
LRU_BLOCKS = 8
LRU_BW = LRU_W // LRU_BLOCKS
LRU_C = 8.0
CONV_W = 4
CONV_PAD = (1, 2)
WIN_HEADS = BRANCH_W // HEAD_DIM
WIN_KV = 2
WIN_G = WIN_HEADS // WIN_KV
WINDOW = 128
GRID_HEADS = BRANCH_W // HEAD_DIM
GRID_KV = 2
GRID_G = GRID_HEADS // GRID_KV
DIFF_HEADS = BRANCH_W // (2 * HEAD_DIM)
D_FF = 4 * D_MODEL
IN_SIZES = (LRU_W, LRU_W,
            WIN_HEADS * HEAD_DIM, WIN_KV * HEAD_DIM, WIN_KV * HEAD_DIM,
            GRID_HEADS * HEAD_DIM, GRID_KV * HEAD_DIM, GRID_KV * HEAD_DIM,
            DIFF_HEADS * 2 * HEAD_DIM, DIFF_HEADS * 2 * HEAD_DIM, DIFF_HEADS * 2 * HEAD_DIM)
D_IN = sum(IN_SIZES)

kernel_name = 'hybrid_flow_prefix_trunk_step'


def rms_norm(x, g, eps=RMS_EPS):
    xf = x.astype(jnp.float32)
    y = xf * lax.rsqrt(jnp.mean(xf * xf, axis=-1, keepdims=True) + eps)
    return y.astype(x.dtype) * g


def modulation(cond, w_ada, b_ada):
    m = jax.nn.silu(cond) @ w_ada + b_ada
    return m.reshape(m.shape[:-1] + (N_MOD, D_MODEL))


def grid_rope(n_tokens):
    n_rows = n_tokens // GRID_W
    row = jnp.repeat(jnp.arange(n_rows, dtype=jnp.float32), GRID_W)
    col = jnp.tile(jnp.arange(GRID_W, dtype=jnp.float32), n_rows)
    inv = ROPE_BASE ** (-jnp.arange(N_FREQ, dtype=jnp.float32) / N_FREQ)
    ang = jnp.stack([row[:, None] * inv, col[:, None] * inv], axis=1)
    return jnp.cos(ang), jnp.sin(ang)


def apply_rope(x, cos, sin):
    shp = x.shape
    xr = x.reshape(shp[0], shp[1], -1, 2, 2, N_FREQ)
    x1, x2 = xr[..., 0, :], xr[..., 1, :]
    c = cos[None, :, None].astype(x.dtype)
    s = sin[None, :, None].astype(x.dtype)
    out = jnp.stack([x1 * c - x2 * s, x2 * c + x1 * s], axis=-2)
    return out.reshape(shp)


def sweep_query_blocks(fn, q):
    b, t = q.shape[0], q.shape[1]
    nb = t // BLOCK
    qb = jnp.moveaxis(q.reshape((b, nb, BLOCK) + q.shape[2:]), 1, 0)
    out = lax.map(fn, qb)
    return jnp.moveaxis(out, 0, 1).reshape((b, t) + out.shape[3:])


def gqa_dense(qb, k, v, sink):
    s = jnp.einsum('bqhgd,bkhd->bhgqk', qb, k, preferred_element_type=jnp.float32) * ATTN_SCALE
    n_keys = k.shape[1]
    if sink is not None:
        sink_col = jnp.broadcast_to(sink.astype(jnp.float32)[None, :, :, None, None], s.shape[:-1] + (1,))
        s = jnp.concatenate([s, sink_col], axis=-1)
    p = jax.nn.softmax(s, axis=-1)[..., :n_keys]
    return jnp.einsum('bhgqk,bkhd->bqhgd', p.astype(v.dtype), v)


def banded_window_attention(q, k, v, kc, vc, sink):
    b, t = q.shape[0], q.shape[1]
    nb = t // BLOCK
    n_ctx = kc.shape[1]
    qb = q.reshape(b, nb, BLOCK, WIN_KV, WIN_G, HEAD_DIM)
    pad = ((0, 0), (BLOCK, BLOCK), (0, 0), (0, 0))
    kp = jnp.pad(k, pad).reshape(b, nb + 2, BLOCK, WIN_KV, HEAD_DIM)
    vp = jnp.pad(v, pad).reshape(b, nb + 2, BLOCK, WIN_KV, HEAD_DIM)
    kw = jnp.concatenate([kp[:, :-2], kp[:, 1:-1], kp[:, 2:]], axis=2)
    vw = jnp.concatenate([vp[:, :-2], vp[:, 1:-1], vp[:, 2:]], axis=2)
    qpos = jnp.arange(t).reshape(nb, BLOCK)
    kpos = jnp.arange(nb)[:, None] * BLOCK - BLOCK + jnp.arange(3 * BLOCK)[None, :]
    mask = ((jnp.abs(qpos[:, :, None] - kpos[:, None, :]) <= WINDOW)
            & (kpos[:, None, :] >= 0) & (kpos[:, None, :] < t))
    s_loc = jnp.einsum('bnqhgd,bnkhd->bnhgqk', qb, kw, preferred_element_type=jnp.float32) * ATTN_SCALE
    s_loc = jnp.where(mask[None, :, None, None], s_loc, -jnp.inf)
    s_ctx = jnp.einsum('bnqhgd,bshd->bnhgqs', qb, kc, preferred_element_type=jnp.float32) * ATTN_SCALE
    sink_col = jnp.broadcast_to(sink.astype(jnp.float32)[None, None, :, :, None, None], s_loc.shape[:-1] + (1,))
    p = jax.nn.softmax(jnp.concatenate([s_loc, s_ctx, sink_col], axis=-1), axis=-1)
    p_loc = p[..., :3 * BLOCK].astype(v.dtype)
    p_ctx = p[..., 3 * BLOCK:3 * BLOCK + n_ctx].astype(v.dtype)
    out = (jnp.einsum('bnhgqk,bnkhd->bnqhgd', p_loc, vw)
           + jnp.einsum('bnhgqs,bshd->bnqhgd', p_ctx, vc))
    return out.reshape(b, t, WIN_KV * WIN_G * HEAD_DIM)


def diff_dense(qb, k, v, lam):
    s = jnp.einsum('bqhjd,bkhjd->bhjqk', qb, k, preferred_element_type=jnp.float32) * ATTN_SCALE
    p = jax.nn.softmax(s, axis=-1)
    w = p[:, :, 0] - lam * p[:, :, 1]
    return jnp.einsum('bhqk,bkhe->bqhe', w.astype(v.dtype), v)


def lru_combine(left, right):
    a_l, b_l = left
    a_r, b_r = right
    return a_l * a_r, a_r * b_l + b_r


def linear_scan(a, b, h0, reverse):
    if h0 is not None:
        idx = -1 if reverse else 0
        b = b.at[:, idx].add(a[:, idx] * h0)
    _, h = lax.associative_scan(lru_combine, (a, b), reverse=reverse, axis=1)
    return h


def lru_branch(xa, ya, conv_w, conv_b, wr, br, wi, bi, lam, h0):
    b, t, _ = xa.shape
    u = lax.conv_general_dilated(xa, conv_w[:, None, :], window_strides=(1,), padding=[CONV_PAD],
                                 dimension_numbers=('NWC', 'WIO', 'NWC'),
                                 feature_group_count=LRU_W) + conv_b
    ub = u.reshape(b, t, LRU_BLOCKS, LRU_BW)
    r = jax.nn.sigmoid(jnp.einsum('btnc,kncd->btknd', ub, wr, preferred_element_type=jnp.float32)
                       .reshape(b, t, 2, LRU_W) + br)
    ig = jax.nn.sigmoid(jnp.einsum('btnc,kncd->btknd', ub, wi, preferred_element_type=jnp.float32)
                        .reshape(b, t, 2, LRU_W) + bi)
    log_a = -LRU_C * r * jax.nn.softplus(-lam.astype(jnp.float32))
    a = jnp.exp(log_a)
    inp = jnp.sqrt(-jnp.expm1(2.0 * log_a)) * ig * u.astype(jnp.float32)[:, :, None, :]
    h0f = None if h0 is None else h0[:, 0].astype(jnp.float32)
    h0b = None if h0 is None else h0[:, 1].astype(jnp.float32)
    hf = linear_scan(a[:, :, 0], inp[:, :, 0], h0f, False)
    hb = linear_scan(a[:, :, 1], inp[:, :, 1], h0b, True)
    out = (hf + hb).astype(xa.dtype) * jax.nn.gelu(ya)
    final = None if h0 is not None else jnp.stack([hf[:, -1], hb[:, 0]], axis=1).astype(xa.dtype)
    return out, final


def window_branch(q, k, v, qn, kn, sink, rope, ctx_kv):
    b, t, _ = q.shape
    q = rms_norm(q.reshape(b, t, WIN_HEADS, HEAD_DIM), qn)
    k = rms_norm(k.reshape(b, t, WIN_KV, HEAD_DIM), kn)
    v = v.reshape(b, t, WIN_KV, HEAD_DIM)
    sink = sink.reshape(WIN_KV, WIN_G)
    if ctx_kv is None:
        o = sweep_query_blocks(lambda qb: gqa_dense(qb, k, v, sink),
                               q.reshape(b, t, WIN_KV, WIN_G, HEAD_DIM))
    else:
        cos, sin = rope
        qr = apply_rope(q, cos, sin).reshape(b, t, WIN_KV, WIN_G, HEAD_DIM)
        o = banded_window_attention(qr, apply_rope(k, cos, sin), v, ctx_kv[0], ctx_kv[1], sink)
    return o.reshape(b, t, BRANCH_W), k, v


def grid_branch(q, k, v, qn, kn, rope, ctx_kv):
    b, t, _ = q.shape
    q = rms_norm(q.reshape(b, t, GRID_HEADS, HEAD_DIM), qn)
    k = rms_norm(k.reshape(b, t, GRID_KV, HEAD_DIM), kn)
    v = v.reshape(b, t, GRID_KV, HEAD_DIM)
    if ctx_kv is None:
        keys, vals = k, v
    else:
        cos, sin = rope
        q = apply_rope(q, cos, sin)
        keys = jnp.concatenate([apply_rope(k, cos, sin), ctx_kv[0]], axis=1)
        vals = jnp.concatenate([v, ctx_kv[1]], axis=1)
    o = sweep_query_blocks(lambda qb: gqa_dense(qb, keys, vals, None),
                           q.reshape(b, t, GRID_KV, GRID_G, HEAD_DIM))
    return o.reshape(b, t, BRANCH_W), k, v


def diff_branch(q, k, v, qn, kn, lq1, lk1, lq2, lk2, out_g, layer, rope, ctx_kv):
    b, t, _ = q.shape
    q = rms_norm(q.reshape(b, t, DIFF_HEADS, 2, HEAD_DIM), qn)
    k = rms_norm(k.reshape(b, t, DIFF_HEADS, 2, HEAD_DIM), kn)
    v = v.reshape(b, t, DIFF_HEADS, 2 * HEAD_DIM)
    lam_init = 0.8 - 0.6 * math.exp(-0.3 * layer)
    lam = (jnp.exp(jnp.sum(lq1.astype(jnp.float32) * lk1.astype(jnp.float32)))
           - jnp.exp(jnp.sum(lq2.astype(jnp.float32) * lk2.astype(jnp.float32))) + lam_init)
    if ctx_kv is None:
        keys, vals = k, v
    else:
        cos, sin = rope
        q = apply_rope(q, cos, sin)
        keys = jnp.concatenate([apply_rope(k, cos, sin), ctx_kv[0]], axis=1)
        vals = jnp.concatenate([v, ctx_kv[1]], axis=1)
    o = sweep_query_blocks(lambda qb: diff_dense(qb, keys, vals, lam), q)
    o = rms_norm(o, out_g) * (1.0 - lam_init)
    return o.reshape(b, t, BRANCH_W), k, v


def trunk_layer(x, mod, lp, layer, rope, ctx):
    b, t, _ = x.shape
    shift1, scale1, gate1, shift2, scale2, gate2 = (mod[..., i, :] for i in range(N_MOD))
    h = rms_norm(x, lp['norm1_g']) * (1 + scale1) + shift1
    z = h @ lp['w_in']
    points = [int(p) for p in np.cumsum(IN_SIZES)[:-1]]
    xa, ya, qw, kw, vw, qg, kg, vg, qd, kd, vd = jnp.split(z, points, axis=-1)
    if ctx is None:
        win_ctx = grid_ctx = diff_ctx = lru_h0 = None
    else:
        win_ctx, grid_ctx, diff_ctx, lru_h0 = (ctx[0], ctx[1]), (ctx[2], ctx[3]), (ctx[4], ctx[5]), ctx[6]
    a_out, lru_final = lru_branch(xa, ya, lp['conv_w'], lp['conv_b'], lp['lru_wr'], lp['lru_br'],
                                  lp['lru_wi'], lp['lru_bi'], lp['lru_lambda'], lru_h0)
    w_out, k_w, v_w = window_branch(qw, kw, vw, lp['win_qn'], lp['win_kn'], lp['win_sink'], rope, win_ctx)
    g_out, k_g, v_g = grid_branch(qg, kg, vg, lp['grid_qn'], lp['grid_kn'], rope, grid_ctx)
    d_out, k_d, v_d = diff_branch(qd, kd, vd, lp['diff_qn'], lp['diff_kn'], lp['diff_lq1'], lp['diff_lk1'],
                                  lp['diff_lq2'], lp['diff_lk2'], lp['diff_out_g'], layer, rope, diff_ctx)
    branches = jnp.stack([a_out, w_out, g_out, d_out], axis=2)
    proj = jnp.einsum('btnc,ncd->btnd', branches, lp['w_branch'])
    gates = jax.nn.sigmoid(h @ lp['w_gate'] + lp['b_gate']).reshape(b, t, N_BRANCH, D_MODEL)
    x = x + gate1 * (jnp.sum(gates * proj, axis=2) @ lp['w_o'])
    h2 = rms_norm(x, lp['norm2_g']) * (1 + scale2) + shift2
    ff = jnp.square(jax.nn.relu(h2 @ lp['w_ff1'] + lp['b_ff1'])) @ lp['w_ff2'] + lp['b_ff2']
    x = x + gate2 * ff
    if ctx is None:
        return x, (k_w, v_w, k_g, v_g, k_d, v_d, lru_final)
    return x, None


def setup_inputs(seed: int = 0) -> dict:
    key = jax.random.key(seed)
    ks = iter(jax.random.split(key, 48))

    def nrm(shape, scale):
        return jax.random.normal(next(ks), shape, jnp.float32) * scale

    def gain(shape):
        return 1.0 + nrm(shape, 0.02)

    a_base = jax.random.uniform(next(ks), (DEPTH, 2, LRU_W), jnp.float32, minval=0.9, maxval=0.999)
    s_base = a_base ** (1.0 / LRU_C)
    lru_lambda = jnp.log(s_base) - jnp.log1p(-s_base)
    return {
        'x_prompt': nrm((BATCH, SEQ, D_MODEL), 1.0),
        'x_sample': nrm((DEC_BATCH, DEC_SEQ, D_MODEL), 1.0),
        'cache_win_k': nrm((DEC_BATCH, DEPTH, PAST_LEN, WIN_KV, HEAD_DIM), 1.0),
        'cache_win_v': nrm((DEC_BATCH, DEPTH, PAST_LEN, WIN_KV, HEAD_DIM), 1.0),
        'cache_grid_k': nrm((DEC_BATCH, DEPTH, PAST_LEN, GRID_KV, HEAD_DIM), 1.0),
        'cache_grid_v': nrm((DEC_BATCH, DEPTH, PAST_LEN, GRID_KV, HEAD_DIM), 1.0),
        'cache_diff_k': nrm((DEC_BATCH, DEPTH, PAST_LEN, DIFF_HEADS, 2, HEAD_DIM), 1.0),
        'cache_diff_v': nrm((DEC_BATCH, DEPTH, PAST_LEN, DIFF_HEADS, 2 * HEAD_DIM), 1.0),
        'state_lru': nrm((DEC_BATCH, DEPTH, 2, LRU_W), 0.5),
        'c': nrm((DEC_BATCH, D_MODEL), 1.0),
        'c_ctx': nrm((D_MODEL,), 1.0),
        'w_ada': nrm((DEPTH, D_MODEL, N_MOD * D_MODEL), 0.5 * D_MODEL ** -0.5),
        'b_ada': nrm((DEPTH, N_MOD * D_MODEL), 0.02),
        'norm1_g': gain((DEPTH, D_MODEL)),
        'norm2_g': gain((DEPTH, D_MODEL)),
        'w_in': nrm((DEPTH, D_MODEL, D_IN), D_MODEL ** -0.5),
        'conv_w': nrm((DEPTH, CONV_W, LRU_W), CONV_W ** -0.5),
        'conv_b': nrm((DEPTH, LRU_W), 0.02),
        'lru_wr': nrm((DEPTH, 2, LRU_BLOCKS, LRU_BW, LRU_BW), LRU_BW ** -0.5),
        'lru_br': nrm((DEPTH, 2, LRU_W), 0.1),
        'lru_wi': nrm((DEPTH, 2, LRU_BLOCKS, LRU_BW, LRU_BW), LRU_BW ** -0.5),
        'lru_bi': nrm((DEPTH, 2, LRU_W), 0.1),
        'lru_lambda': lru_lambda,
        'win_qn': gain((DEPTH, HEAD_DIM)),
        'win_kn': gain((DEPTH, HEAD_DIM)),
        'win_sink': nrm((DEPTH, WIN_HEADS), 0.5),
        'grid_qn': gain((DEPTH, HEAD_DIM)),
        'grid_kn': gain((DEPTH, HEAD_DIM)),
        'diff_qn': gain((DEPTH, HEAD_DIM)),
        'diff_kn': gain((DEPTH, HEAD_DIM)),
        'diff_lq1': nrm((DEPTH, HEAD_DIM), 0.1),
        'diff_lk1': nrm((DEPTH, HEAD_DIM), 0.1),
        'diff_lq2': nrm((DEPTH, HEAD_DIM), 0.1),
        'diff_lk2': nrm((DEPTH, HEAD_DIM), 0.1),
        'diff_out_g': gain((DEPTH, 2 * HEAD_DIM)),
        'w_branch': nrm((DEPTH, N_BRANCH, BRANCH_W, D_MODEL), BRANCH_W ** -0.5),
        'w_gate': nrm((DEPTH, D_MODEL, N_BRANCH * D_MODEL), D_MODEL ** -0.5),
        'b_gate': nrm((DEPTH, N_BRANCH * D_MODEL), 0.02),
        'w_o': nrm((DEPTH, D_MODEL, D_MODEL), D_MODEL ** -0.5),
        'w_ff1': nrm((DEPTH, D_MODEL, D_FF), D_MODEL ** -0.5),
        'b_ff1': nrm((DEPTH, D_FF), 0.02),
        'w_ff2': nrm((DEPTH, D_FF, D_MODEL), D_FF ** -0.5),
        'b_ff2': nrm((DEPTH, D_MODEL), 0.02),
    }


def reference(x_prompt, x_sample, cache_win_k, cache_win_v, cache_grid_k, cache_grid_v, cache_diff_k,
              cache_diff_v, state_lru, c, c_ctx, w_ada, b_ada, norm1_g, norm2_g, w_in, conv_w, conv_b,
              lru_wr, lru_br, lru_wi, lru_bi, lru_lambda, win_qn, win_kn, win_sink, grid_qn, grid_kn,
              diff_qn, diff_kn, diff_lq1, diff_lk1, diff_lq2, diff_lk2, diff_out_g, w_branch, w_gate,
              b_gate, w_o, w_ff1, b_ff1, w_ff2, b_ff2):
    rope = grid_rope(x_sample.shape[1])
    y_p, y_s = x_prompt, x_sample
    wk, wv, gk, gv, dk, dv, ls = [], [], [], [], [], [], []
    for l in range(DEPTH):
        lp = {
            'norm1_g': norm1_g[l], 'norm2_g': norm2_g[l], 'w_in': w_in[l],
            'conv_w': conv_w[l], 'conv_b': conv_b[l],
            'lru_wr': lru_wr[l], 'lru_br': lru_br[l], 'lru_wi': lru_wi[l], 'lru_bi': lru_bi[l],
            'lru_lambda': lru_lambda[l],
            'win_qn': win_qn[l], 'win_kn': win_kn[l], 'win_sink': win_sink[l],
            'grid_qn': grid_qn[l], 'grid_kn': grid_kn[l],
            'diff_qn': diff_qn[l], 'diff_kn': diff_kn[l], 'diff_lq1': diff_lq1[l], 'diff_lk1': diff_lk1[l],
            'diff_lq2': diff_lq2[l], 'diff_lk2': diff_lk2[l], 'diff_out_g': diff_out_g[l],
            'w_branch': w_branch[l], 'w_gate': w_gate[l], 'b_gate': b_gate[l], 'w_o': w_o[l],
            'w_ff1': w_ff1[l], 'b_ff1': b_ff1[l], 'w_ff2': w_ff2[l], 'b_ff2': b_ff2[l],
        }
        y_p, ctx_l = trunk_layer(y_p, modulation(c_ctx, w_ada[l], b_ada[l]), lp, l, None, None)
        wk.append(ctx_l[0]); wv.append(ctx_l[1]); gk.append(ctx_l[2]); gv.append(ctx_l[3])
        dk.append(ctx_l[4]); dv.append(ctx_l[5]); ls.append(ctx_l[6])
        cached = (cache_win_k[:, l], cache_win_v[:, l], cache_grid_k[:, l], cache_grid_v[:, l],
                  cache_diff_k[:, l], cache_diff_v[:, l], state_lru[:, l])
        y_s, _ = trunk_layer(y_s, modulation(c, w_ada[l], b_ada[l])[:, None], lp, l, rope, cached)
    new_win_k = jnp.stack(wk, axis=1)
    new_win_v = jnp.stack(wv, axis=1)
    new_grid_k = jnp.stack(gk, axis=1)
    new_grid_v = jnp.stack(gv, axis=1)
    new_diff_k = jnp.stack(dk, axis=1)
    new_diff_v = jnp.stack(dv, axis=1)
    new_lru_state = jnp.stack(ls, axis=1)
    return (y_p, y_s, new_win_k, new_win_v, new_grid_k, new_grid_v, new_diff_k, new_diff_v, new_lru_state)
```

```python
import contextlib
import math
import numpy as np
import ml_dtypes
import concourse.bass as bass
import concourse.mybir as mybir
from concourse.bass_utils import run_bass_kernel_spmd

F32 = mybir.dt.float32
BF16 = mybir.dt.bfloat16
AF = mybir.ActivationFunctionType
ALU = mybir.AluOpType
AX = mybir.AxisListType

D = 2048
KC = 16
TC = 512
NCTX = 512
EPS = 1e-6
GRID_W = 64
WMAP = {"ada": (2048, 12288), "in": (2048, 4096), "gate": (2048, 8192), "br": (2048, 2048),
        "o": (2048, 2048), "ff1": (2048, 8192), "ff2": (8192, 2048)}
WNAME = {"ada": "w_ada", "in": "w_in", "gate": "w_gate", "br": "w_branch", "o": "w_o", "ff1": "w_ff1",
         "ff2": "w_ff2"}


class Res:
    __slots__ = ("name", "w", "r")

    def __init__(self, name):
        self.name = name
        self.w = None
        self.r = {}


class KB:
    def __init__(self, nc, stack):
        self.nc = nc
        self.stack = stack
        self.eng = {"pe": nc.tensor, "act": nc.scalar, "dve": nc.vector, "pool": nc.gpsimd, "sp": nc.sync}
        self.esem = {e: stack.enter_context(nc.semaphore("es_" + e)) for e in self.eng}
        self.cnt = {e: 0 for e in self.eng}
        self.seen = {e: {} for e in self.eng}
        self.res = {}
        self.dsems = {}
        self.n_inst = 0

    def R(self, *key):
        r = self.res.get(key)
        if r is None:
            r = Res(key)
            self.res[key] = r
        return r

    def dsem(self, name):
        d = self.dsems.get(name)
        if d is None:
            d = [self.stack.enter_context(self.nc.semaphore("ds_" + name)), 0]
            self.dsems[name] = d
        return d

    def _collect(self, e, reads, writes):
        waits = {}

        def need(tok, war=False):
            if tok is None:
                return
            kind, key, val = tok
            if kind == "eng":
                if key == e and (war or e == "pe"):
                    return
                waits[("eng", key)] = max(waits.get(("eng", key), 0), val)
            else:
                waits[("dma", key)] = 1
        for r in reads:
            need(r.w)
        for w in writes:
            need(w.w)
            for t in w.r.values():
                need(t, war=True)
        return waits

    def _emit_waits(self, e, waits):
        eng = self.eng[e]
        for (kind, key) in waits:
            if kind == "eng":
                val = waits[(kind, key)]
                sem = self.esem[key]
            else:
                d = self.dsems[key]
                val = d[1]
                sem = d[0]
            if self.seen[e].get((kind, key), 0) >= val:
                continue
            eng.wait_ge(sem, val)
            self.seen[e][(kind, key)] = val

    def op(self, e, fn, reads=(), writes=()):
        if DEAD[0]:
            return None
        waits = self._collect(e, reads, writes)
        self._emit_waits(e, waits)
        ins = fn(self.eng[e])
        self.cnt[e] += 1
        ins.then_inc(self.esem[e], 1)
        tok = ("eng", e, self.cnt[e])
        for r in reads:
            r.r[("eng", e)] = tok
        for w in writes:
            w.w = tok
            w.r = {}
        self.n_inst += 1
        return tok

    def dma(self, q, out, in_, reads=(), writes=(), sem=None, **kw):
        if DEAD[0]:
            return None
        waits = self._collect(q, reads, writes)
        self._emit_waits(q, waits)
        d = self.dsem(sem)
        ins = self.eng[q].dma_start(out=out, in_=in_, **kw)
        d[1] += 16
        ins.then_inc(d[0], 16)
        tok = ("dma", sem, d[1])
        for r in reads:
            r.r[("dma", sem)] = tok
        for w in writes:
            w.w = tok
            w.r = {}
        self.n_inst += 1
        return tok

    def barrier(self):
        if DEAD[0]:
            return
        for e in self.eng:
            eng = self.eng[e]
            for o in self.eng:
                if o == e or self.cnt[o] == 0:
                    continue
                if self.seen[e].get(("eng", o), 0) >= self.cnt[o]:
                    continue
                eng.wait_ge(self.esem[o], self.cnt[o])
                self.seen[e][("eng", o)] = self.cnt[o]
            for name, d in self.dsems.items():
                if name.startswith("cast") or name == "cstv" or d[1] == 0:
                    continue
                if self.seen[e].get(("dma", name), 0) >= d[1]:
                    continue
                eng.wait_ge(d[0], d[1])
                self.seen[e][("dma", name)] = d[1]

    def finish(self):
        sp = self.eng["sp"]
        for name, d in self.dsems.items():
            if d[1] > 0:
                sp.wait_ge(d[0], d[1])
        for e in self.eng:
            if e != "sp" and self.cnt[e] > 0:
                sp.wait_ge(self.esem[e], self.cnt[e])


class _Stop(Exception):
    pass


STOP = [99]


DEAD = [False]
DBG = [False]
LAST = [None]


def chk(n):
    if STOP[0] == n:
        DEAD[0] = True


def build_nc(NS, NPS, DEPTH):
    NP = NPS * 256
    NT = NS + NP
    NSCH = NS // TC
    NCH = NT // TC
    KOFFP = NS + NCTX
    TK = NS + NCTX + NP

    nc = bass.Bass("TRN2", target_bir_lowering=False)
    DEAD[0] = False

    def din(name, shape, dt=F32):
        return nc.dram_tensor(name, list(shape), dt, kind="ExternalInput").ap()

    def dout(name, shape, dt=F32):
        return nc.dram_tensor(name, list(shape), dt, kind="ExternalOutput").ap()

    def dscr(name, shape, dt=BF16):
        return nc.dram_tensor(name, list(shape), dt, kind="Internal").ap()

    xs = din("xs", [NS, D]); xp = din("xp", [NP, D])
    cwk = din("cwk", [DEPTH, NCTX, 128]); cwv = din("cwv", [DEPTH, NCTX, 128])
    cgk = din("cgk", [DEPTH, NCTX, 128]); cgv = din("cgv", [DEPTH, NCTX, 128])
    cdk = din("cdk", [DEPTH, NCTX, 512]); cdv = din("cdv", [DEPTH, NCTX, 512])
    slru = din("slru", [DEPTH, 2, 512])
    cond = din("cond", [2, D])
    W = {}
    for kk, (K_, N_) in WMAP.items():
        if kk == "br":
            W[kk] = din("w_branch", [DEPTH, 4, 512, 2048])
        else:
            W[kk] = din(WNAME[kk], [DEPTH, K_, N_])
    b_ada = din("b_ada", [DEPTH, 12288]); norm1_g = din("norm1_g", [DEPTH, D]); norm2_g = din("norm2_g", [DEPTH, D])
    conv_w = din("conv_w", [DEPTH, 4, 512]); conv_b = din("conv_b", [DEPTH, 512])
    lru_wr = din("lru_wr", [DEPTH, 2, 8, 64, 64]); lru_br = din("lru_br", [DEPTH, 2, 512])
    lru_wi = din("lru_wi", [DEPTH, 2, 8, 64, 64]); lru_bi = din("lru_bi", [DEPTH, 2, 512])
    lru_lambda = din("lru_lambda", [DEPTH, 2, 512])
    gains = {n: din(n, [DEPTH, 64]) for n in ("win_qn", "win_kn", "grid_qn", "grid_kn", "diff_qn", "diff_kn",
                                           "diff_lq1", "diff_lk1", "diff_lq2", "diff_lk2")}
    win_sink = din("win_sink", [DEPTH, 8]); diff_out_g = din("diff_out_g", [DEPTH, 128])
    b_gate = din("b_gate", [DEPTH, 8192]); b_ff1 = din("b_ff1", [DEPTH, 8192]); b_ff2 = din("b_ff2", [DEPTH, D])
    ropeC = din("ropeC", [128, NS]); ropeS = din("ropeS", [128, NS])
    cmat = din("cmat", [5, 128, 128])
    cmask = din("cmask", [2, 128, 512], BF16)
    cidb = din("cidb", [128, 128], BF16)

    y_s = dout("y_s", [NS, D]); y_p = dout("y_p", [NP, D])
    o_wk = dout("o_wk", [NPS, DEPTH, 256, 128]); o_wv = dout("o_wv", [NPS, DEPTH, 256, 128])
    o_gk = dout("o_gk", [NPS, DEPTH, 256, 128]); o_gv = dout("o_gv", [NPS, DEPTH, 256, 128])
    o_dk = dout("o_dk", [NPS, DEPTH, 256, 512]); o_dv = dout("o_dv", [NPS, DEPTH, 256, 512])
    o_lru = dout("o_lru", [NPS, DEPTH, 2, 512])

    wbf = {kk: dscr("wbf_" + kk, [DEPTH, K_, N_]) for kk, (K_, N_) in WMAP.items()}
    hT_d = dout("hT_d", [D, NT], BF16) if DBG[0] else dscr("hT_d", [D, NT])
    xaT_d = dscr("xaT_d", [512, NT], F32); yaT_d = dscr("yaT_d", [512, NT], F32)
    dd = (lambda n, s_: dout(n, s_, BF16)) if DBG[0] else dscr
    qT_d = {b: dd("qT_d%d" % b, [512, NT]) for b in (1, 2, 3)}
    kT_d = {1: dd("kT_d1", [128, TK]), 2: dd("kT_d2", [128, TK]), 3: dd("kT_d3", [512, TK])}
    v_d = {1: dd("v_d1", [TK, 128]), 2: dd("v_d2", [TK, 128]), 3: dd("v_d3", [TK, 512])}
    brT_d = dout("brT_d", [D, NT], BF16) if DBG[0] else dscr("brT_d", [D, NT])
    gbc_d = dout("gbc_d", [2, 2, 128, D], F32) if DBG[0] else dscr("gbc_d", [2, 2, 128, D], F32)

    def xsrc(l, c):
        if c < NSCH:
            src = xs if l == 0 else y_s
            dst = y_s
            r0 = c * TC
        else:
            src = xp if l == 0 else y_p
            dst = y_p
            r0 = (c - NSCH) * TC
        f = lambda t: t[r0:r0 + TC, :].rearrange("(tt p) d -> p tt d", p=128)
        return f(src), f(dst)

    with contextlib.ExitStack() as st:
        k = KB(nc, st)
        R = k.R

        def sbt(stack, name, shape, dt=F32):
            return stack.enter_context(nc.sbuf_tensor(name, list(shape), dt))

        PB = [st.enter_context(nc.psum_tensor("pb%d" % i, [128, 512], F32)) for i in range(8)]
        pb_i = [0]

        def pbank(lo=0, hi=8):
            i = lo + (pb_i[0] % (hi - lo))
            pb_i[0] += 1
            return PB[i], R("pb", i)

        cm = sbt(st, "cm", [128, 5, 128])
        idb = sbt(st, "idb", [128, 128], BF16)
        onesb = sbt(st, "onesb", [128, 128], BF16)
        masks = sbt(st, "masks", [128, 2, 512], BF16)
        epsc = sbt(st, "epsc", [128, 1]); onec = sbt(st, "onec", [128, 1])
        NW = 3
        wring = [sbt(st, "wring%d" % i, [128, KC, 512], BF16) for i in range(NW)]
        k.dma("sp", cm[:], cmat.rearrange("c p n -> p c n"), writes=[R("cm")], sem="c0")
        k.dma("sp", idb[:], cidb[:, :], writes=[R("idb")], sem="c0")
        k.dma("sp", masks[:], cmask.rearrange("c p n -> p c n"), writes=[R("masks")], sem="c0")
        k.op("dve", lambda e: e.memset(epsc[:], EPS), writes=[R("epsc")])
        k.op("dve", lambda e: e.memset(onec[:], 1.0), writes=[R("onec")])
        k.op("dve", lambda e: e.memset(onesb[:], 1.0), writes=[R("onesb")])
        identf = cm[:, 0, :]; permf = cm[:, 1, :]; bd64 = cm[:, 2, :]; avg128 = cm[:, 3, :]; onesf = cm[:, 4, :]
        RC = R("cm")

        def cast_layer(l):
            for kk, (K_, N_) in WMAP.items():
                src = W[kk][l] if kk != "br" else W[kk][l].rearrange("b c n -> (b c) n")
                npieces = K_ // 512
                for i in range(npieces):
                    k.dma("pool", wbf[kk][l, i * 512:(i + 1) * 512, :], src[i * 512:(i + 1) * 512, :],
                          writes=[R("wbf", kk, l, i)], sem="cast%d" % l, max_dma_last_dim=4096)

        def wres(kk, l):
            return [R("wbf", kk, l, i) for i in range(WMAP[kk][0] // 512)]

        class WStream:
            cur = [0]

            def __init__(self, items):
                self.items = items
                self.issued = 0
                self.taken = 0
                self.slots = {}

            def _issue(self):
                kk, l, r0, c0 = self.items[self.issued]
                s = WStream.cur[0] % NW
                WStream.cur[0] += 1
                src = wbf[kk][l, r0:r0 + 2048, c0:c0 + 512].rearrange("(kc p) n -> p kc n", p=128)
                k.dma("sp", wring[s][:], src, reads=wres(kk, l), writes=[R("wring", s)], sem="w%d" % s)
                self.slots[self.issued] = s
                self.issued += 1

            def get(self):
                while self.issued < len(self.items) and self.issued < self.taken + NW - 1:
                    self._issue()
                if self.issued <= self.taken:
                    self._issue()
                s = self.slots.pop(self.taken)
                self.taken += 1
                return wring[s], R("wring", s)

        def mm(out_ap, pairs, reads, wr, start=True, stop=True):
            def fn(e):
                n = len(pairs)
                ins = None
                for i, (l_, r_) in enumerate(pairs):
                    ins = e.matmul(out_ap, l_, r_, start=(start and i == 0), stop=(stop and i == n - 1))
                return ins
            k.op("pe", fn, reads=reads, writes=[wr])

        def act(out, in_, func, reads, writes, scale=1.0, bias=None, eng="act"):
            kw = {}
            if bias is not None:
                kw["bias"] = bias
            k.op("act", lambda e: e.activation(out=out, in_=in_, func=func, scale=scale, **kw), reads=reads, writes=writes)

        def tt(eng, out, in0, in1, op, reads, writes):
            k.op(eng, lambda e: e.tensor_tensor(out=out, in0=in0, in1=in1, op=op), reads=reads, writes=writes)

        def ts(eng, out, in0, s1, s2, op0, op1, reads, writes):
            if s2 is None:
                k.op(eng, lambda e: e.tensor_scalar(out=out, in0=in0, scalar1=s1, scalar2=None, op0=op0), reads=reads, writes=writes)
            else:
                k.op(eng, lambda e: e.tensor_scalar(out=out, in0=in0, scalar1=s1, scalar2=s2, op0=op0, op1=op1), reads=reads, writes=writes)

        def stt(eng, out, in0, scalar, in1, op0, op1, reads, writes, accum_out=None):
            kw = {} if accum_out is None else {"accum_out": accum_out}
            k.op(eng, lambda e: e.scalar_tensor_tensor(out=out, in0=in0, scalar=scalar, in1=in1, op0=op0, op1=op1, **kw),
                 reads=reads, writes=writes)

        def copy(eng, out, in_, reads, writes):
            if eng == "act":
                act(out, in_, AF.Identity, reads, writes)
            else:
                k.op(eng, lambda e: e.tensor_copy(out=out, in_=in_), reads=reads, writes=writes)

        def recip(out, in_, reads, writes):
            k.op("dve", lambda e: e.reciprocal(out=out, in_=in_), reads=reads, writes=writes)

        uid = [0]

        def U(prefix):
            uid[0] += 1
            return "%s_%d" % (prefix, uid[0])

        def _layers():
            for l in range(DEPTH):
                k.barrier()
                lam_init = 0.8 - 0.6 * math.exp(-0.3 * l)
                with contextlib.ExitStack() as ls:
                    T1 = sbt(ls, U("T1"), [128, 128]); T2 = sbt(ls, U("T2"), [128, 128]); T3 = sbt(ls, U("T3"), [128, 64])
                    modT = sbt(ls, U("modT"), [128, 96, 2])
                    A1 = sbt(ls, U("A1"), [128, KC, 2]); A2 = sbt(ls, U("A2"), [128, KC, 2])
                    bdw = sbt(ls, U("bdw"), [128, 16, 128], BF16)
                    clam = sbt(ls, U("clam"), [128, 8])
                    lamt = sbt(ls, U("lamt"), [128, 4])
                    sinkT = sbt(ls, U("sinkT"), [65, 8, 128])
                    b2bc = sbt(ls, U("b2bc"), [128, D])
                    knbc = sbt(ls, U("knbc"), [128, 3, 64])
                    RL = R("layerparams", l)
                    with contextlib.ExitStack() as ps_:
                        stg = sbt(ps_, U("stg"), [128, 128]); stg2 = sbt(ps_, U("stg2"), [128, 128]); stg3 = sbt(ps_, U("stg3"), [64, 128])
                        k.dma("sp", stg[0:96, :], b_ada[l].rearrange("(r p) -> r p", p=128), writes=[R("stg")], sem="p_stg")
                        k.dma("sp", stg[96:112, :], norm1_g[l].rearrange("(r p) -> r p", p=128), writes=[R("stg")], sem="p_stg")
                        k.dma("sp", stg[112:128, :], norm2_g[l].rearrange("(r p) -> r p", p=128), writes=[R("stg")], sem="p_stg")
                        pbk, pr = pbank()
                        k.op("pe", lambda e: e.transpose(pbk[:, 0:128], stg[:], identf), reads=[R("stg"), RC], writes=[pr])
                        copy("dve", T1[:], pbk[:, 0:128], [pr], [RL])
                        k.dma("sp", stg2[0:64, :], b_gate[l].rearrange("(r p) -> r p", p=128), writes=[R("stg2")], sem="p_stg2")
                        k.dma("sp", stg2[64:128, :], b_ff1[l].rearrange("(r p) -> r p", p=128), writes=[R("stg2")], sem="p_stg2")
                        pbk2, pr2 = pbank()
                        k.op("pe", lambda e: e.transpose(pbk2[:, 0:128], stg2[:], identf), reads=[R("stg2"), RC], writes=[pr2])
                        copy("dve", T2[:], pbk2[:, 0:128], [pr2], [RL])
                        S3 = R("stg3")
                        k.op("dve", lambda e: e.memset(stg3[:], 0.0), writes=[S3])
                        k.dma("sp", stg3[0:16, :], conv_w[l].rearrange("j (fc p) -> (j fc) p", p=128), writes=[S3], sem="p_stg3")
                        k.dma("sp", stg3[16:20, :], conv_b[l].rearrange("(fc p) -> fc p", p=128), writes=[S3], sem="p_stg3")
                        k.dma("sp", stg3[20:28, :], lru_br[l].rearrange("k (fc p) -> (k fc) p", p=128), writes=[S3], sem="p_stg3")
                        k.dma("sp", stg3[28:36, :], lru_bi[l].rearrange("k (fc p) -> (k fc) p", p=128), writes=[S3], sem="p_stg3")
                        k.dma("sp", stg3[36:44, :], lru_lambda[l].rearrange("k (fc p) -> (k fc) p", p=128), writes=[S3], sem="p_stg3")
                        k.dma("sp", stg3[44:45, :], diff_out_g[l:l + 1, :], writes=[S3], sem="p_stg3")
                        for gi, gn in enumerate(("win_qn", "win_kn", "grid_qn", "grid_kn", "diff_qn", "diff_kn")):
                            for hh in range(2):
                                k.dma("sp", stg3[45 + gi:46 + gi, hh * 64:(hh + 1) * 64], gains[gn][l:l + 1, :], writes=[S3], sem="p_stg3")
                        pbk3, pr3 = pbank()
                        k.op("pe", lambda e: e.transpose(pbk3[:, 0:64], stg3[:], identf[0:64, 0:64]), reads=[S3, RC], writes=[pr3])
                        copy("dve", T3[:], pbk3[:, 0:64], [pr3], [RL])
                        for gi in (45, 47, 49):
                            ts("dve", T3[:, gi:gi + 1], T3[:, gi:gi + 1], 0.125, None, ALU.mult, None, [RL], [RL])
                        act(clam[:], T3[:, 36:44], AF.Exp, [RL], [RL], scale=-1.0)
                        act(clam[:], clam[:], AF.Ln, [RL], [RL], bias=onec[:])
                        ts("dve", clam[:], clam[:], -8.0, None, ALU.mult, None, [RL], [RL])
                        stg4 = sbt(ps_, U("stg4"), [32, 128])
                        k.dma("sp", stg4[:], cond.rearrange("j (kc p) -> (j kc) p", p=128), writes=[R("stg4")], sem="p_stg4")
                        pbk4, pr4 = pbank()
                        k.op("pe", lambda e: e.transpose(pbk4[:, 0:32], stg4[:], identf[0:32, 0:32]), reads=[R("stg4"), RC], writes=[pr4])
                        scT = sbt(ps_, U("scT"), [128, KC, 2], BF16)
                        act(scT[:].rearrange("p kc j -> p j kc"), pbk4[:, 0:32].rearrange("p (j kc) -> p j kc", j=2), AF.Silu, [pr4], [R("scT")])
                        pm, pmr = pbank()
                        ws = WStream([("ada", l, 0, t * 512) for t in range(24)])
                        for t in range(24):
                            wt, wr_ = ws.get()
                            for mb in range(4):
                                col = (t * 4 + mb) * 2
                                mm(pm[:, col:col + 2], [(wt[:, kc, mb * 128:(mb + 1) * 128], scT[:, kc, :]) for kc in range(KC)],
                                   [wr_, R("scT")], pmr)
                        for j in range(2):
                            tt("dve", modT[:, :, j], pm[:, 0:192].rearrange("p (c j) -> p c j", j=2)[:, :, j], T1[:, 0:96], ALU.add, [pmr, RL], [RL])
                        for j in range(2):
                            stt("dve", A1[:, :, j], modT[:, 16:32, j], 1.0, T1[:, 96:112], ALU.add, ALU.mult, [RL], [RL])
                            stt("dve", A2[:, :, j], modT[:, 64:80, j], 1.0, T1[:, 112:128], ALU.add, ALU.mult, [RL], [RL])
                        gst = sbt(ps_, U("gst"), [128, D]); gm = sbt(ps_, U("gm"), [128, 128])
                        for i, base in enumerate((32, 80)):
                            for j in range(2):
                                for q4 in range(4):
                                    pg, pgr = pbank()
                                    for kk4 in range(4):
                                        kc = q4 * 4 + kk4
                                        ts("dve", gm[:], onesf, modT[:, base + kc, j:j + 1], None, ALU.mult, None, [RL, RC], [R("gm")])
                                        mm(pg[:, kk4 * 128:(kk4 + 1) * 128], [(gm[:], identf)], [R("gm"), RC], pgr)
                                    copy("act", gst[:, q4 * 512:(q4 + 1) * 512], pg[:], [pgr], [R("gst")])
                                k.dma("sp", gbc_d[i, j], gst[:], reads=[R("gst")], writes=[R("gbc", i, j)], sem="gst")
                        bst = sbt(ps_, U("bst"), [128, 16, 128])
                        k.op("dve", lambda e: e.memset(bst[:], 0.0), writes=[R("bst")])
                        for ri, wsrc in enumerate((lru_wr, lru_wi)):
                            for kd in range(2):
                                for par in range(2):
                                    for fc in range(4):
                                        idx = (fc * 2 + kd) * 2 + ri
                                        k.dma("sp", bst[par * 64:(par + 1) * 64, idx, par * 64:(par + 1) * 64], wsrc[l, kd, 2 * fc + par],
                                              writes=[R("bst")], sem="p_bst")
                        copy("dve", bdw[:], bst[:], [R("bst")], [RL])
                        lq = sbt(ps_, U("lq"), [128, 4, 64])
                        for i, gn in enumerate(("diff_lq1", "diff_lk1", "diff_lq2", "diff_lk2")):
                            k.dma("sp", lq[:, i, :], gains[gn][l:l + 1, :].partition_broadcast(128), writes=[R("lq")], sem="p_lq")
                        lj = sbt(ps_, U("lj"), [128, 64]); l2 = sbt(ps_, U("l2"), [128, 2])
                        for i_ in range(2):
                            tt("dve", lj[:], lq[:, 2 * i_, :], lq[:, 2 * i_ + 1, :], ALU.mult, [R("lq")], [R("lj")])
                            k.op("dve", lambda e, i_=i_: e.tensor_reduce(out=l2[:, i_:i_ + 1], in_=lj[:], axis=AX.X, op=ALU.add), reads=[R("lj")], writes=[R("l2")])
                        act(l2[:], l2[:], AF.Exp, [R("l2")], [R("l2")])
                        tt("dve", lamt[:, 0:1], l2[:, 0:1], l2[:, 1:2], ALU.subtract, [R("l2")], [RL])
                        ts("dve", lamt[:, 0:1], lamt[:, 0:1], lam_init, None, ALU.add, None, [RL], [RL])
                        ts("dve", lamt[:, 1:2], lamt[:, 0:1], -1.0, None, ALU.mult, None, [RL], [RL])
                        ts("dve", lamt[:, 2:3], T3[:, 44:45], 1.0 - lam_init, None, ALU.mult, None, [RL], [RL])
                        sraw = sbt(ps_, U("sraw"), [65, 8])
                        k.dma("sp", sraw[64:65, :], win_sink[l:l + 1, :], writes=[R("sraw")], sem="p_sraw")
                        act(sraw[64:65, :], sraw[64:65, :], AF.Exp, [R("sraw")], [R("sraw")])
                        for h in range(8):
                            ts("dve", sinkT[64:65, h, :], onesf[64:65, :], sraw[64:65, h:h + 1], None, ALU.mult, None, [R("sraw"), RC], [RL])
                        k.dma("sp", b2bc[:], b_ff2[l:l + 1, :].partition_broadcast(128), writes=[RL], sem="p_b2bc")
                        for i, gn in enumerate(("win_kn", "grid_kn", "diff_kn")):
                            k.dma("sp", knbc[:, i, :], gains[gn][l:l + 1, :].partition_broadcast(128), writes=[RL], sem="p_knbc")
                        cst = sbt(ps_, U("cst"), [128, 4, 512]); cko = sbt(ps_, U("cko"), [128, 4, 512], BF16)
                        for b, (ck, cv, nf) in {1: (cwk, cwv, 128), 2: (cgk, cgv, 128), 3: (cdk, cdv, 512)}.items():
                            k.dma("pool", v_d[b][NS:NS + NCTX, :], cv[l], writes=[R("v_d", b, "ctx")], sem="cstv")
                            k.dma("sp", cst[:, :, 0:nf], ck[l].rearrange("(tt p) f -> p tt f", p=128), writes=[R("cst")], sem="cst")
                            for fc in range(nf // 128):
                                pk, pkr = pbank()
                                def fnT(e, pk=pk, fc=fc):
                                    ins = None
                                    for tt_ in range(4):
                                        ins = e.transpose(pk[:, tt_ * 128:(tt_ + 1) * 128], cst[:, tt_, fc * 128:(fc + 1) * 128], identf)
                                    return ins
                                k.op("pe", fnT, reads=[R("cst"), RC], writes=[pkr])
                                copy("act", cko[:, fc, :], pk[:], [pkr], [R("cko")])
                            k.dma("sp", kT_d[b][:, NS:NS + NCTX].rearrange("(fc p) t -> p fc t", p=128), cko[:, 0:nf // 128, :],
                                  reads=[R("cko")], writes=[R("kT_d", b, "ctx")], sem="cstk")

                    chk(1)
                    k.barrier()
                    if l + 1 < DEPTH:
                        cast_layer(l + 1)
                    with contextlib.ExitStack() as p1:
                        xt = sbt(p1, U("xt"), [128, 4, D]); xn = sbt(p1, U("xn"), [128, 4, D], BF16)
                        hT = sbt(p1, U("hT"), [128, KC, TC], BF16)
                        ssq = sbt(p1, U("ssq"), [128, 8])
                        rC = sbt(p1, U("rC"), [128, TC]); rS = sbt(p1, U("rS"), [128, TC])
                        zs = sbt(p1, U("zs"), [128, TC]); sq = sbt(p1, U("sq"), [128, TC]); sd = sbt(p1, U("sd"), [128, TC])
                        qn = sbt(p1, U("qn"), [128, TC]); t1 = sbt(p1, U("t1"), [128, TC])
                        qo = sbt(p1, U("qo"), [128, 4, TC], BF16)
                        xo = sbt(p1, U("xo"), [128, 4, TC])
                        vo = sbt(p1, U("vo"), [128, 4, 768], BF16); vof = sbt(p1, U("vof"), [128, 4, 768])
                        kof = sbt(p1, U("kof"), [128, 4, 768]); ksq = sbt(p1, U("ksq"), [128, 768]); kss = sbt(p1, U("kss"), [128, 12])
                        items = [("in", l, 0, t * 512) for c in range(NCH) for t in range(8)]
                        ws = WStream(items)
                        for c in range(NCH):
                            j = 0 if c < NSCH else 1
                            is_s = c < NSCH
                            t0 = c * TC
                            xsrc_, _ = xsrc(l, c)
                            Rx = R("xres", c)
                            k.dma("sp", xt[:], xsrc_, reads=[Rx], writes=[R("xt")], sem="xt")
                            if is_s:
                                k.dma("sp", rC[:], ropeC[:, t0:t0 + TC], writes=[R("rC")], sem="rope")
                                k.dma("sp", rS[:], ropeS[:, t0:t0 + TC], writes=[R("rS")], sem="rope")
                            for tt_ in range(4):
                                xof = xo[:].rearrange("p a b -> p (a b)")
                                tt("dve", xof, xt[:, tt_, :], xt[:, tt_, :], ALU.mult, [R("xt")], [R("xo")])
                                k.op("dve", lambda e, tt_=tt_, xof=xof: e.tensor_reduce(out=ssq[:, tt_:tt_ + 1], in_=xof, axis=AX.X, op=ALU.add),
                                     reads=[R("xo")], writes=[R("ssq")])
                            act(ssq[:, 4:8], ssq[:, 0:4], AF.Sqrt, [R("ssq")], [R("ssq")], scale=1.0 / D, bias=epsc[:])
                            recip(ssq[:, 4:8], ssq[:, 4:8], [R("ssq")], [R("ssq")])
                            for tt_ in range(4):
                                ts("dve", xn[:, tt_, :], xt[:, tt_, :], ssq[:, 4 + tt_:5 + tt_], None, ALU.mult, None, [R("xt"), R("ssq")], [R("xn")])
                            for kc in range(KC):
                                pk, pkr = pbank()
                                pkb = pk[:].bitcast(BF16)
                                def fnT(e, pkb=pkb, kc=kc):
                                    ins = None
                                    for tt_ in range(4):
                                        ins = e.transpose(pkb[:, tt_ * 128:(tt_ + 1) * 128], xn[:, tt_, kc * 128:(kc + 1) * 128], idb[:])
                                    return ins
                                k.op("pe", fnT, reads=[R("xn"), R("idb")], writes=[pkr])
                                act(hT[:, kc, :], pkb[:, 0:TC], AF.Identity, [pkr, RL], [R("hT")], scale=A1[:, kc, j:j + 1], bias=modT[:, kc, j:j + 1])
                            k.dma("sp", hT_d[:, t0:t0 + TC].rearrange("(kc p) t -> p kc t", p=128), hT[:], reads=[R("hT")], writes=[R("hT_d", c)], sem="hTst")

                            chk(11)

                            def fm_block(wt, wr_, mb):
                                pz, pzr = pbank()
                                mm(pz[:], [(wt[:, kc, mb * 128:(mb + 1) * 128], hT[:, kc, :]) for kc in range(KC)], [wr_, R("hT")], pzr)
                                return pz, pzr

                            def qk_block(pz, pzr, gcol, dst, slot, rope):
                                copy("act", zs[:], pz[:], [pzr], [R("zs")])
                                tt("pool", sq[:], zs[:], zs[:], ALU.mult, [R("zs")], [R("sq")])
                                pm_, pmr_ = pbank()
                                mm(pm_[:], [(bd64, sq[:])], [RC, R("sq")], pmr_)
                                act(sd[:], pm_[:], AF.Sqrt, [pmr_], [R("sd")], bias=epsc[:])
                                recip(sd[:], sd[:], [R("sd")], [R("sd")])
                                if not rope:
                                    stt("dve", dst[:, slot, :], zs[:], T3[:, gcol:gcol + 1], sd[:], ALU.mult, ALU.mult, [R("zs"), R("sd"), RL], [R("qo")])
                                    return
                                stt("dve", qn[:], zs[:], T3[:, gcol:gcol + 1], sd[:], ALU.mult, ALU.mult, [R("zs"), R("sd"), RL], [R("qn")])
                                ps_w, psr = pbank()
                                mm(ps_w[:], [(permf, qn[:])], [RC, R("qn")], psr)
                                tt("pool", t1[:], qn[:], rC[:], ALU.mult, [R("qn"), R("rC")], [R("t1")])
                                tt("dve", qn[:], ps_w[:], rS[:], ALU.mult, [psr, R("rS")], [R("qn")])
                                tt("dve", dst[:, slot, :], t1[:], qn[:], ALU.add, [R("t1"), R("qn")], [R("qo")])

                            def tm_block(wt, wr_, tt_, c0, c1, dst_off):
                                pz, pzr = pbank()
                                mm(pz[:, 0:c1 - c0], [(hT[:, kc, tt_ * 128:(tt_ + 1) * 128], wt[:, kc, c0:c1]) for kc in range(KC)], [wr_, R("hT")], pzr)
                                return pz, pzr

                            kcol = t0 if is_s else KOFFP + (t0 - NS)
                            for t in range(8):
                                chk(20 + t)
                                wt, wr_ = ws.get()
                                if t in (0, 1):
                                    for mb in range(4):
                                        pz, pzr = fm_block(wt, wr_, mb)
                                        copy("act", xo[:, mb, :], pz[:], [pzr], [R("xo")])
                                    dstd = xaT_d if t == 0 else yaT_d
                                    k.dma("sp", dstd[:, t0:t0 + TC].rearrange("(fc p) t -> p fc t", p=128), xo[:], reads=[R("xo")],
                                          writes=[R("xyT_d", t, c)], sem="xost")
                                elif t in (2, 5):
                                    b = 1 if t == 2 else 3
                                    for mb in range(4):
                                        pz, pzr = fm_block(wt, wr_, mb)
                                        qk_block(pz, pzr, 45 if b == 1 else 49, qo, mb, is_s)
                                    k.dma("sp", qT_d[b][:, t0:t0 + TC].rearrange("(fc p) t -> p fc t", p=128), qo[:], reads=[R("qo")],
                                          writes=[R("qT_d", b, c)], sem="qost")
                                elif t == 3:
                                    pz, pzr = fm_block(wt, wr_, 0)
                                    qk_block(pz, pzr, 46, qo, 0, is_s)
                                    k.dma("sp", kT_d[1][:, kcol:kcol + TC], qo[:, 0, :], reads=[R("qo")], writes=[R("kT_d", 1, c)], sem="qost")
                                    for mb in (2, 3):
                                        pz, pzr = fm_block(wt, wr_, mb)
                                        qk_block(pz, pzr, 47, qo, mb, is_s)
                                    for tt_ in range(4):
                                        pz, pzr = tm_block(wt, wr_, tt_, 128, 256, 0)
                                        if is_s:
                                            copy("act", vo[:, tt_, 0:128], pz[:, 0:128], [pzr], [R("vo")])
                                        else:
                                            copy("act", vof[:, tt_, 0:128], pz[:, 0:128], [pzr], [R("vof")])
                                            copy("pool", vo[:, tt_, 0:128], vof[:, tt_, 0:128], [R("vof")], [R("vo")])
                                            pz2, pzr2 = tm_block(wt, wr_, tt_, 0, 128, 0)
                                            copy("act", kof[:, tt_, 0:128], pz2[:, 0:128], [pzr2], [R("kof")])
                                elif t == 4:
                                    for mb in (0, 1):
                                        pz, pzr = fm_block(wt, wr_, mb)
                                        qk_block(pz, pzr, 47, qo, mb, is_s)
                                    k.dma("sp", qT_d[2][0:256, t0:t0 + TC].rearrange("(fc p) t -> p fc t", p=128), qo[:, 2:4, :], reads=[R("qo")],
                                          writes=[R("qT_d", 2, c, 0)], sem="qost")
                                    k.dma("sp", qT_d[2][256:512, t0:t0 + TC].rearrange("(fc p) t -> p fc t", p=128), qo[:, 0:2, :], reads=[R("qo")],
                                          writes=[R("qT_d", 2, c, 1)], sem="qost")
                                    pz, pzr = fm_block(wt, wr_, 2)
                                    qk_block(pz, pzr, 48, qo, 2, is_s)
                                    k.dma("sp", kT_d[2][:, kcol:kcol + TC], qo[:, 2, :], reads=[R("qo")], writes=[R("kT_d", 2, c)], sem="qost")
                                    for tt_ in range(4):
                                        pz, pzr = tm_block(wt, wr_, tt_, 384, 512, 0)
                                        if is_s:
                                            copy("act", vo[:, tt_, 128:256], pz[:, 0:128], [pzr], [R("vo")])
                                        else:
                                            copy("act", vof[:, tt_, 128:256], pz[:, 0:128], [pzr], [R("vof")])
                                            copy("pool", vo[:, tt_, 128:256], vof[:, tt_, 128:256], [R("vof")], [R("vo")])
                                            pz2, pzr2 = tm_block(wt, wr_, tt_, 256, 384, 0)
                                            copy("act", kof[:, tt_, 128:256], pz2[:, 0:128], [pzr2], [R("kof")])
                                elif t == 6:
                                    for mb in range(4):
                                        pz, pzr = fm_block(wt, wr_, mb)
                                        qk_block(pz, pzr, 50, qo, mb, is_s)
                                    k.dma("sp", kT_d[3][:, kcol:kcol + TC].rearrange("(fc p) t -> p fc t", p=128), qo[:], reads=[R("qo")],
                                          writes=[R("kT_d", 3, c)], sem="qost")
                                    if not is_s:
                                        for tt_ in range(4):
                                            pz2, pzr2 = tm_block(wt, wr_, tt_, 0, 512, 0)
                                            copy("act", kof[:, tt_, 256:768], pz2[:, 0:512], [pzr2], [R("kof")])
                                else:
                                    for tt_ in range(4):
                                        pz, pzr = tm_block(wt, wr_, tt_, 0, 512, 0)
                                        if is_s:
                                            copy("act", vo[:, tt_, 256:768], pz[:, 0:512], [pzr], [R("vo")])
                                        else:
                                            copy("act", vof[:, tt_, 256:768], pz[:, 0:512], [pzr], [R("vof")])
                                            copy("pool", vo[:, tt_, 256:768], vof[:, tt_, 256:768], [R("vof")], [R("vo")])
                            chk(12)
                            for b, (c0, c1) in {1: (0, 128), 2: (128, 256), 3: (256, 768)}.items():
                                k.dma("sp", v_d[b][kcol:kcol + TC, :].rearrange("(tt p) f -> p tt f", p=128), vo[:, :, c0:c1], reads=[R("vo")],
                                      writes=[R("v_d", b, c)], sem="vost")
                            chk(13)
                            if not is_s:
                                s0 = (t0 - NS) // 256
                                for (b, odst, c0, c1) in ((1, o_wv, 0, 128), (2, o_gv, 128, 256), (3, o_dv, 256, 768)):
                                    for sq_ in range(2):
                                        k.dma("sp", odst[s0 + sq_, l].rearrange("(tt p) f -> p tt f", p=128), vof[:, 2 * sq_:2 * sq_ + 2, c0:c1],
                                              reads=[R("vof")], sem="ovst")
                                for tt_ in range(4):
                                    tt("pool", ksq[:], kof[:, tt_, :], kof[:, tt_, :], ALU.mult, [R("kof")], [R("ksq")])
                                    k.op("dve", lambda e: e.tensor_reduce(out=kss[:], in_=ksq[:].rearrange("p (h d) -> p h d", d=64), axis=AX.X, op=ALU.add),
                                         reads=[R("ksq")], writes=[R("kss")])
                                    act(kss[:], kss[:], AF.Sqrt, [R("kss")], [R("kss")], scale=1.0 / 64, bias=epsc[:])
                                    recip(kss[:], kss[:], [R("kss")], [R("kss")])
                                    for h in range(12):
                                        gi = 0 if h < 2 else (1 if h < 4 else 2)
                                        stt("dve", kof[:, tt_, h * 64:(h + 1) * 64], kof[:, tt_, h * 64:(h + 1) * 64], kss[:, h:h + 1], knbc[:, gi, :],
                                            ALU.mult, ALU.mult, [R("kof"), R("kss"), RL], [R("kof")])
                                for (b, odst, c0, c1) in ((1, o_wk, 0, 128), (2, o_gk, 128, 256), (3, o_dk, 256, 768)):
                                    for sq_ in range(2):
                                        k.dma("sp", odst[s0 + sq_, l].rearrange("(tt p) f -> p tt f", p=128), kof[:, 2 * sq_:2 * sq_ + 2, c0:c1],
                                              reads=[R("kof")], sem="ovst")

                    chk(2)
                    k.barrier()
                    seqs = [(0, NS, True, None)] + [(NS + s * 256, 256, False, s) for s in range(NPS)]
                    with contextlib.ExitStack() as p2:
                        LM = NS
                        Tm = [sbt(p2, U("lt%d" % i), [128, LM]) for i in range(6)]
                        ub = sbt(p2, U("ub"), [128, LM], BF16); ao = sbt(p2, U("ao"), [128, LM], BF16)
                        h0t = sbt(p2, U("h0t"), [128, 8])
                        h0s = sbt(p2, U("h0s"), [8, 128])
                        k.dma("sp", h0s[:], slru[l].rearrange("k (fc p) -> (k fc) p", p=128), writes=[R("h0s")], sem="p_h0s")
                        pbk, pr = pbank()
                        k.op("pe", lambda e: e.transpose(pbk[:, 0:8], h0s[:], identf[0:8, 0:8]), reads=[R("h0s"), RC], writes=[pr])
                        copy("dve", h0t[:], pbk[:, 0:8], [pr], [R("h0t")])
                        for (t0, L, is_s, sidx) in seqs:
                            all_c = range(t0 // TC, (t0 + L + TC - 1) // TC)
                            for fc in range(4):
                                xa, u, ra, ig, m4, ya = [t[:, 0:L] for t in Tm]
                                rd = [R("xyT_d", 0, c) for c in all_c]
                                k.dma("sp", xa, xaT_d[fc * 128:(fc + 1) * 128, t0:t0 + L], reads=rd, writes=[R("lt", 0)], sem="lru_in")
                                k.dma("sp", ya, yaT_d[fc * 128:(fc + 1) * 128, t0:t0 + L], reads=[R("xyT_d", 1, c) for c in all_c], writes=[R("lt", 5)], sem="lru_in2")
                                act(u, xa, AF.Identity, [R("lt", 0), RL], [R("lt", 1)], scale=T3[:, 4 + fc:5 + fc], bias=T3[:, 16 + fc:17 + fc])
                                stt("dve", u[:, 1:L], xa[:, 0:L - 1], T3[:, 0 + fc:1 + fc], u[:, 1:L], ALU.mult, ALU.add, [R("lt", 0), R("lt", 1), RL], [R("lt", 1)])
                                stt("dve", u[:, 0:L - 1], xa[:, 1:L], T3[:, 8 + fc:9 + fc], u[:, 0:L - 1], ALU.mult, ALU.add, [R("lt", 0), R("lt", 1), RL], [R("lt", 1)])
                                stt("dve", u[:, 0:L - 2], xa[:, 2:L], T3[:, 12 + fc:13 + fc], u[:, 0:L - 2], ALU.mult, ALU.add, [R("lt", 0), R("lt", 1), RL], [R("lt", 1)])
                                copy("pool", ub[:, 0:L], u, [R("lt", 1)], [R("ub")])
                                for kd in range(2):
                                    for n0 in range(0, L, TC):
                                        n1 = min(L, n0 + TC)
                                        for ri, dstt in ((0, ra), (1, ig)):
                                            idx = (fc * 2 + kd) * 2 + ri
                                            pz, pzr = pbank()
                                            mm(pz[:, 0:n1 - n0], [(bdw[:, idx, :], ub[:, n0:n1])], [RL, R("ub")], pzr)
                                            bcol = (20 if ri == 0 else 28) + kd * 4 + fc
                                            act(dstt[:, n0:n1], pz[:, 0:n1 - n0], AF.Sigmoid, [pzr, RL], [R("lt", 2 + ri)], bias=T3[:, bcol:bcol + 1])
                                    act(ra, ra, AF.Exp, [R("lt", 2), RL], [R("lt", 2)], scale=clam[:, kd * 4 + fc:kd * 4 + fc + 1])
                                    tt("pool", m4, ra, ra, ALU.mult, [R("lt", 2)], [R("lt", 4)])
                                    act(m4, m4, AF.Sqrt, [R("lt", 4)], [R("lt", 4)], scale=-1.0, bias=onec[:])
                                    tt("dve", ig, ig, m4, ALU.mult, [R("lt", 3), R("lt", 4)], [R("lt", 3)])
                                    tt("dve", ig, ig, u, ALU.mult, [R("lt", 3), R("lt", 1)], [R("lt", 3)])
                                    if kd == 0:
                                        init = h0t[:, fc:fc + 1] if is_s else 0.0
                                        k.op("dve", lambda e, init=init: e.tensor_tensor_scan(out=xa, data0=ra, data1=ig, initial=init, op0=ALU.mult, op1=ALU.add),
                                             reads=[R("lt", 2), R("lt", 3), R("h0t")], writes=[R("lt", 0)])
                                    else:
                                        init = h0t[:, 4 + fc:5 + fc] if is_s else 0.0
                                        k.op("dve", lambda e, init=init: e.tensor_tensor_scan(out=m4[:, ::-1], data0=ra[:, ::-1], data1=ig[:, ::-1], initial=init,
                                                                                              op0=ALU.mult, op1=ALU.add),
                                             reads=[R("lt", 2), R("lt", 3), R("h0t")], writes=[R("lt", 4)])
                                if not is_s:
                                    k.dma("sp", o_lru[sidx, l, 0, fc * 128:(fc + 1) * 128].rearrange("(p o) -> p o", o=1), xa[:, L - 1:L], reads=[R("lt", 0)], sem="olru")
                                    k.dma("sp", o_lru[sidx, l, 1, fc * 128:(fc + 1) * 128].rearrange("(p o) -> p o", o=1), m4[:, 0:1], reads=[R("lt", 4)], sem="olru")
                                tt("dve", xa, xa, m4, ALU.add, [R("lt", 0), R("lt", 4)], [R("lt", 0)])
                                tt("pool", ra, ya, ya, ALU.mult, [R("lt", 5)], [R("lt", 2)])
                                ts("dve", ra, ra, 0.044715, 1.0, ALU.mult, ALU.add, [R("lt", 2)], [R("lt", 2)])
                                tt("dve", ra, ra, ya, ALU.mult, [R("lt", 2), R("lt", 5)], [R("lt", 2)])
                                act(ra, ra, AF.Sigmoid, [R("lt", 2)], [R("lt", 2)], scale=2.0 * math.sqrt(2.0 / math.pi))
                                tt("pool", ra, ra, ya, ALU.mult, [R("lt", 2), R("lt", 5)], [R("lt", 2)])
                                tt("dve", ao[:, 0:L], xa, ra, ALU.mult, [R("lt", 0), R("lt", 2)], [R("ao")])
                                k.dma("sp", brT_d[fc * 128:(fc + 1) * 128, t0:t0 + L], ao[:, 0:L], reads=[R("ao")], writes=[R("brT_d", 0, fc, t0)], sem="aost")

                    chk(3)
                    k.barrier()
                    def key_reads(b, k0, n):
                        out_k, out_v = [], []
                        if k0 < NS:
                            for c in range(k0 // TC, (k0 + n + TC - 1) // TC):
                                out_k.append(R("kT_d", b, c)); out_v.append(R("v_d", b, c))
                            if k0 + n > NS:
                                out_k.append(R("kT_d", b, "ctx")); out_v.append(R("v_d", b, "ctx"))
                        else:
                            tok0 = NS + (k0 - KOFFP)
                            for c in range(tok0 // TC, (tok0 + n + TC - 1) // TC):
                                out_k.append(R("kT_d", b, c)); out_v.append(R("v_d", b, c))
                        return out_k, out_v

                    with contextlib.ExitStack() as p3:
                        PTn = 4
                        PT = [sbt(p3, U("pt%d" % i), [128, 512], BF16) for i in range(PTn)]
                        pt_i = [0]
                        osb = sbt(p3, U("osb"), [128, 512]); osb2 = sbt(p3, U("osb2"), [128, 512]); rz = sbt(p3, U("rz"), [128, 512])
                        ost = sbt(p3, U("ost"), [128, 2, 512], BF16)

                        def score_exp(lhsT_k, rhs_q, n, kread, qread, mask=None):
                            pS, pSr = pbank(0, 3)
                            mm(pS[:, 0:n], [(lhsT_k, rhs_q)], [kread, qread], pSr)
                            i = pt_i[0] % PTn
                            pt_i[0] += 1
                            act(PT[i][:, 0:n], pS[:, 0:n], AF.Exp, [pSr], [R("pt", i)])
                            if mask is not None:
                                tt("pool", PT[i][:, 0:n], PT[i][:, 0:n], mask, ALU.mult, [R("pt", i), R("masks")], [R("pt", i)])
                            return PT[i], R("pt", i)

                        for (t0, L, is_s, sidx) in seqs:
                            k0 = 0 if is_s else KOFFP + (t0 - NS)
                            nk = (L + NCTX) if is_s else L
                            nkb = nk // 128
                            qchunks = range(t0 // TC, (t0 + L + TC - 1) // TC)
                            for b in (1, 2):
                                with contextlib.ExitStack() as pa:
                                    Kt = sbt(pa, U("Kt"), [128, nk], BF16)
                                    Va = sbt(pa, U("Va"), [128, nkb, 2, 65], BF16)
                                    Qt = sbt(pa, U("Qt"), [128, 4, 128], BF16)
                                    kr, vr = key_reads(b, k0, nk)
                                    k.dma("sp", Kt[:], kT_d[b][:, k0:k0 + nk], reads=kr, writes=[R("Kt")], sem="kld")
                                    k.op("pool", lambda e: e.memset(Va[:, :, :, 64:65], 1.0), writes=[R("Va")])
                                    for hv in range(2):
                                        k.dma("sp", Va[:, :, hv, 0:64], v_d[b][k0:k0 + nk, hv * 64:(hv + 1) * 64].rearrange("(kb p) d -> p kb d", p=128), reads=vr, writes=[R("Va")], sem="vld")
                                    for qb in range(L // 128):
                                        q0 = t0 + qb * 128
                                        qc = q0 // TC
                                        qrd = [R("qT_d", b, qc)] if b != 2 else [R("qT_d", 2, qc, 0), R("qT_d", 2, qc, 1)]
                                        for kvh in range(2):
                                            k.dma("sp", Qt[kvh * 64:(kvh + 1) * 64, :, :], qT_d[b][kvh * 256:(kvh + 1) * 256, q0:q0 + 128].rearrange("(g d) q -> d g q", d=64),
                                                  reads=qrd, writes=[R("Qt")], sem="qld")
                                        for kvh in range(2):
                                            if b == 1 and is_s:
                                                kbs = [(kb, (0 if kb == qb - 1 else (1 if kb == qb + 1 else None))) for kb in (qb - 1, qb, qb + 1) if 0 <= kb < L // 128]
                                                kbs += [(L // 128 + i, None) for i in range(NCTX // 128)]
                                            else:
                                                kbs = [(kb, None) for kb in range(nkb)]
                                            pO, pOr = PB[3 + kvh], R("pb", 3 + kvh)
                                            for i, (kb, mi) in enumerate(kbs):
                                                ptile, ptr = score_exp(Kt[kvh * 64:(kvh + 1) * 64, kb * 128:(kb + 1) * 128],
                                                                       Qt[kvh * 64:(kvh + 1) * 64, :, :].rearrange("p g q -> p (g q)"), 512, R("Kt"), R("Qt"),
                                                                       mask=None if mi is None else masks[:, mi, :])
                                                mm(pO[0:65, :], [(Va[:, kb, kvh, :], ptile[:])], [R("Va"), ptr], pOr, start=(i == 0), stop=(i == len(kbs) - 1))
                                            copy("act", osb[0:65, :], pO[0:65, :], [pOr], [R("osb")])
                                            if b == 1:
                                                tt("dve", osb[64:65, :], osb[64:65, :], sinkT[64:65, kvh * 4:(kvh + 1) * 4, :].rearrange("p g q -> p (g q)"), ALU.add,
                                                   [R("osb"), RL], [R("osb")])
                                            recip(rz[64:65, :], osb[64:65, :], [R("osb")], [R("rz")])
                                            pZ, pZr = pbank(5, 7)
                                            mm(pZ[0:64, :], [(onesf[64:65, 0:64], rz[64:65, :])], [RC, R("rz")], pZr)
                                            tt("dve", ost[0:64, kvh, :], osb[0:64, :], pZ[0:64, :], ALU.mult, [R("osb"), pZr], [R("ost")])
                                        for kvh in range(2):
                                            k.dma("sp", brT_d[b * 512 + kvh * 256:b * 512 + (kvh + 1) * 256, q0:q0 + 128].rearrange("(g d) q -> d g q", d=64),
                                                  ost[0:64, kvh, :].rearrange("p (g q) -> p g q", g=4), reads=[R("ost")], writes=[R("brT_d", b, q0, kvh)], sem="ost")
                                k.barrier()
                            with contextlib.ExitStack() as pa:
                                Kt = sbt(pa, U("Ktd"), [128, 4, nk], BF16)
                                Vd = sbt(pa, U("Vd"), [128, nkb, 512], BF16)
                                Qd = sbt(pa, U("Qd"), [128, 4, 512], BF16)
                                o32 = sbt(pa, U("o32"), [128, 512]); dst_ = sbt(pa, U("dst"), [128, 512], BF16)
                                kr, vr = key_reads(3, k0, nk)
                                k.dma("sp", Kt[:], kT_d[3][:, k0:k0 + nk].rearrange("(h p) t -> p h t", p=128), reads=kr, writes=[R("Ktd")], sem="kld")
                                k.dma("sp", Vd[:], v_d[3][k0:k0 + nk, :].rearrange("(kb p) f -> p kb f", p=128), reads=vr, writes=[R("Vd")], sem="vld")
                                QN = min(512, L)
                                for qq in range(L // QN):
                                    q0 = t0 + qq * QN
                                    qc = q0 // TC
                                    k.dma("sp", Qd[:, :, 0:QN], qT_d[3][:, q0:q0 + QN].rearrange("(h p) t -> p h t", p=128), reads=[R("qT_d", 3, qc)], writes=[R("Qd")], sem="qld")
                                    for h in range(4):
                                        accs = [(PB[3], R("pb", 3), PB[4], R("pb", 4)), (PB[5], R("pb", 5), PB[6], R("pb", 6))]
                                        for kb in range(nkb):
                                            for jm in range(2):
                                                pO, pOr, pZ, pZr = accs[jm]
                                                ptile, ptr = score_exp(Kt[jm * 64:(jm + 1) * 64, h, kb * 128:(kb + 1) * 128], Qd[jm * 64:(jm + 1) * 64, h, 0:QN], QN,
                                                                       R("Ktd"), R("Qd"))
                                                mm(pO[:, 0:QN], [(Vd[:, kb, h * 128:(h + 1) * 128], ptile[:, 0:QN])], [R("Vd"), ptr], pOr, start=(kb == 0), stop=(kb == nkb - 1))
                                                mm(pZ[:, 0:QN], [(onesb[:], ptile[:, 0:QN])], [R("onesb"), ptr], pZr, start=(kb == 0), stop=(kb == nkb - 1))
                                        recip(rz[:, 0:QN], accs[0][2][:, 0:QN], [accs[0][3]], [R("rz")])
                                        tt("dve", osb[:, 0:QN], accs[0][0][:, 0:QN], rz[:, 0:QN], ALU.mult, [accs[0][1], R("rz")], [R("osb")])
                                        recip(rz[:, 0:QN], accs[1][2][:, 0:QN], [accs[1][3]], [R("rz")])
                                        tt("dve", osb2[:, 0:QN], accs[1][0][:, 0:QN], rz[:, 0:QN], ALU.mult, [accs[1][1], R("rz")], [R("osb2")])
                                        stt("dve", o32[:, 0:QN], osb2[:, 0:QN], lamt[:, 1:2], osb[:, 0:QN], ALU.mult, ALU.add, [R("osb"), R("osb2"), RL], [R("o32")])
                                        tt("pool", osb[:, 0:QN], o32[:, 0:QN], o32[:, 0:QN], ALU.mult, [R("o32")], [R("osb")])
                                        pM, pMr = pbank(0, 3)
                                        mm(pM[:, 0:QN], [(avg128, osb[:, 0:QN])], [RC, R("osb")], pMr)
                                        act(rz[:, 0:QN], pM[:, 0:QN], AF.Sqrt, [pMr], [R("rz")], bias=epsc[:])
                                        recip(rz[:, 0:QN], rz[:, 0:QN], [R("rz")], [R("rz")])
                                        stt("dve", dst_[:, 0:QN], o32[:, 0:QN], lamt[:, 2:3], rz[:, 0:QN], ALU.mult, ALU.mult, [R("o32"), R("rz"), RL], [R("dst")])
                                        k.dma("sp", brT_d[3 * 512 + h * 128:3 * 512 + (h + 1) * 128, q0:q0 + QN], dst_[:, 0:QN], reads=[R("dst")],
                                              writes=[R("brT_d", 3, q0, h)], sem="ost")
                            k.barrier()

                    chk(4)
                    k.barrier()
                    def br_reads(c):
                        t0 = c * TC
                        rr = []
                        if c < NSCH:
                            rr += [R("brT_d", 0, fc, 0) for fc in range(4)]
                        else:
                            for s in range(NPS):
                                if NS + s * 256 >= t0 and NS + s * 256 < t0 + TC:
                                    rr += [R("brT_d", 0, fc, NS + s * 256) for fc in range(4)]
                        for q0 in range(t0, t0 + TC, 128):
                            rr += [R("brT_d", b, q0, kvh) for b in (1, 2) for kvh in range(2)]
                        for h in range(4):
                            if c < NSCH:
                                rr.append(R("brT_d", 3, t0, h))
                            else:
                                rr += [R("brT_d", 3, t0, h), R("brT_d", 3, t0 + 256, h)]
                        return rr

                    with contextlib.ExitStack() as p4:
                        hT = sbt(p4, U("hT3"), [128, KC, TC], BF16); bT = sbt(p4, U("bT"), [128, KC, TC], BF16)
                        mT = sbt(p4, U("mT"), [128, KC, TC], BF16)
                        xt = sbt(p4, U("xt3"), [128, 4, D]); xn = bT[:].rearrange("p (a e) b -> p a (e b)", a=4)
                        gbc = sbt(p4, U("gbc"), [128, D])
                        acc = sbt(p4, U("acc"), [128, 4, TC]); sg = sbt(p4, U("sg"), [128, TC]); tmp = sbt(p4, U("tmp"), [128, TC])
                        ssq = sbt(p4, U("ssq3"), [128, 8])
                        wbrr = [sbt(p4, U("wbrr%d" % i_), [128, 4, 512], BF16) for i_ in range(2)]
                        wbr_i = [0]
                        items = []
                        for c in range(NCH):
                            for mg in range(4):
                                for b in range(4):
                                    items.append(("gate", l, 0, (b * 4 + mg) * 512))
                            for ct in range(4):
                                items.append(("o", l, 0, ct * 512))
                        ws = WStream(items)
                        cur_j = -1
                        for c in range(NCH):
                            j = 0 if c < NSCH else 1
                            t0 = c * TC
                            if j != cur_j:
                                k.dma("sp", gbc[:], gbc_d[0, j], reads=[R("gbc", 0, j)], writes=[R("gbcs")], sem="gbl")
                                cur_j = j
                            k.dma("sp", hT[:], hT_d[:, t0:t0 + TC].rearrange("(kc p) t -> p kc t", p=128), reads=[R("hT_d", c)], writes=[R("hT3")], sem="hTld")
                            k.dma("sp", bT[:], brT_d[:, t0:t0 + TC].rearrange("(kc p) t -> p kc t", p=128), reads=br_reads(c), writes=[R("bT")], sem="bTld")
                            xsrc_, xdst_ = xsrc(l, c)
                            Rx = R("xres", c)
                            k.dma("sp", xt[:], xsrc_, reads=[Rx], writes=[R("xt3")], sem="xt3")
                            for mg in range(4):
                                for b in range(4):
                                    wg_, wgr_ = ws.get()
                                    si_ = wbr_i[0] % 2
                                    wbr_i[0] += 1
                                    wb_, wbr_ = wbrr[si_], R("wbrr", si_)
                                    k.dma("sp", wb_[:], wbf["br"][l, b * 512:(b + 1) * 512, mg * 512:(mg + 1) * 512].rearrange("(kc p) n -> p kc n", p=128),
                                          reads=wres("br", l), writes=[wbr_], sem="wbr%d" % si_)
                                    for mm_ in range(4):
                                        m = mg * 4 + mm_
                                        pg, pgr = pbank()
                                        mm(pg[:], [(wg_[:, kc, mm_ * 128:(mm_ + 1) * 128], hT[:, kc, :]) for kc in range(KC)], [wgr_, R("hT3")], pgr)
                                        pp, ppr = pbank()
                                        mm(pp[:], [(wb_[:, k4, mm_ * 128:(mm_ + 1) * 128], bT[:, b * 4 + k4, :]) for k4 in range(4)], [wbr_, R("bT")], ppr)
                                        act(sg[:], pg[:], AF.Sigmoid, [pgr, RL], [R("sg")], bias=T2[:, b * 16 + m:b * 16 + m + 1])
                                        if b == 0:
                                            tt("dve", acc[:, mm_, :], sg[:], pp[:], ALU.mult, [R("sg"), ppr], [R("acc", mm_)])
                                        else:
                                            tt("dve", tmp[:], sg[:], pp[:], ALU.mult, [R("sg"), ppr], [R("tmp")])
                                            if b < 3:
                                                tt("pool", acc[:, mm_, :], acc[:, mm_, :], tmp[:], ALU.add, [R("acc", mm_), R("tmp")], [R("acc", mm_)])
                                            else:
                                                tt("pool", mT[:, m, :], acc[:, mm_, :], tmp[:], ALU.add, [R("acc", mm_), R("tmp")], [R("mT")])
                            for ct in range(4):
                                wo_, wor_ = ws.get()
                                for tt_ in range(4):
                                    po, por = pbank()
                                    mm(po[:], [(mT[:, kc, tt_ * 128:(tt_ + 1) * 128], wo_[:, kc, :]) for kc in range(KC)], [wor_, R("mT")], por)
                                    tt("dve", tmp[:], po[:], gbc[:, ct * 512:(ct + 1) * 512], ALU.mult, [por, R("gbcs")], [R("tmp")])
                                    tt("pool", xt[:, tt_, ct * 512:(ct + 1) * 512], xt[:, tt_, ct * 512:(ct + 1) * 512], tmp[:], ALU.add, [R("xt3"), R("tmp")], [R("xt3")])
                            k.dma("sp", xdst_, xt[:], reads=[R("xt3")], writes=[Rx], sem="x1st")
                            for tt_ in range(4):
                                accf = acc[:].rearrange("p a b -> p (a b)")
                                tt("dve", accf, xt[:, tt_, :], xt[:, tt_, :], ALU.mult, [R("xt3")], [R("acc", i_) for i_ in range(4)])
                                k.op("dve", lambda e, tt_=tt_, accf=accf: e.tensor_reduce(out=ssq[:, tt_:tt_ + 1], in_=accf, axis=AX.X, op=ALU.add),
                                     reads=[R("acc", i_) for i_ in range(4)], writes=[R("ssq3")])
                            act(ssq[:, 4:8], ssq[:, 0:4], AF.Sqrt, [R("ssq3")], [R("ssq3")], scale=1.0 / D, bias=epsc[:])
                            recip(ssq[:, 4:8], ssq[:, 4:8], [R("ssq3")], [R("ssq3")])
                            for tt_ in range(4):
                                ts("dve", xn[:, tt_, :], xt[:, tt_, :], ssq[:, 4 + tt_:5 + tt_], None, ALU.mult, None, [R("xt3"), R("ssq3")], [R("bT")])
                            for kc in range(KC):
                                pk, pkr = pbank()
                                pkb = pk[:].bitcast(BF16)
                                def fnT(e, pkb=pkb, kc=kc):
                                    ins = None
                                    for tt_ in range(4):
                                        ins = e.transpose(pkb[:, tt_ * 128:(tt_ + 1) * 128], xn[:, tt_, kc * 128:(kc + 1) * 128], idb[:])
                                    return ins
                                k.op("pe", fnT, reads=[R("bT"), R("idb")], writes=[pkr])
                                act(hT[:, kc, :], pkb[:, 0:TC], AF.Identity, [pkr, RL], [R("hT3")], scale=A2[:, kc, j:j + 1], bias=modT[:, 48 + kc, j:j + 1])
                            k.dma("sp", hT_d[:, t0:t0 + TC].rearrange("(kc p) t -> p kc t", p=128), hT[:], reads=[R("hT3")], writes=[R("hT_d", c)], sem="hTst")

                    chk(5)
                    k.barrier()
                    with contextlib.ExitStack() as p5:
                        hT = sbt(p5, U("hT5"), [128, KC, TC], BF16)
                        aT = sbt(p5, U("aT"), [128, 64, TC], BF16)
                        rl = sbt(p5, U("rl"), [128, TC])
                        gbc = sbt(p5, U("gbc5"), [128, D])
                        xc = sbt(p5, U("xc"), [128, 4, 512]); tmp = sbt(p5, U("tmp5"), [128, 512])
                        items = []
                        for c in range(NCH):
                            for ft in range(16):
                                items.append(("ff1", l, 0, ft * 512))
                            for ct in range(4):
                                for g in range(4):
                                    items.append(("ff2", l, g * 2048, ct * 512))
                        ws = WStream(items)
                        cur_j = -1
                        for c in range(NCH):
                            j = 0 if c < NSCH else 1
                            t0 = c * TC
                            if j != cur_j:
                                k.dma("sp", gbc[:], gbc_d[1, j], reads=[R("gbc", 1, j)], writes=[R("gbc5")], sem="gbl")
                                cur_j = j
                            k.dma("sp", hT[:], hT_d[:, t0:t0 + TC].rearrange("(kc p) t -> p kc t", p=128), reads=[R("hT_d", c)], writes=[R("hT5")], sem="hTld")
                            _, xdst_ = xsrc(l, c)
                            Rx = R("xres", c)
                            for ft in range(16):
                                w1, w1r = ws.get()
                                for mm_ in range(4):
                                    f = ft * 4 + mm_
                                    pf, pfr = pbank()
                                    mm(pf[:], [(w1[:, kc, mm_ * 128:(mm_ + 1) * 128], hT[:, kc, :]) for kc in range(KC)], [w1r, R("hT5")], pfr)
                                    act(rl[:], pf[:], AF.Relu, [pfr, RL], [R("rl")], bias=T2[:, 64 + f:65 + f])
                                    tt("dve" if mm_ % 2 == 0 else "pool", aT[:, f, :], rl[:], rl[:], ALU.mult, [R("rl")], [R("aT", f)])
                            for ct in range(4):
                                k.dma("sp", xc[:], xdst_[:, :, ct * 512:(ct + 1) * 512], reads=[Rx], writes=[R("xc")], sem="xc")
                                banks = [pbank() for _ in range(4)]
                                for g in range(4):
                                    w2, w2r = ws.get()
                                    for tt_ in range(4):
                                        po, por = banks[tt_]
                                        mm(po[:], [(aT[:, g * 16 + kc, tt_ * 128:(tt_ + 1) * 128], w2[:, kc, :]) for kc in range(KC)],
                                           [w2r] + [R("aT", g * 16 + kc) for kc in range(KC)], por, start=(g == 0), stop=(g == 3))
                                for tt_ in range(4):
                                    po, por = banks[tt_]
                                    tt("dve", tmp[:], po[:], b2bc[:, ct * 512:(ct + 1) * 512], ALU.add, [por, RL], [R("tmp5")])
                                    tt("pool", tmp[:], tmp[:], gbc[:, ct * 512:(ct + 1) * 512], ALU.mult, [R("tmp5"), R("gbc5")], [R("tmp5")])
                                    tt("dve", xc[:, tt_, :], xc[:, tt_, :], tmp[:], ALU.add, [R("xc"), R("tmp5")], [R("xc")])
                                k.dma("sp", xdst_[:, :, ct * 512:(ct + 1) * 512], xc[:], reads=[R("xc")], writes=[Rx], sem="x2st")
        cast_layer(0)
        try:
            _layers()
        except _Stop:
            pass
        k.finish()
    nc._n_inst = k.n_inst
    return nc


def host_consts(NS):
    n_freq = 16
    t = np.arange(NS)
    row = (t // GRID_W).astype(np.float32)
    col = (t % GRID_W).astype(np.float32)
    inv = (10000.0 ** (-np.arange(n_freq, dtype=np.float32) / n_freq)).astype(np.float32)
    C = np.zeros((128, NS), np.float32)
    S = np.zeros((128, NS), np.float32)
    for p in range(128):
        d = p % 64
        axis, half, fr = d // 32, (d % 32) // 16, d % 16
        pos = row if axis == 0 else col
        ang = (pos * inv[fr]).astype(np.float32)
        C[p] = np.cos(ang)
        S[p] = np.sin(ang) * (-1.0 if half == 0 else 1.0)
    cm = np.zeros((5, 128, 128), np.float32)
    cm[0] = np.eye(128)
    for p in range(128):
        d = p % 64
        half = (d % 32) // 16
        cm[1][p, p + 16 if half == 0 else p - 16] = 1.0
    for hh in range(2):
        cm[2][hh * 64:(hh + 1) * 64, hh * 64:(hh + 1) * 64] = 1.0 / 64
    cm[3][:] = 1.0 / 128
    cm[4][:] = 1.0
    kk = np.arange(128)[:, None]
    qq = np.arange(128)[None, :]
    mA = (kk >= qq).astype(np.float32)
    mB = (kk <= qq).astype(np.float32)
    cmask = np.stack([np.tile(mA, (1, 4)), np.tile(mB, (1, 4))]).astype(ml_dtypes.bfloat16)
    cidb = np.eye(128, dtype=np.float32).astype(ml_dtypes.bfloat16)
    return {"ropeC": C, "ropeS": S, "cmat": cm, "cmask": cmask, "cidb": cidb}


WEIGHT_KEYS = ["w_ada", "b_ada", "norm1_g", "norm2_g", "w_in", "conv_w", "conv_b", "lru_wr", "lru_br", "lru_wi", "lru_bi",
               "lru_lambda", "win_qn", "win_kn", "win_sink", "grid_qn", "grid_kn", "diff_qn", "diff_kn", "diff_lq1",
               "diff_lk1", "diff_lq2", "diff_lk2", "diff_out_g", "w_branch", "w_gate", "b_gate", "w_o", "w_ff1", "b_ff1",
               "w_ff2", "b_ff2"]


def run(inputs, n_cores, NS, NPS, DEPTH):
    f = lambda a: np.ascontiguousarray(np.asarray(a, dtype=np.float32))
    inp = {kk: f(v) for kk, v in inputs.items()}
    nb_s = inp["x_sample"].shape[0]
    nc = build_nc(NS, NPS, DEPTH)
    consts = host_consts(NS)
    in_maps = []
    for core in range(n_cores):
        b = core % nb_s
        m = {kk: inp[kk] for kk in WEIGHT_KEYS}
        m.update(consts)
        m["xs"] = inp["x_sample"][b]
        m["xp"] = f(inp["x_prompt"][core * NPS:(core + 1) * NPS].reshape(NPS * 256, D))
        m["cwk"] = f(inp["cache_win_k"][b].reshape(DEPTH, NCTX, 128)); m["cwv"] = f(inp["cache_win_v"][b].reshape(DEPTH, NCTX, 128))
        m["cgk"] = f(inp["cache_grid_k"][b].reshape(DEPTH, NCTX, 128)); m["cgv"] = f(inp["cache_grid_v"][b].reshape(DEPTH, NCTX, 128))
        m["cdk"] = f(inp["cache_diff_k"][b].reshape(DEPTH, NCTX, 512)); m["cdv"] = f(inp["cache_diff_v"][b].reshape(DEPTH, NCTX, 512))
        m["slru"] = f(inp["state_lru"][b])
        m["cond"] = f(np.stack([inp["c"][b], inp["c_ctx"]]))
        in_maps.append(m)
    res = run_bass_kernel_spmd(nc, in_maps, core_ids=list(range(n_cores)))
    rs = res.results
    LAST[0] = rs
    y_p = np.concatenate([r["y_p"].reshape(NPS, 256, D) for r in rs], axis=0)
    y_s = np.stack([rs[b]["y_s"] for b in range(nb_s)], axis=0)
    cat = lambda kk, shp: np.concatenate([r[kk].reshape((NPS, DEPTH, 256) + shp) for r in rs], axis=0)
    outs = (y_p, y_s, cat("o_wk", (2, 64)), cat("o_wv", (2, 64)), cat("o_gk", (2, 64)), cat("o_gv", (2, 64)),
            cat("o_dk", (4, 2, 64)), cat("o_dv", (4, 128)),
            np.concatenate([r["o_lru"].reshape(NPS, DEPTH, 2, 512) for r in rs], axis=0))
    return tuple(np.ascontiguousarray(o, dtype=np.float32) for o in outs)


def kernel(**inputs):
    return run(inputs, 8, 4096, 4, 4)
```

```python
import contextlib
import math
import numpy as np
import ml_dtypes
import concourse.bass as bass
import concourse.mybir as mybir
from concourse.bass_utils import run_bass_kernel_spmd

F32 = mybir.dt.float32
BF16 = mybir.dt.bfloat16
AF = mybir.ActivationFunctionType
ALU = mybir.AluOpType
AX = mybir.AxisListType

D = 2048
KC = 16
TC = 512
NCTX = 512
EPS = 1e-6
GRID_W = 64
WMAP = {"ada": (2048, 12288), "in": (2048, 4096), "gate": (2048, 8192), "br": (2048, 2048),
        "o": (2048, 2048), "ff1": (2048, 8192), "ff2": (8192, 2048)}
WNAME = {"ada": "w_ada", "in": "w_in", "gate": "w_gate", "br": "w_branch", "o": "w_o", "ff1": "w_ff1",
         "ff2": "w_ff2"}


class Res:
    __slots__ = ("name", "w", "r")

    def __init__(self, name):
        self.name = name
        self.w = None
        self.r = {}


class KB:
    def __init__(self, nc, stack):
        self.nc = nc
        self.stack = stack
        self.eng = {"pe": nc.tensor, "act": nc.scalar, "dve": nc.vector, "pool": nc.gpsimd, "sp": nc.sync}
        self.esem = {e: stack.enter_context(nc.semaphore("es_" + e)) for e in self.eng}
        self.cnt = {e: 0 for e in self.eng}
        self.seen = {e: {} for e in self.eng}
        self.res = {}
        self.dsems = {}
        self.n_inst = 0

    def R(self, *key):
        r = self.res.get(key)
        if r is None:
            r = Res(key)
            self.res[key] = r
        return r

    def dsem(self, name):
        d = self.dsems.get(name)
        if d is None:
            d = [self.stack.enter_context(self.nc.semaphore("ds_" + name)), 0]
            self.dsems[name] = d
        return d

    def _collect(self, e, reads, writes):
        waits = {}

        def need(tok, war=False):
            if tok is None:
                return
            kind, key, val = tok
            if kind == "eng":
                if key == e and (war or e == "pe"):
                    return
                waits[("eng", key)] = max(waits.get(("eng", key), 0), val)
            else:
                waits[("dma", key)] = 1
        for r in reads:
            need(r.w)
        for w in writes:
            need(w.w)
            for t in w.r.values():
                need(t, war=True)
        return waits

    def _emit_waits(self, e, waits):
        eng = self.eng[e]
        for (kind, key) in waits:
            if kind == "eng":
                val = waits[(kind, key)]
                sem = self.esem[key]
            else:
                d = self.dsems[key]
                val = d[1]
                sem = d[0]
            if self.seen[e].get((kind, key), 0) >= val:
                continue
            eng.wait_ge(sem, val)
            self.seen[e][(kind, key)] = val

    def op(self, e, fn, reads=(), writes=()):
        if DEAD[0]:
            return None
        waits = self._collect(e, reads, writes)
        self._emit_waits(e, waits)
        ins = fn(self.eng[e])
        self.cnt[e] += 1
        ins.then_inc(self.esem[e], 1)
        tok = ("eng", e, self.cnt[e])
        for r in reads:
            r.r[("eng", e)] = tok
        for w in writes:
            w.w = tok
            w.r = {}
        self.n_inst += 1
        return tok

    def dma(self, q, out, in_, reads=(), writes=(), sem=None, **kw):
        if DEAD[0]:
            return None
        waits = self._collect(q, reads, writes)
        self._emit_waits(q, waits)
        d = self.dsem(sem)
        ins = self.eng[q].dma_start(out=out, in_=in_, **kw)
        d[1] += 16
        ins.then_inc(d[0], 16)
        tok = ("dma", sem, d[1])
        for r in reads:
            r.r[("dma", sem)] = tok
        for w in writes:
            w.w = tok
            w.r = {}
        self.n_inst += 1
        return tok

    def barrier(self):
        if DEAD[0]:
            return
        for e in self.eng:
            eng = self.eng[e]
            for o in self.eng:
                if o == e or self.cnt[o] == 0:
                    continue
                if self.seen[e].get(("eng", o), 0) >= self.cnt[o]:
                    continue
                eng.wait_ge(self.esem[o], self.cnt[o])
                self.seen[e][("eng", o)] = self.cnt[o]
            for name, d in self.dsems.items():
                if name.startswith("cast") or name == "cstv" or d[1] == 0:
                    continue
                if self.seen[e].get(("dma", name), 0) >= d[1]:
                    continue
                eng.wait_ge(d[0], d[1])
                self.seen[e][("dma", name)] = d[1]

    def finish(self):
        sp = self.eng["sp"]
        for name, d in self.dsems.items():
            if d[1] > 0:
                sp.wait_ge(d[0], d[1])
        for e in self.eng:
            if e != "sp" and self.cnt[e] > 0:
                sp.wait_ge(self.esem[e], self.cnt[e])


class _Stop(Exception):
    pass


STOP = [99]


DEAD = [False]
DBG = [False]
LAST = [None]


def chk(n):
    if STOP[0] == n:
        DEAD[0] = True


def build_nc(NS, NPS, DEPTH):
    NP = NPS * 256
    NT = NS + NP
    NSCH = NS // TC
    NCH = NT // TC
    KOFFP = NS + NCTX
    TK = NS + NCTX + NP

    nc = bass.Bass("TRN2", target_bir_lowering=False)
    DEAD[0] = False

    def din(name, shape, dt=F32):
        return nc.dram_tensor(name, list(shape), dt, kind="ExternalInput").ap()

    def dout(name, shape, dt=F32):
        return nc.dram_tensor(name, list(shape), dt, kind="ExternalOutput").ap()

    def dscr(name, shape, dt=BF16):
        return nc.dram_tensor(name, list(shape), dt, kind="Internal").ap()

    xs = din("xs", [NS, D]); xp = din("xp", [NP, D])
    cwk = din("cwk", [DEPTH, NCTX, 128]); cwv = din("cwv", [DEPTH, NCTX, 128])
    cgk = din("cgk", [DEPTH, NCTX, 128]); cgv = din("cgv", [DEPTH, NCTX, 128])
    cdk = din("cdk", [DEPTH, NCTX, 512]); cdv = din("cdv", [DEPTH, NCTX, 512])
    slru = din("slru", [DEPTH, 2, 512])
    cond = din("cond", [2, D])
    W = {}
    for kk, (K_, N_) in WMAP.items():
        if kk == "br":
            W[kk] = din("w_branch", [DEPTH, 4, 512, 2048])
        else:
            W[kk] = din(WNAME[kk], [DEPTH, K_, N_])
    b_ada = din("b_ada", [DEPTH, 12288]); norm1_g = din("norm1_g", [DEPTH, D]); norm2_g = din("norm2_g", [DEPTH, D])
    conv_w = din("conv_w", [DEPTH, 4, 512]); conv_b = din("conv_b", [DEPTH, 512])
    lru_wr = din("lru_wr", [DEPTH, 2, 8, 64, 64]); lru_br = din("lru_br", [DEPTH, 2, 512])
    lru_wi = din("lru_wi", [DEPTH, 2, 8, 64, 64]); lru_bi = din("lru_bi", [DEPTH, 2, 512])
    lru_lambda = din("lru_lambda", [DEPTH, 2, 512])
    gains = {n: din(n, [DEPTH, 64]) for n in ("win_qn", "win_kn", "grid_qn", "grid_kn", "diff_qn", "diff_kn",
                                           "diff_lq1", "diff_lk1", "diff_lq2", "diff_lk2")}
    win_sink = din("win_sink", [DEPTH, 8]); diff_out_g = din("diff_out_g", [DEPTH, 128])
    b_gate = din("b_gate", [DEPTH, 8192]); b_ff1 = din("b_ff1", [DEPTH, 8192]); b_ff2 = din("b_ff2", [DEPTH, D])
    ropeC = din("ropeC", [128, NS]); ropeS = din("ropeS", [128, NS])
    cmat = din("cmat", [5, 128, 128])
    cmask = din("cmask", [2, 128, 512], BF16)
    cidb = din("cidb", [128, 128], BF16)

    y_s = dout("y_s", [NS, D]); y_p = dout("y_p", [NP, D])
    o_wk = dout("o_wk", [NPS, DEPTH, 256, 128]); o_wv = dout("o_wv", [NPS, DEPTH, 256, 128])
    o_gk = dout("o_gk", [NPS, DEPTH, 256, 128]); o_gv = dout("o_gv", [NPS, DEPTH, 256, 128])
    o_dk = dout("o_dk", [NPS, DEPTH, 256, 512]); o_dv = dout("o_dv", [NPS, DEPTH, 256, 512])
    o_lru = dout("o_lru", [NPS, DEPTH, 2, 512])

    wbf = {kk: dscr("wbf_" + kk, [DEPTH, K_, N_]) for kk, (K_, N_) in WMAP.items()}
    hT_d = dout("hT_d", [D, NT], BF16) if DBG[0] else dscr("hT_d", [D, NT])
    xaT_d = dscr("xaT_d", [512, NT], F32); yaT_d = dscr("yaT_d", [512, NT], F32)
    dd = (lambda n, s_: dout(n, s_, BF16)) if DBG[0] else dscr
    qT_d = {b: dd("qT_d%d" % b, [512, NT]) for b in (1, 2, 3)}
    kT_d = {1: dd("kT_d1", [128, TK]), 2: dd("kT_d2", [128, TK]), 3: dd("kT_d3", [512, TK])}
    v_d = {1: dd("v_d1", [TK, 128]), 2: dd("v_d2", [TK, 128]), 3: dd("v_d3", [TK, 512])}
    brT_d = dout("brT_d", [D, NT], BF16) if DBG[0] else dscr("brT_d", [D, NT])
    gbc_d = dout("gbc_d", [2, 2, 128, D], F32) if DBG[0] else dscr("gbc_d", [2, 2, 128, D], F32)

    def xsrc(l, c):
        if c < NSCH:
            src = xs if l == 0 else y_s
            dst = y_s
            r0 = c * TC
        else:
            src = xp if l == 0 else y_p
            dst = y_p
            r0 = (c - NSCH) * TC
        f = lambda t: t[r0:r0 + TC, :].rearrange("(tt p) d -> p tt d", p=128)
        return f(src), f(dst)

    with contextlib.ExitStack() as st:
        k = KB(nc, st)
        R = k.R

        def sbt(stack, name, shape, dt=F32):
            return stack.enter_context(nc.sbuf_tensor(name, list(shape), dt))

        PB = [st.enter_context(nc.psum_tensor("pb%d" % i, [128, 512], F32)) for i in range(8)]
        pb_i = [0]

        def pbank(lo=0, hi=8):
            i = lo + (pb_i[0] % (hi - lo))
            pb_i[0] += 1
            return PB[i], R("pb", i)

        cm = sbt(st, "cm", [128, 5, 128])
        idb = sbt(st, "idb", [128, 128], BF16)
        onesb = sbt(st, "onesb", [128, 128], BF16)
        masks = sbt(st, "masks", [128, 2, 512], BF16)
        epsc = sbt(st, "epsc", [128, 1]); onec = sbt(st, "onec", [128, 1])
        NW = 3
        wring = [sbt(st, "wring%d" % i, [128, KC, 512], BF16) for i in range(NW)]
        k.dma("sp", cm[:], cmat.rearrange("c p n -> p c n"), writes=[R("cm")], sem="c0")
        k.dma("sp", idb[:], cidb[:, :], writes=[R("idb")], sem="c0")
        k.dma("sp", masks[:], cmask.rearrange("c p n -> p c n"), writes=[R("masks")], sem="c0")
        k.op("dve", lambda e: e.memset(epsc[:], EPS), writes=[R("epsc")])
        k.op("dve", lambda e: e.memset(onec[:], 1.0), writes=[R("onec")])
        k.op("dve", lambda e: e.memset(onesb[:], 1.0), writes=[R("onesb")])
        identf = cm[:, 0, :]; permf = cm[:, 1, :]; bd64 = cm[:, 2, :]; avg128 = cm[:, 3, :]; onesf = cm[:, 4, :]
        RC = R("cm")

        def cast_layer(l):
            for kk, (K_, N_) in WMAP.items():
                src = W[kk][l] if kk != "br" else W[kk][l].rearrange("b c n -> (b c) n")
                npieces = K_ // 512
                for i in range(npieces):
                    k.dma("pool", wbf[kk][l, i * 512:(i + 1) * 512, :], src[i * 512:(i + 1) * 512, :],
                          writes=[R("wbf", kk, l, i)], sem="cast%d" % l, max_dma_last_dim=4096)

        def wres(kk, l):
            return [R("wbf", kk, l, i) for i in range(WMAP[kk][0] // 512)]

        class WStream:
            cur = [0]

            def __init__(self, items):
                self.items = items
                self.issued = 0
                self.taken = 0
                self.slots = {}

            def _issue(self):
                kk, l, r0, c0 = self.items[self.issued]
                s = WStream.cur[0] % NW
                WStream.cur[0] += 1
                src = wbf[kk][l, r0:r0 + 2048, c0:c0 + 512].rearrange("(kc p) n -> p kc n", p=128)
                k.dma("sp", wring[s][:], src, reads=wres(kk, l), writes=[R("wring", s)], sem="w%d" % s)
                self.slots[self.issued] = s
                self.issued += 1

            def get(self):
                while self.issued < len(self.items) and self.issued < self.taken + NW - 1:
                    self._issue()
                if self.issued <= self.taken:
                    self._issue()
                s = self.slots.pop(self.taken)
                self.taken += 1
                return wring[s], R("wring", s)

        def mm(out_ap, pairs, reads, wr, start=True, stop=True):
            def fn(e):
                n = len(pairs)
                ins = None
                for i, (l_, r_) in enumerate(pairs):
                    ins = e.matmul(out_ap, l_, r_, start=(start and i == 0), stop=(stop and i == n - 1))
                return ins
            k.op("pe", fn, reads=reads, writes=[wr])

        def act(out, in_, func, reads, writes, scale=1.0, bias=None, eng="act"):
            kw = {}
            if bias is not None:
                kw["bias"] = bias
            k.op("act", lambda e: e.activation(out=out, in_=in_, func=func, scale=scale, **kw), reads=reads, writes=writes)

        def tt(eng, out, in0, in1, op, reads, writes):
            k.op(eng, lambda e: e.tensor_tensor(out=out, in0=in0, in1=in1, op=op), reads=reads, writes=writes)

        def ts(eng, out, in0, s1, s2, op0, op1, reads, writes):
            if s2 is None:
                k.op(eng, lambda e: e.tensor_scalar(out=out, in0=in0, scalar1=s1, scalar2=None, op0=op0), reads=reads, writes=writes)
            else:
                k.op(eng, lambda e: e.tensor_scalar(out=out, in0=in0, scalar1=s1, scalar2=s2, op0=op0, op1=op1), reads=reads, writes=writes)

        def stt(eng, out, in0, scalar, in1, op0, op1, reads, writes, accum_out=None):
            kw = {} if accum_out is None else {"accum_out": accum_out}
            k.op(eng, lambda e: e.scalar_tensor_tensor(out=out, in0=in0, scalar=scalar, in1=in1, op0=op0, op1=op1, **kw),
                 reads=reads, writes=writes)

        def copy(eng, out, in_, reads, writes):
            if eng == "act":
                act(out, in_, AF.Identity, reads, writes)
            else:
                k.op(eng, lambda e: e.tensor_copy(out=out, in_=in_), reads=reads, writes=writes)

        def recip(out, in_, reads, writes):
            k.op("dve", lambda e: e.reciprocal(out=out, in_=in_), reads=reads, writes=writes)

        uid = [0]

        def U(prefix):
            uid[0] += 1
            return "%s_%d" % (prefix, uid[0])

        def _layers():
            for l in range(DEPTH):
                k.barrier()
                lam_init = 0.8 - 0.6 * math.exp(-0.3 * l)
                with contextlib.ExitStack() as ls:
                    T1 = sbt(ls, U("T1"), [128, 128]); T2 = sbt(ls, U("T2"), [128, 128]); T3 = sbt(ls, U("T3"), [128, 64])
                    modT = sbt(ls, U("modT"), [128, 96, 2])
                    A1 = sbt(ls, U("A1"), [128, KC, 2]); A2 = sbt(ls, U("A2"), [128, KC, 2])
                    bdw = sbt(ls, U("bdw"), [128, 16, 128], BF16)
                    clam = sbt(ls, U("clam"), [128, 8])
                    lamt = sbt(ls, U("lamt"), [128, 4])
                    sinkT = sbt(ls, U("sinkT"), [65, 8, 128])
                    b2bc = sbt(ls, U("b2bc"), [128, D])
                    knbc = sbt(ls, U("knbc"), [128, 3, 64])
                    RL = R("layerparams", l)
                    with contextlib.ExitStack() as ps_:
                        stg = sbt(ps_, U("stg"), [128, 128]); stg2 = sbt(ps_, U("stg2"), [128, 128]); stg3 = sbt(ps_, U("stg3"), [64, 128])
                        k.dma("sp", stg[0:96, :], b_ada[l].rearrange("(r p) -> r p", p=128), writes=[R("stg")], sem="p_stg")
                        k.dma("sp", stg[96:112, :], norm1_g[l].rearrange("(r p) -> r p", p=128), writes=[R("stg")], sem="p_stg")
                        k.dma("sp", stg[112:128, :], norm2_g[l].rearrange("(r p) -> r p", p=128), writes=[R("stg")], sem="p_stg")
                        pbk, pr = pbank()
                        k.op("pe", lambda e: e.transpose(pbk[:, 0:128], stg[:], identf), reads=[R("stg"), RC], writes=[pr])
                        copy("dve", T1[:], pbk[:, 0:128], [pr], [RL])
                        k.dma("sp", stg2[0:64, :], b_gate[l].rearrange("(r p) -> r p", p=128), writes=[R("stg2")], sem="p_stg2")
                        k.dma("sp", stg2[64:128, :], b_ff1[l].rearrange("(r p) -> r p", p=128), writes=[R("stg2")], sem="p_stg2")
                        pbk2, pr2 = pbank()
                        k.op("pe", lambda e: e.transpose(pbk2[:, 0:128], stg2[:], identf), reads=[R("stg2"), RC], writes=[pr2])
                        copy("dve", T2[:], pbk2[:, 0:128], [pr2], [RL])
                        S3 = R("stg3")
                        k.op("dve", lambda e: e.memset(stg3[:], 0.0), writes=[S3])
                        k.dma("sp", stg3[0:16, :], conv_w[l].rearrange("j (fc p) -> (j fc) p", p=128), writes=[S3], sem="p_stg3")
                        k.dma("sp", stg3[16:20, :], conv_b[l].rearrange("(fc p) -> fc p", p=128), writes=[S3], sem="p_stg3")
                        k.dma("sp", stg3[20:28, :], lru_br[l].rearrange("k (fc p) -> (k fc) p", p=128), writes=[S3], sem="p_stg3")
                        k.dma("sp", stg3[28:36, :], lru_bi[l].rearrange("k (fc p) -> (k fc) p", p=128), writes=[S3], sem="p_stg3")
                        k.dma("sp", stg3[36:44, :], lru_lambda[l].rearrange("k (fc p) -> (k fc) p", p=128), writes=[S3], sem="p_stg3")
                        k.dma("sp", stg3[44:45, :], diff_out_g[l:l + 1, :], writes=[S3], sem="p_stg3")
                        for gi, gn in enumerate(("win_qn", "win_kn", "grid_qn", "grid_kn", "diff_qn", "diff_kn")):
                            for hh in range(2):
                                k.dma("sp", stg3[45 + gi:46 + gi, hh * 64:(hh + 1) * 64], gains[gn][l:l + 1, :], writes=[S3], sem="p_stg3")
                        pbk3, pr3 = pbank()
                        k.op("pe", lambda e: e.transpose(pbk3[:, 0:64], stg3[:], identf[0:64, 0:64]), reads=[S3, RC], writes=[pr3])
                        copy("dve", T3[:], pbk3[:, 0:64], [pr3], [RL])
                        for gi in (45, 47, 49):
                            ts("dve", T3[:, gi:gi + 1], T3[:, gi:gi + 1], 0.125, None, ALU.mult, None, [RL], [RL])
                        act(clam[:], T3[:, 36:44], AF.Exp, [RL], [RL], scale=-1.0)
                        act(clam[:], clam[:], AF.Ln, [RL], [RL], bias=onec[:])
                        ts("dve", clam[:], clam[:], -8.0, None, ALU.mult, None, [RL], [RL])
                        stg4 = sbt(ps_, U("stg4"), [32, 128])
                        k.dma("sp", stg4[:], cond.rearrange("j (kc p) -> (j kc) p", p=128), writes=[R("stg4")], sem="p_stg4")
                        pbk4, pr4 = pbank()
                        k.op("pe", lambda e: e.transpose(pbk4[:, 0:32], stg4[:], identf[0:32, 0:32]), reads=[R("stg4"), RC], writes=[pr4])
                        scT = sbt(ps_, U("scT"), [128, KC, 2], BF16)
                        act(scT[:].rearrange("p kc j -> p j kc"), pbk4[:, 0:32].rearrange("p (j kc) -> p j kc", j=2), AF.Silu, [pr4], [R("scT")])
                        pm, pmr = pbank()
                        ws = WStream([("ada", l, 0, t * 512) for t in range(24)])
                        for t in range(24):
                            wt, wr_ = ws.get()
                            for mb in range(4):
                                col = (t * 4 + mb) * 2
                                mm(pm[:, col:col + 2], [(wt[:, kc, mb * 128:(mb + 1) * 128], scT[:, kc, :]) for kc in range(KC)],
                                   [wr_, R("scT")], pmr)
                        for j in range(2):
                            tt("dve", modT[:, :, j], pm[:, 0:192].rearrange("p (c j) -> p c j", j=2)[:, :, j], T1[:, 0:96], ALU.add, [pmr, RL], [RL])
                        for j in range(2):
                            stt("dve", A1[:, :, j], modT[:, 16:32, j], 1.0, T1[:, 96:112], ALU.add, ALU.mult, [RL], [RL])
                            stt("dve", A2[:, :, j], modT[:, 64:80, j], 1.0, T1[:, 112:128], ALU.add, ALU.mult, [RL], [RL])
                        gst = sbt(ps_, U("gst"), [128, D]); gm = sbt(ps_, U("gm"), [128, 128])
                        for i, base in enumerate((32, 80)):
                            for j in range(2):
                                for q4 in range(4):
                                    pg, pgr = pbank()
                                    for kk4 in range(4):
                                        kc = q4 * 4 + kk4
                                        ts("dve", gm[:], onesf, modT[:, base + kc, j:j + 1], None, ALU.mult, None, [RL, RC], [R("gm")])
                                        mm(pg[:, kk4 * 128:(kk4 + 1) * 128], [(gm[:], identf)], [R("gm"), RC], pgr)
                                    copy("act", gst[:, q4 * 512:(q4 + 1) * 512], pg[:], [pgr], [R("gst")])
                                k.dma("sp", gbc_d[i, j], gst[:], reads=[R("gst")], writes=[R("gbc", i, j)], sem="gst")
                        bst = sbt(ps_, U("bst"), [128, 16, 128])
                        k.op("dve", lambda e: e.memset(bst[:], 0.0), writes=[R("bst")])
                        for ri, wsrc in enumerate((lru_wr, lru_wi)):
                            for kd in range(2):
                                for par in range(2):
                                    for fc in range(4):
                                        idx = (fc * 2 + kd) * 2 + ri
                                        k.dma("sp", bst[par * 64:(par + 1) * 64, idx, par * 64:(par + 1) * 64], wsrc[l, kd, 2 * fc + par],
                                              writes=[R("bst")], sem="p_bst")
                        copy("dve", bdw[:], bst[:], [R("bst")], [RL])
                        lq = sbt(ps_, U("lq"), [128, 4, 64])
                        for i, gn in enumerate(("diff_lq1", "diff_lk1", "diff_lq2", "diff_lk2")):
                            k.dma("sp", lq[:, i, :], gains[gn][l:l + 1, :].partition_broadcast(128), writes=[R("lq")], sem="p_lq")
                        lj = sbt(ps_, U("lj"), [128, 64]); l2 = sbt(ps_, U("l2"), [128, 2])
                        for i_ in range(2):
                            tt("dve", lj[:], lq[:, 2 * i_, :], lq[:, 2 * i_ + 1, :], ALU.mult, [R("lq")], [R("lj")])
                            k.op("dve", lambda e, i_=i_: e.tensor_reduce(out=l2[:, i_:i_ + 1], in_=lj[:], axis=AX.X, op=ALU.add), reads=[R("lj")], writes=[R("l2")])
                        act(l2[:], l2[:], AF.Exp, [R("l2")], [R("l2")])
                        tt("dve", lamt[:, 0:1], l2[:, 0:1], l2[:, 1:2], ALU.subtract, [R("l2")], [RL])
                        ts("dve", lamt[:, 0:1], lamt[:, 0:1], lam_init, None, ALU.add, None, [RL], [RL])
                        ts("dve", lamt[:, 1:2], lamt[:, 0:1], -1.0, None, ALU.mult, None, [RL], [RL])
                        ts("dve", lamt[:, 2:3], T3[:, 44:45], 1.0 - lam_init, None, ALU.mult, None, [RL], [RL])
                        sraw = sbt(ps_, U("sraw"), [65, 8])
                        k.dma("sp", sraw[64:65, :], win_sink[l:l + 1, :], writes=[R("sraw")], sem="p_sraw")
                        act(sraw[64:65, :], sraw[64:65, :], AF.Exp, [R("sraw")], [R("sraw")])
                        for h in range(8):
                            ts("dve", sinkT[64:65, h, :], onesf[64:65, :], sraw[64:65, h:h + 1], None, ALU.mult, None, [R("sraw"), RC], [RL])
                        k.dma("sp", b2bc[:], b_ff2[l:l + 1, :].partition_broadcast(128), writes=[RL], sem="p_b2bc")
                        for i, gn in enumerate(("win_kn", "grid_kn", "diff_kn")):
                            k.dma("sp", knbc[:, i, :], gains[gn][l:l + 1, :].partition_broadcast(128), writes=[RL], sem="p_knbc")
                        cst = sbt(ps_, U("cst"), [128, 4, 512]); cko = sbt(ps_, U("cko"), [128, 4, 512], BF16)
                        for b, (ck, cv, nf) in {1: (cwk, cwv, 128), 2: (cgk, cgv, 128), 3: (cdk, cdv, 512)}.items():
                            k.dma("pool", v_d[b][NS:NS + NCTX, :], cv[l], writes=[R("v_d", b, "ctx")], sem="cstv")
                            k.dma("sp", cst[:, :, 0:nf], ck[l].rearrange("(tt p) f -> p tt f", p=128), writes=[R("cst")], sem="cst")
                            for fc in range(nf // 128):
                                pk, pkr = pbank()
                                def fnT(e, pk=pk, fc=fc):
                                    ins = None
                                    for tt_ in range(4):
                                        ins = e.transpose(pk[:, tt_ * 128:(tt_ + 1) * 128], cst[:, tt_, fc * 128:(fc + 1) * 128], identf)
                                    return ins
                                k.op("pe", fnT, reads=[R("cst"), RC], writes=[pkr])
                                copy("act", cko[:, fc, :], pk[:], [pkr], [R("cko")])
                            k.dma("sp", kT_d[b][:, NS:NS + NCTX].rearrange("(fc p) t -> p fc t", p=128), cko[:, 0:nf // 128, :],
                                  reads=[R("cko")], writes=[R("kT_d", b, "ctx")], sem="cstk")

                    chk(1)
                    k.barrier()
                    if l + 1 < DEPTH:
                        cast_layer(l + 1)
                    with contextlib.ExitStack() as p1:
                        xt = sbt(p1, U("xt"), [128, 4, D]); xn = sbt(p1, U("xn"), [128, 4, D], BF16)
                        hT = sbt(p1, U("hT"), [128, KC, TC], BF16)
                        ssq = sbt(p1, U("ssq"), [128, 8])
                        rC = sbt(p1, U("rC"), [128, TC]); rS = sbt(p1, U("rS"), [128, TC])
                        zs = sbt(p1, U("zs"), [128, TC]); sq = sbt(p1, U("sq"), [128, TC]); sd = sbt(p1, U("sd"), [128, TC])
                        qn = sbt(p1, U("qn"), [128, TC]); t1 = sbt(p1, U("t1"), [128, TC])
                        qo = sbt(p1, U("qo"), [128, 4, TC], BF16)
                        xo = sbt(p1, U("xo"), [128, 4, TC])
                        vo = sbt(p1, U("vo"), [128, 4, 768], BF16); vof = sbt(p1, U("vof"), [128, 4, 768])
                        kof = sbt(p1, U("kof"), [128, 4, 768]); ksq = sbt(p1, U("ksq"), [128, 768]); kss = sbt(p1, U("kss"), [128, 12])
                        items = [("in", l, 0, t * 512) for c in range(NCH) for t in range(8)]
                        ws = WStream(items)
                        for c in range(NCH):
                            j = 0 if c < NSCH else 1
                            is_s = c < NSCH
                            t0 = c * TC
                            xsrc_, _ = xsrc(l, c)
                            Rx = R("xres", c)
                            k.dma("sp", xt[:], xsrc_, reads=[Rx], writes=[R("xt")], sem="xt")
                            if is_s:
                                k.dma("sp", rC[:], ropeC[:, t0:t0 + TC], writes=[R("rC")], sem="rope")
                                k.dma("sp", rS[:], ropeS[:, t0:t0 + TC], writes=[R("rS")], sem="rope")
                            for tt_ in range(4):
                                xof = xo[:].rearrange("p a b -> p (a b)")
                                tt("dve", xof, xt[:, tt_, :], xt[:, tt_, :], ALU.mult, [R("xt")], [R("xo")])
                                k.op("dve", lambda e, tt_=tt_, xof=xof: e.tensor_reduce(out=ssq[:, tt_:tt_ + 1], in_=xof, axis=AX.X, op=ALU.add),
                                     reads=[R("xo")], writes=[R("ssq")])
                            act(ssq[:, 4:8], ssq[:, 0:4], AF.Sqrt, [R("ssq")], [R("ssq")], scale=1.0 / D, bias=epsc[:])
                            recip(ssq[:, 4:8], ssq[:, 4:8], [R("ssq")], [R("ssq")])
                            for tt_ in range(4):
                                ts("dve", xn[:, tt_, :], xt[:, tt_, :], ssq[:, 4 + tt_:5 + tt_], None, ALU.mult, None, [R("xt"), R("ssq")], [R("xn")])
                            for kc in range(KC):
                                pk, pkr = pbank()
                                pkb = pk[:].bitcast(BF16)
                                def fnT(e, pkb=pkb, kc=kc):
                                    ins = None
                                    for tt_ in range(4):
                                        ins = e.transpose(pkb[:, tt_ * 128:(tt_ + 1) * 128], xn[:, tt_, kc * 128:(kc + 1) * 128], idb[:])
                                    return ins
                                k.op("pe", fnT, reads=[R("xn"), R("idb")], writes=[pkr])
                                act(hT[:, kc, :], pkb[:, 0:TC], AF.Identity, [pkr, RL], [R("hT")], scale=A1[:, kc, j:j + 1], bias=modT[:, kc, j:j + 1])
                            k.dma("sp", hT_d[:, t0:t0 + TC].rearrange("(kc p) t -> p kc t", p=128), hT[:], reads=[R("hT")], writes=[R("hT_d", c)], sem="hTst")

                            chk(11)

                            def fm_block(wt, wr_, mb):
                                pz, pzr = pbank()
                                mm(pz[:], [(wt[:, kc, mb * 128:(mb + 1) * 128], hT[:, kc, :]) for kc in range(KC)], [wr_, R("hT")], pzr)
                                return pz, pzr

                            def qk_block(pz, pzr, gcol, dst, slot, rope):
                                copy("act", zs[:], pz[:], [pzr], [R("zs")])
                                tt("pool", sq[:], zs[:], zs[:], ALU.mult, [R("zs")], [R("sq")])
                                pm_, pmr_ = pbank()
                                mm(pm_[:], [(bd64, sq[:])], [RC, R("sq")], pmr_)
                                act(sd[:], pm_[:], AF.Sqrt, [pmr_], [R("sd")], bias=epsc[:])
                                recip(sd[:], sd[:], [R("sd")], [R("sd")])
                                if not rope:
                                    stt("dve", dst[:, slot, :], zs[:], T3[:, gcol:gcol + 1], sd[:], ALU.mult, ALU.mult, [R("zs"), R("sd"), RL], [R("qo")])
                                    return
                                stt("dve", qn[:], zs[:], T3[:, gcol:gcol + 1], sd[:], ALU.mult, ALU.mult, [R("zs"), R("sd"), RL], [R("qn")])
                                ps_w, psr = pbank()
                                mm(ps_w[:], [(permf, qn[:])], [RC, R("qn")], psr)
                                tt("pool", t1[:], qn[:], rC[:], ALU.mult, [R("qn"), R("rC")], [R("t1")])
                                tt("dve", qn[:], ps_w[:], rS[:], ALU.mult, [psr, R("rS")], [R("qn")])
                                tt("dve", dst[:, slot, :], t1[:], qn[:], ALU.add, [R("t1"), R("qn")], [R("qo")])

                            def tm_block(wt, wr_, tt_, c0, c1, dst_off):
                                pz, pzr = pbank()
                                mm(pz[:, 0:c1 - c0], [(hT[:, kc, tt_ * 128:(tt_ + 1) * 128], wt[:, kc, c0:c1]) for kc in range(KC)], [wr_, R("hT")], pzr)
                                return pz, pzr

                            kcol = t0 if is_s else KOFFP + (t0 - NS)
                            for t in range(8):
                                chk(20 + t)
                                wt, wr_ = ws.get()
                                if t in (0, 1):
                                    for mb in range(4):
                                        pz, pzr = fm_block(wt, wr_, mb)
                                        copy("act", xo[:, mb, :], pz[:], [pzr], [R("xo")])
                                    dstd = xaT_d if t == 0 else yaT_d
                                    k.dma("sp", dstd[:, t0:t0 + TC].rearrange("(fc p) t -> p fc t", p=128), xo[:], reads=[R("xo")],
                                          writes=[R("xyT_d", t, c)], sem="xost")
                                elif t in (2, 5):
                                    b = 1 if t == 2 else 3
                                    for mb in range(4):
                                        pz, pzr = fm_block(wt, wr_, mb)
                                        qk_block(pz, pzr, 45 if b == 1 else 49, qo, mb, is_s)
                                    k.dma("sp", qT_d[b][:, t0:t0 + TC].rearrange("(fc p) t -> p fc t", p=128), qo[:], reads=[R("qo")],
                                          writes=[R("qT_d", b, c)], sem="qost")
                                elif t == 3:
                                    pz, pzr = fm_block(wt, wr_, 0)
                                    qk_block(pz, pzr, 46, qo, 0, is_s)
                                    k.dma("sp", kT_d[1][:, kcol:kcol + TC], qo[:, 0, :], reads=[R("qo")], writes=[R("kT_d", 1, c)], sem="qost")
                                    for mb in (2, 3):
                                        pz, pzr = fm_block(wt, wr_, mb)
                                        qk_block(pz, pzr, 47, qo, mb, is_s)
                                    for tt_ in range(4):
                                        pz, pzr = tm_block(wt, wr_, tt_, 128, 256, 0)
                                        if is_s:
                                            copy("act", vo[:, tt_, 0:128], pz[:, 0:128], [pzr], [R("vo")])
                                        else:
                                            copy("act", vof[:, tt_, 0:128], pz[:, 0:128], [pzr], [R("vof")])
                                            copy("pool", vo[:, tt_, 0:128], vof[:, tt_, 0:128], [R("vof")], [R("vo")])
                                            pz2, pzr2 = tm_block(wt, wr_, tt_, 0, 128, 0)
                                            copy("act", kof[:, tt_, 0:128], pz2[:, 0:128], [pzr2], [R("kof")])
                                elif t == 4:
                                    for mb in (0, 1):
                                        pz, pzr = fm_block(wt, wr_, mb)
                                        qk_block(pz, pzr, 47, qo, mb, is_s)
                                    k.dma("sp", qT_d[2][0:256, t0:t0 + TC].rearrange("(fc p) t -> p fc t", p=128), qo[:, 2:4, :], reads=[R("qo")],
                                          writes=[R("qT_d", 2, c, 0)], sem="qost")
                                    k.dma("sp", qT_d[2][256:512, t0:t0 + TC].rearrange("(fc p) t -> p fc t", p=128), qo[:, 0:2, :], reads=[R("qo")],
                                          writes=[R("qT_d", 2, c, 1)], sem="qost")
                                    pz, pzr = fm_block(wt, wr_, 2)
                                    qk_block(pz, pzr, 48, qo, 2, is_s)
                                    k.dma("sp", kT_d[2][:, kcol:kcol + TC], qo[:, 2, :], reads=[R("qo")], writes=[R("kT_d", 2, c)], sem="qost")
                                    for tt_ in range(4):
                                        pz, pzr = tm_block(wt, wr_, tt_, 384, 512, 0)
                                        if is_s:
                                            copy("act", vo[:, tt_, 128:256], pz[:, 0:128], [pzr], [R("vo")])
                                        else:
                                            copy("act", vof[:, tt_, 128:256], pz[:, 0:128], [pzr], [R("vof")])
                                            copy("pool", vo[:, tt_, 128:256], vof[:, tt_, 128:256], [R("vof")], [R("vo")])
                                            pz2, pzr2 = tm_block(wt, wr_, tt_, 256, 384, 0)
                                            copy("act", kof[:, tt_, 128:256], pz2[:, 0:128], [pzr2], [R("kof")])
                                elif t == 6:
                                    for mb in range(4):
                                        pz, pzr = fm_block(wt, wr_, mb)
                                        qk_block(pz, pzr, 50, qo, mb, is_s)
                                    k.dma("sp", kT_d[3][:, kcol:kcol + TC].rearrange("(fc p) t -> p fc t", p=128), qo[:], reads=[R("qo")],
                                          writes=[R("kT_d", 3, c)], sem="qost")
                                    if not is_s:
                                        for tt_ in range(4):
                                            pz2, pzr2 = tm_block(wt, wr_, tt_, 0, 512, 0)
                                            copy("act", kof[:, tt_, 256:768], pz2[:, 0:512], [pzr2], [R("kof")])
                                else:
                                    for tt_ in range(4):
                                        pz, pzr = tm_block(wt, wr_, tt_, 0, 512, 0)
                                        if is_s:
                                            copy("act", vo[:, tt_, 256:768], pz[:, 0:512], [pzr], [R("vo")])
                                        else:
                                            copy("act", vof[:, tt_, 256:768], pz[:, 0:512], [pzr], [R("vof")])
                                            copy("pool", vo[:, tt_, 256:768], vof[:, tt_, 256:768], [R("vof")], [R("vo")])
                            chk(12)
                            for b, (c0, c1) in {1: (0, 128), 2: (128, 256), 3: (256, 768)}.items():
                                k.dma("sp", v_d[b][kcol:kcol + TC, :].rearrange("(tt p) f -> p tt f", p=128), vo[:, :, c0:c1], reads=[R("vo")],
                                      writes=[R("v_d", b, c)], sem="vost")
                            chk(13)
                            if not is_s:
                                s0 = (t0 - NS) // 256
                                for (b, odst, c0, c1) in ((1, o_wv, 0, 128), (2, o_gv, 128, 256), (3, o_dv, 256, 768)):
                                    for sq_ in range(2):
                                        k.dma("sp", odst[s0 + sq_, l].rearrange("(tt p) f -> p tt f", p=128), vof[:, 2 * sq_:2 * sq_ + 2, c0:c1],
                                              reads=[R("vof")], sem="ovst")
                                for tt_ in range(4):
                                    tt("pool", ksq[:], kof[:, tt_, :], kof[:, tt_, :], ALU.mult, [R("kof")], [R("ksq")])
                                    k.op("dve", lambda e: e.tensor_reduce(out=kss[:], in_=ksq[:].rearrange("p (h d) -> p h d", d=64), axis=AX.X, op=ALU.add),
                                         reads=[R("ksq")], writes=[R("kss")])
                                    act(kss[:], kss[:], AF.Sqrt, [R("kss")], [R("kss")], scale=1.0 / 64, bias=epsc[:])
                                    recip(kss[:], kss[:], [R("kss")], [R("kss")])
                                    for h in range(12):
                                        gi = 0 if h < 2 else (1 if h < 4 else 2)
                                        stt("dve", kof[:, tt_, h * 64:(h + 1) * 64], kof[:, tt_, h * 64:(h + 1) * 64], kss[:, h:h + 1], knbc[:, gi, :],
                                            ALU.mult, ALU.mult, [R("kof"), R("kss"), RL], [R("kof")])
                                for (b, odst, c0, c1) in ((1, o_wk, 0, 128), (2, o_gk, 128, 256), (3, o_dk, 256, 768)):
                                    for sq_ in range(2):
                                        k.dma("sp", odst[s0 + sq_, l].rearrange("(tt p) f -> p tt f", p=128), kof[:, 2 * sq_:2 * sq_ + 2, c0:c1],
                                              reads=[R("kof")], sem="ovst")

                    chk(2)
                    k.barrier()
                    seqs = [(0, NS, True, None)] + [(NS + s * 256, 256, False, s) for s in range(NPS)]
                    with contextlib.ExitStack() as p2:
                        LM = NS
                        Tm = [sbt(p2, U("lt%d" % i), [128, LM]) for i in range(6)]
                        ub = sbt(p2, U("ub"), [128, LM], BF16); ao = sbt(p2, U("ao"), [128, LM], BF16)
                        h0t = sbt(p2, U("h0t"), [128, 8])
                        h0s = sbt(p2, U("h0s"), [8, 128])
                        k.dma("sp", h0s[:], slru[l].rearrange("k (fc p) -> (k fc) p", p=128), writes=[R("h0s")], sem="p_h0s")
                        pbk, pr = pbank()
                        k.op("pe", lambda e: e.transpose(pbk[:, 0:8], h0s[:], identf[0:8, 0:8]), reads=[R("h0s"), RC], writes=[pr])
                        copy("dve", h0t[:], pbk[:, 0:8], [pr], [R("h0t")])
                        for (t0, L, is_s, sidx) in seqs:
                            all_c = range(t0 // TC, (t0 + L + TC - 1) // TC)
                            for fc in range(4):
                                xa, u, ra, ig, m4, ya = [t[:, 0:L] for t in Tm]
                                rd = [R("xyT_d", 0, c) for c in all_c]
                                k.dma("sp", xa, xaT_d[fc * 128:(fc + 1) * 128, t0:t0 + L], reads=rd, writes=[R("lt", 0)], sem="lru_in")
                                k.dma("sp", ya, yaT_d[fc * 128:(fc + 1) * 128, t0:t0 + L], reads=[R("xyT_d", 1, c) for c in all_c], writes=[R("lt", 5)], sem="lru_in2")
                                act(u, xa, AF.Identity, [R("lt", 0), RL], [R("lt", 1)], scale=T3[:, 4 + fc:5 + fc], bias=T3[:, 16 + fc:17 + fc])
                                stt("dve", u[:, 1:L], xa[:, 0:L - 1], T3[:, 0 + fc:1 + fc], u[:, 1:L], ALU.mult, ALU.add, [R("lt", 0), R("lt", 1), RL], [R("lt", 1)])
                                stt("dve", u[:, 0:L - 1], xa[:, 1:L], T3[:, 8 + fc:9 + fc], u[:, 0:L - 1], ALU.mult, ALU.add, [R("lt", 0), R("lt", 1), RL], [R("lt", 1)])
                                stt("dve", u[:, 0:L - 2], xa[:, 2:L], T3[:, 12 + fc:13 + fc], u[:, 0:L - 2], ALU.mult, ALU.add, [R("lt", 0), R("lt", 1), RL], [R("lt", 1)])
                                copy("pool", ub[:, 0:L], u, [R("lt", 1)], [R("ub")])
                                for kd in range(2):
                                    for n0 in range(0, L, TC):
                                        n1 = min(L, n0 + TC)
                                        for ri, dstt in ((0, ra), (1, ig)):
                                            idx = (fc * 2 + kd) * 2 + ri
                                            pz, pzr = pbank()
                                            mm(pz[:, 0:n1 - n0], [(bdw[:, idx, :], ub[:, n0:n1])], [RL, R("ub")], pzr)
                                            bcol = (20 if ri == 0 else 28) + kd * 4 + fc
                                            act(dstt[:, n0:n1], pz[:, 0:n1 - n0], AF.Sigmoid, [pzr, RL], [R("lt", 2 + ri)], bias=T3[:, bcol:bcol + 1])
                                    act(ra, ra, AF.Exp, [R("lt", 2), RL], [R("lt", 2)], scale=clam[:, kd * 4 + fc:kd * 4 + fc + 1])
                                    tt("pool", m4, ra, ra, ALU.mult, [R("lt", 2)], [R("lt", 4)])
                                    act(m4, m4, AF.Sqrt, [R("lt", 4)], [R("lt", 4)], scale=-1.0, bias=onec[:])
                                    tt("dve", ig, ig, m4, ALU.mult, [R("lt", 3), R("lt", 4)], [R("lt", 3)])
                                    tt("dve", ig, ig, u, ALU.mult, [R("lt", 3), R("lt", 1)], [R("lt", 3)])
                                    if kd == 0:
                                        init = h0t[:, fc:fc + 1] if is_s else 0.0
                                        k.op("dve", lambda e, init=init: e.tensor_tensor_scan(out=xa, data0=ra, data1=ig, initial=init, op0=ALU.mult, op1=ALU.add),
                                             reads=[R("lt", 2), R("lt", 3), R("h0t")], writes=[R("lt", 0)])
                                    else:
                                        init = h0t[:, 4 + fc:5 + fc] if is_s else 0.0
                                        k.op("dve", lambda e, init=init: e.tensor_tensor_scan(out=m4[:, ::-1], data0=ra[:, ::-1], data1=ig[:, ::-1], initial=init,
                                                                                              op0=ALU.mult, op1=ALU.add),
                                             reads=[R("lt", 2), R("lt", 3), R("h0t")], writes=[R("lt", 4)])
                                if not is_s:
                                    k.dma("sp", o_lru[sidx, l, 0, fc * 128:(fc + 1) * 128].rearrange("(p o) -> p o", o=1), xa[:, L - 1:L], reads=[R("lt", 0)], sem="olru")
                                    k.dma("sp", o_lru[sidx, l, 1, fc * 128:(fc + 1) * 128].rearrange("(p o) -> p o", o=1), m4[:, 0:1], reads=[R("lt", 4)], sem="olru")
                                tt("dve", xa, xa, m4, ALU.add, [R("lt", 0), R("lt", 4)], [R("lt", 0)])
                                tt("pool", ra, ya, ya, ALU.mult, [R("lt", 5)], [R("lt", 2)])
                                ts("dve", ra, ra, 0.044715, 1.0, ALU.mult, ALU.add, [R("lt", 2)], [R("lt", 2)])
                                tt("dve", ra, ra, ya, ALU.mult, [R("lt", 2), R("lt", 5)], [R("lt", 2)])
                                act(ra, ra, AF.Sigmoid, [R("lt", 2)], [R("lt", 2)], scale=2.0 * math.sqrt(2.0 / math.pi))
                                tt("pool", ra, ra, ya, ALU.mult, [R("lt", 2), R("lt", 5)], [R("lt", 2)])
                                tt("dve", ao[:, 0:L], xa, ra, ALU.mult, [R("lt", 0), R("lt", 2)], [R("ao")])
                                k.dma("sp", brT_d[fc * 128:(fc + 1) * 128, t0:t0 + L], ao[:, 0:L], reads=[R("ao")], writes=[R("brT_d", 0, fc, t0)], sem="aost")

                    chk(3)
                    k.barrier()
                    def key_reads(b, k0, n):
                        out_k, out_v = [], []
                        if k0 < NS:
                            for c in range(k0 // TC, (k0 + n + TC - 1) // TC):
                                out_k.append(R("kT_d", b, c)); out_v.append(R("v_d", b, c))
                            if k0 + n > NS:
                                out_k.append(R("kT_d", b, "ctx")); out_v.append(R("v_d", b, "ctx"))
                        else:
                            tok0 = NS + (k0 - KOFFP)
                            for c in range(tok0 // TC, (tok0 + n + TC - 1) // TC):
                                out_k.append(R("kT_d", b, c)); out_v.append(R("v_d", b, c))
                        return out_k, out_v

                    with contextlib.ExitStack() as p3:
                        PTn = 6
                        PT = [sbt(p3, U("pt%d" % i), [128, 512], BF16) for i in range(PTn)]
                        pt_i = [0]
                        osb = sbt(p3, U("osb"), [128, 512]); osb2 = sbt(p3, U("osb2"), [128, 512]); rz = sbt(p3, U("rz"), [128, 512])
                        ostt = [sbt(p3, U("ost%d" % i), [128, 2, 512], BF16) for i in range(2)]
                        LOOK = 2
                        pend = []

                        def _pop():
                            pv_, post_ = pend.pop(0)
                            pv_()
                            if post_ is not None:
                                post_()

                        def push(pv_, post_=None):
                            pend.append((pv_, post_))
                            while len(pend) > LOOK:
                                _pop()

                        def flush():
                            while pend:
                                _pop()

                        def score_exp(lhsT_k, rhs_q, n, kread, qread, mask=None):
                            pS, pSr = pbank(0, 3)
                            mm(pS[:, 0:n], [(lhsT_k, rhs_q)], [kread, qread], pSr)
                            i = pt_i[0] % PTn
                            pt_i[0] += 1
                            act(PT[i][:, 0:n], pS[:, 0:n], AF.Exp, [pSr], [R("pt", i)])
                            if mask is not None:
                                tt("pool", PT[i][:, 0:n], PT[i][:, 0:n], mask, ALU.mult, [R("pt", i), R("masks")], [R("pt", i)])
                            return PT[i], R("pt", i)

                        for (t0, L, is_s, sidx) in seqs:
                            k0 = 0 if is_s else KOFFP + (t0 - NS)
                            nk = (L + NCTX) if is_s else L
                            nkb = nk // 128
                            for b in (1, 2):
                                with contextlib.ExitStack() as pa:
                                    Kt = sbt(pa, U("Kt"), [128, nk], BF16)
                                    Va = sbt(pa, U("Va"), [128, nkb, 2, 65], BF16)
                                    Qts = [sbt(pa, U("Qt%d" % i), [128, 4, 128], BF16) for i in range(2)]
                                    kr, vr = key_reads(b, k0, nk)
                                    k.dma("sp", Kt[:], kT_d[b][:, k0:k0 + nk], reads=kr, writes=[R("Kt")], sem="kld")
                                    k.op("pool", lambda e: e.memset(Va[:, :, :, 64:65], 1.0), writes=[R("Va")])
                                    for hv in range(2):
                                        k.dma("sp", Va[:, :, hv, 0:64], v_d[b][k0:k0 + nk, hv * 64:(hv + 1) * 64].rearrange("(kb p) d -> p kb d", p=128), reads=vr, writes=[R("Va")], sem="vld")
                                    nqb = L // 128

                                    def load_q(qb):
                                        q0_ = t0 + qb * 128
                                        qc = q0_ // TC
                                        qrd = [R("qT_d", b, qc)] if b != 2 else [R("qT_d", 2, qc, 0), R("qT_d", 2, qc, 1)]
                                        for kvh_ in range(2):
                                            k.dma("sp", Qts[qb % 2][kvh_ * 64:(kvh_ + 1) * 64, :, :],
                                                  qT_d[b][kvh_ * 256:(kvh_ + 1) * 256, q0_:q0_ + 128].rearrange("(g d) q -> d g q", d=64),
                                                  reads=qrd, writes=[R("Qt", qb % 2)], sem="qld%d" % (qb % 2))
                                    load_q(0)
                                    for qb in range(nqb):
                                        if qb + 1 < nqb:
                                            load_q(qb + 1)
                                        q0 = t0 + qb * 128
                                        Qt = Qts[qb % 2]; RQ = R("Qt", qb % 2)
                                        ost = ostt[qb % 2]; Rost = R("ost", qb % 2)
                                        for kvh in range(2):
                                            if b == 1 and is_s:
                                                kbs = [(kb, (0 if kb == qb - 1 else (1 if kb == qb + 1 else None))) for kb in (qb - 1, qb, qb + 1) if 0 <= kb < L // 128]
                                                kbs += [(L // 128 + i, None) for i in range(NCTX // 128)]
                                            else:
                                                kbs = [(kb, None) for kb in range(nkb)]
                                            pO, pOr = PB[3 + kvh], R("pb", 3 + kvh)
                                            for i, (kb, mi) in enumerate(kbs):
                                                ptile, ptr = score_exp(Kt[kvh * 64:(kvh + 1) * 64, kb * 128:(kb + 1) * 128],
                                                                       Qt[kvh * 64:(kvh + 1) * 64, :, :].rearrange("p g q -> p (g q)"), 512, R("Kt"), RQ,
                                                                       mask=None if mi is None else masks[:, mi, :])
                                                last = (i == len(kbs) - 1)

                                                def pv_(pO=pO, pOr=pOr, kb=kb, kvh=kvh, ptile=ptile, ptr=ptr, i=i, last=last):
                                                    mm(pO[0:65, :], [(Va[:, kb, kvh, :], ptile[:])], [R("Va"), ptr], pOr, start=(i == 0), stop=last)
                                                post_ = None
                                                if last:
                                                    def post_(pO=pO, pOr=pOr, kvh=kvh, ost=ost, Rost=Rost, q0=q0, b=b, par=qb % 2):
                                                        copy("act", osb[0:65, :], pO[0:65, :], [pOr], [R("osb")])
                                                        if b == 1:
                                                            tt("dve", osb[64:65, :], osb[64:65, :], sinkT[64:65, kvh * 4:(kvh + 1) * 4, :].rearrange("p g q -> p (g q)"), ALU.add,
                                                               [R("osb"), RL], [R("osb")])
                                                        recip(rz[64:65, :], osb[64:65, :], [R("osb")], [R("rz")])
                                                        pZ, pZr = pbank(5, 7)
                                                        mm(pZ[0:64, :], [(onesf[64:65, 0:64], rz[64:65, :])], [RC, R("rz")], pZr)
                                                        tt("dve", ost[0:64, kvh, :], osb[0:64, :], pZ[0:64, :], ALU.mult, [R("osb"), pZr], [Rost])
                                                        if kvh == 1:
                                                            for kv2 in range(2):
                                                                k.dma("sp", brT_d[b * 512 + kv2 * 256:b * 512 + (kv2 + 1) * 256, q0:q0 + 128].rearrange("(g d) q -> d g q", d=64),
                                                                      ost[0:64, kv2, :].rearrange("p (g q) -> p g q", g=4), reads=[Rost], writes=[R("brT_d", b, q0, kv2)], sem="ost%d" % par)
                                                push(pv_, post_)
                                    flush()
                                k.barrier()
                            with contextlib.ExitStack() as pa:
                                Kt = sbt(pa, U("Ktd"), [128, 4, nk], BF16)
                                Vd = sbt(pa, U("Vd"), [128, nkb, 512], BF16)
                                Qds = [sbt(pa, U("Qd%d" % i), [128, 4, 512], BF16) for i in range(2)]
                                o32 = sbt(pa, U("o32"), [128, 512])
                                dsts = [sbt(pa, U("dst%d" % i), [128, 512], BF16) for i in range(2)]
                                kr, vr = key_reads(3, k0, nk)
                                k.dma("sp", Kt[:], kT_d[3][:, k0:k0 + nk].rearrange("(h p) t -> p h t", p=128), reads=kr, writes=[R("Ktd")], sem="kld")
                                k.dma("sp", Vd[:], v_d[3][k0:k0 + nk, :].rearrange("(kb p) f -> p kb f", p=128), reads=vr, writes=[R("Vd")], sem="vld")
                                QN = min(512, L)
                                nqq = L // QN

                                def load_qd(qq):
                                    q0_ = t0 + qq * QN
                                    k.dma("sp", Qds[qq % 2][:, :, 0:QN], qT_d[3][:, q0_:q0_ + QN].rearrange("(h p) t -> p h t", p=128), reads=[R("qT_d", 3, q0_ // TC)],
                                          writes=[R("Qd", qq % 2)], sem="qdl%d" % (qq % 2))
                                load_qd(0)
                                hcnt = [0]
                                for qq in range(nqq):
                                    if qq + 1 < nqq:
                                        load_qd(qq + 1)
                                    q0 = t0 + qq * QN
                                    Qd = Qds[qq % 2]; RQd = R("Qd", qq % 2)
                                    for h in range(4):
                                        accs = [(PB[3], R("pb", 3), PB[4], R("pb", 4)), (PB[5], R("pb", 5), PB[6], R("pb", 6))]
                                        for kb in range(nkb):
                                            for jm in range(2):
                                                pO, pOr, pZ, pZr = accs[jm]
                                                ptile, ptr = score_exp(Kt[jm * 64:(jm + 1) * 64, h, kb * 128:(kb + 1) * 128], Qd[jm * 64:(jm + 1) * 64, h, 0:QN], QN,
                                                                       R("Ktd"), RQd)

                                                def pv_(pO=pO, pOr=pOr, pZ=pZ, pZr=pZr, kb=kb, h=h, ptile=ptile, ptr=ptr):
                                                    mm(pO[:, 0:QN], [(Vd[:, kb, h * 128:(h + 1) * 128], ptile[:, 0:QN])], [R("Vd"), ptr], pOr, start=(kb == 0), stop=(kb == nkb - 1))
                                                    mm(pZ[:, 0:QN], [(onesb[:], ptile[:, 0:QN])], [R("onesb"), ptr], pZr, start=(kb == 0), stop=(kb == nkb - 1))
                                                post_ = None
                                                if kb == nkb - 1 and jm == 1:
                                                    def post_(accs=accs, h=h, q0=q0):
                                                        di = hcnt[0] % 2
                                                        hcnt[0] += 1
                                                        dst_ = dsts[di]
                                                        recip(rz[:, 0:QN], accs[0][2][:, 0:QN], [accs[0][3]], [R("rz")])
                                                        tt("dve", osb[:, 0:QN], accs[0][0][:, 0:QN], rz[:, 0:QN], ALU.mult, [accs[0][1], R("rz")], [R("osb")])
                                                        recip(rz[:, 0:QN], accs[1][2][:, 0:QN], [accs[1][3]], [R("rz")])
                                                        tt("dve", osb2[:, 0:QN], accs[1][0][:, 0:QN], rz[:, 0:QN], ALU.mult, [accs[1][1], R("rz")], [R("osb2")])
                                                        stt("dve", o32[:, 0:QN], osb2[:, 0:QN], lamt[:, 1:2], osb[:, 0:QN], ALU.mult, ALU.add, [R("osb"), R("osb2"), RL], [R("o32")])
                                                        tt("pool", osb[:, 0:QN], o32[:, 0:QN], o32[:, 0:QN], ALU.mult, [R("o32")], [R("osb")])
                                                        pM, pMr = PB[7], R("pb", 7)
                                                        mm(pM[:, 0:QN], [(avg128, osb[:, 0:QN])], [RC, R("osb")], pMr)
                                                        act(rz[:, 0:QN], pM[:, 0:QN], AF.Ln, [pMr], [R("rz")], bias=epsc[:])
                                                        act(rz[:, 0:QN], rz[:, 0:QN], AF.Exp, [R("rz")], [R("rz")], scale=-0.5)
                                                        stt("dve", dst_[:, 0:QN], o32[:, 0:QN], lamt[:, 2:3], rz[:, 0:QN], ALU.mult, ALU.mult, [R("o32"), R("rz"), RL], [R("dst", di)])
                                                        k.dma("sp", brT_d[3 * 512 + h * 128:3 * 512 + (h + 1) * 128, q0:q0 + QN], dst_[:, 0:QN], reads=[R("dst", di)],
                                                              writes=[R("brT_d", 3, q0, h)], sem="dst%d" % di)
                                                push(pv_, post_)
                                flush()
                            k.barrier()

                    chk(4)
                    k.barrier()
                    def br_reads(c):
                        t0 = c * TC
                        rr = []
                        if c < NSCH:
                            rr += [R("brT_d", 0, fc, 0) for fc in range(4)]
                        else:
                            for s in range(NPS):
                                if NS + s * 256 >= t0 and NS + s * 256 < t0 + TC:
                                    rr += [R("brT_d", 0, fc, NS + s * 256) for fc in range(4)]
                        for q0 in range(t0, t0 + TC, 128):
                            rr += [R("brT_d", b, q0, kvh) for b in (1, 2) for kvh in range(2)]
                        for h in range(4):
                            if c < NSCH:
                                rr.append(R("brT_d", 3, t0, h))
                            else:
                                rr += [R("brT_d", 3, t0, h), R("brT_d", 3, t0 + 256, h)]
                        return rr

                    with contextlib.ExitStack() as p4:
                        hT = sbt(p4, U("hT3"), [128, KC, TC], BF16); bT = sbt(p4, U("bT"), [128, KC, TC], BF16)
                        mT = sbt(p4, U("mT"), [128, KC, TC], BF16)
                        xt = sbt(p4, U("xt3"), [128, 4, D]); xn = bT[:].rearrange("p (a e) b -> p a (e b)", a=4)
                        gbc = sbt(p4, U("gbc"), [128, D])
                        acc = sbt(p4, U("acc"), [128, 4, TC]); sg = sbt(p4, U("sg"), [128, TC]); tmp = sbt(p4, U("tmp"), [128, TC])
                        ssq = sbt(p4, U("ssq3"), [128, 8])
                        wbrr = [sbt(p4, U("wbrr%d" % i_), [128, 4, 512], BF16) for i_ in range(2)]
                        wbr_i = [0]
                        items = []
                        for c in range(NCH):
                            for mg in range(4):
                                for b in range(4):
                                    items.append(("gate", l, 0, (b * 4 + mg) * 512))
                            for ct in range(4):
                                items.append(("o", l, 0, ct * 512))
                        ws = WStream(items)
                        cur_j = -1
                        for c in range(NCH):
                            j = 0 if c < NSCH else 1
                            t0 = c * TC
                            if j != cur_j:
                                k.dma("sp", gbc[:], gbc_d[0, j], reads=[R("gbc", 0, j)], writes=[R("gbcs")], sem="gbl")
                                cur_j = j
                            k.dma("sp", hT[:], hT_d[:, t0:t0 + TC].rearrange("(kc p) t -> p kc t", p=128), reads=[R("hT_d", c)], writes=[R("hT3")], sem="hTld")
                            k.dma("sp", bT[:], brT_d[:, t0:t0 + TC].rearrange("(kc p) t -> p kc t", p=128), reads=br_reads(c), writes=[R("bT")], sem="bTld")
                            xsrc_, xdst_ = xsrc(l, c)
                            Rx = R("xres", c)
                            k.dma("sp", xt[:], xsrc_, reads=[Rx], writes=[R("xt3")], sem="xt3")
                            for mg in range(4):
                                for b in range(4):
                                    wg_, wgr_ = ws.get()
                                    si_ = wbr_i[0] % 2
                                    wbr_i[0] += 1
                                    wb_, wbr_ = wbrr[si_], R("wbrr", si_)
                                    k.dma("sp", wb_[:], wbf["br"][l, b * 512:(b + 1) * 512, mg * 512:(mg + 1) * 512].rearrange("(kc p) n -> p kc n", p=128),
                                          reads=wres("br", l), writes=[wbr_], sem="wbr%d" % si_)
                                    for mm_ in range(4):
                                        m = mg * 4 + mm_
                                        pg, pgr = pbank()
                                        mm(pg[:], [(wg_[:, kc, mm_ * 128:(mm_ + 1) * 128], hT[:, kc, :]) for kc in range(KC)], [wgr_, R("hT3")], pgr)
                                        pp, ppr = pbank()
                                        mm(pp[:], [(wb_[:, k4, mm_ * 128:(mm_ + 1) * 128], bT[:, b * 4 + k4, :]) for k4 in range(4)], [wbr_, R("bT")], ppr)
                                        act(sg[:], pg[:], AF.Sigmoid, [pgr, RL], [R("sg")], bias=T2[:, b * 16 + m:b * 16 + m + 1])
                                        if b == 0:
                                            tt("dve", acc[:, mm_, :], sg[:], pp[:], ALU.mult, [R("sg"), ppr], [R("acc", mm_)])
                                        else:
                                            tt("dve", tmp[:], sg[:], pp[:], ALU.mult, [R("sg"), ppr], [R("tmp")])
                                            if b < 3:
                                                tt("pool", acc[:, mm_, :], acc[:, mm_, :], tmp[:], ALU.add, [R("acc", mm_), R("tmp")], [R("acc", mm_)])
                                            else:
                                                tt("pool", mT[:, m, :], acc[:, mm_, :], tmp[:], ALU.add, [R("acc", mm_), R("tmp")], [R("mT")])
                            for ct in range(4):
                                wo_, wor_ = ws.get()
                                for tt_ in range(4):
                                    po, por = pbank()
                                    mm(po[:], [(mT[:, kc, tt_ * 128:(tt_ + 1) * 128], wo_[:, kc, :]) for kc in range(KC)], [wor_, R("mT")], por)
                                    tt("dve", tmp[:], po[:], gbc[:, ct * 512:(ct + 1) * 512], ALU.mult, [por, R("gbcs")], [R("tmp")])
                                    tt("pool", xt[:, tt_, ct * 512:(ct + 1) * 512], xt[:, tt_, ct * 512:(ct + 1) * 512], tmp[:], ALU.add, [R("xt3"), R("tmp")], [R("xt3")])
                            k.dma("sp", xdst_, xt[:], reads=[R("xt3")], writes=[Rx], sem="x1st")
                            for tt_ in range(4):
                                accf = acc[:].rearrange("p a b -> p (a b)")
                                tt("dve", accf, xt[:, tt_, :], xt[:, tt_, :], ALU.mult, [R("xt3")], [R("acc", i_) for i_ in range(4)])
                                k.op("dve", lambda e, tt_=tt_, accf=accf: e.tensor_reduce(out=ssq[:, tt_:tt_ + 1], in_=accf, axis=AX.X, op=ALU.add),
                                     reads=[R("acc", i_) for i_ in range(4)], writes=[R("ssq3")])
                            act(ssq[:, 4:8], ssq[:, 0:4], AF.Sqrt, [R("ssq3")], [R("ssq3")], scale=1.0 / D, bias=epsc[:])
                            recip(ssq[:, 4:8], ssq[:, 4:8], [R("ssq3")], [R("ssq3")])
                            for tt_ in range(4):
                                ts("dve", xn[:, tt_, :], xt[:, tt_, :], ssq[:, 4 + tt_:5 + tt_], None, ALU.mult, None, [R("xt3"), R("ssq3")], [R("bT")])
                            for kc in range(KC):
                                pk, pkr = pbank()
                                pkb = pk[:].bitcast(BF16)
                                def fnT(e, pkb=pkb, kc=kc):
                                    ins = None
                                    for tt_ in range(4):
                                        ins = e.transpose(pkb[:, tt_ * 128:(tt_ + 1) * 128], xn[:, tt_, kc * 128:(kc + 1) * 128], idb[:])
                                    return ins
                                k.op("pe", fnT, reads=[R("bT"), R("idb")], writes=[pkr])
                                act(hT[:, kc, :], pkb[:, 0:TC], AF.Identity, [pkr, RL], [R("hT3")], scale=A2[:, kc, j:j + 1], bias=modT[:, 48 + kc, j:j + 1])
                            k.dma("sp", hT_d[:, t0:t0 + TC].rearrange("(kc p) t -> p kc t", p=128), hT[:], reads=[R("hT3")], writes=[R("hT_d", c)], sem="hTst")

                    chk(5)
                    k.barrier()
                    with contextlib.ExitStack() as p5:
                        hT = sbt(p5, U("hT5"), [128, KC, TC], BF16)
                        aT = sbt(p5, U("aT"), [128, 64, TC], BF16)
                        rl = sbt(p5, U("rl"), [128, TC])
                        gbc = sbt(p5, U("gbc5"), [128, D])
                        xc = sbt(p5, U("xc"), [128, 4, 512]); tmp = sbt(p5, U("tmp5"), [128, 512])
                        items = []
                        for c in range(NCH):
                            for ft in range(16):
                                items.append(("ff1", l, 0, ft * 512))
                            for ct in range(4):
                                for g in range(4):
                                    items.append(("ff2", l, g * 2048, ct * 512))
                        ws = WStream(items)
                        cur_j = -1
                        for c in range(NCH):
                            j = 0 if c < NSCH else 1
                            t0 = c * TC
                            if j != cur_j:
                                k.dma("sp", gbc[:], gbc_d[1, j], reads=[R("gbc", 1, j)], writes=[R("gbc5")], sem="gbl")
                                cur_j = j
                            k.dma("sp", hT[:], hT_d[:, t0:t0 + TC].rearrange("(kc p) t -> p kc t", p=128), reads=[R("hT_d", c)], writes=[R("hT5")], sem="hTld")
                            _, xdst_ = xsrc(l, c)
                            Rx = R("xres", c)
                            for ft in range(16):
                                w1, w1r = ws.get()
                                for mm_ in range(4):
                                    f = ft * 4 + mm_
                                    pf, pfr = pbank()
                                    mm(pf[:], [(w1[:, kc, mm_ * 128:(mm_ + 1) * 128], hT[:, kc, :]) for kc in range(KC)], [w1r, R("hT5")], pfr)
                                    act(rl[:], pf[:], AF.Relu, [pfr, RL], [R("rl")], bias=T2[:, 64 + f:65 + f])
                                    tt("dve" if mm_ % 2 == 0 else "pool", aT[:, f, :], rl[:], rl[:], ALU.mult, [R("rl")], [R("aT", f)])
                            for ct in range(4):
                                k.dma("sp", xc[:], xdst_[:, :, ct * 512:(ct + 1) * 512], reads=[Rx], writes=[R("xc")], sem="xc")
                                banks = [pbank() for _ in range(4)]
                                for g in range(4):
                                    w2, w2r = ws.get()
                                    for tt_ in range(4):
                                        po, por = banks[tt_]
                                        mm(po[:], [(aT[:, g * 16 + kc, tt_ * 128:(tt_ + 1) * 128], w2[:, kc, :]) for kc in range(KC)],
                                           [w2r] + [R("aT", g * 16 + kc) for kc in range(KC)], por, start=(g == 0), stop=(g == 3))
                                for tt_ in range(4):
                                    po, por = banks[tt_]
                                    tt("dve", tmp[:], po[:], b2bc[:, ct * 512:(ct + 1) * 512], ALU.add, [por, RL], [R("tmp5")])
                                    tt("pool", tmp[:], tmp[:], gbc[:, ct * 512:(ct + 1) * 512], ALU.mult, [R("tmp5"), R("gbc5")], [R("tmp5")])
                                    tt("dve", xc[:, tt_, :], xc[:, tt_, :], tmp[:], ALU.add, [R("xc"), R("tmp5")], [R("xc")])
                                k.dma("sp", xdst_[:, :, ct * 512:(ct + 1) * 512], xc[:], reads=[R("xc")], writes=[Rx], sem="x2st")
        cast_layer(0)
        try:
            _layers()
        except _Stop:
            pass
        k.finish()
    nc._n_inst = k.n_inst
    return nc


def host_consts(NS):
    n_freq = 16
    t = np.arange(NS)
    row = (t // GRID_W).astype(np.float32)
    col = (t % GRID_W).astype(np.float32)
    inv = (10000.0 ** (-np.arange(n_freq, dtype=np.float32) / n_freq)).astype(np.float32)
    C = np.zeros((128, NS), np.float32)
    S = np.zeros((128, NS), np.float32)
    for p in range(128):
        d = p % 64
        axis, half, fr = d // 32, (d % 32) // 16, d % 16
        pos = row if axis == 0 else col
        ang = (pos * inv[fr]).astype(np.float32)
        C[p] = np.cos(ang)
        S[p] = np.sin(ang) * (-1.0 if half == 0 else 1.0)
    cm = np.zeros((5, 128, 128), np.float32)
    cm[0] = np.eye(128)
    for p in range(128):
        d = p % 64
        half = (d % 32) // 16
        cm[1][p, p + 16 if half == 0 else p - 16] = 1.0
    for hh in range(2):
        cm[2][hh * 64:(hh + 1) * 64, hh * 64:(hh + 1) * 64] = 1.0 / 64
    cm[3][:] = 1.0 / 128
    cm[4][:] = 1.0
    kk = np.arange(128)[:, None]
    qq = np.arange(128)[None, :]
    mA = (kk >= qq).astype(np.float32)
    mB = (kk <= qq).astype(np.float32)
    cmask = np.stack([np.tile(mA, (1, 4)), np.tile(mB, (1, 4))]).astype(ml_dtypes.bfloat16)
    cidb = np.eye(128, dtype=np.float32).astype(ml_dtypes.bfloat16)
    return {"ropeC": C, "ropeS": S, "cmat": cm, "cmask": cmask, "cidb": cidb}


WEIGHT_KEYS = ["w_ada", "b_ada", "norm1_g", "norm2_g", "w_in", "conv_w", "conv_b", "lru_wr", "lru_br", "lru_wi", "lru_bi",
               "lru_lambda", "win_qn", "win_kn", "win_sink", "grid_qn", "grid_kn", "diff_qn", "diff_kn", "diff_lq1",
               "diff_lk1", "diff_lq2", "diff_lk2", "diff_out_g", "w_branch", "w_gate", "b_gate", "w_o", "w_ff1", "b_ff1",
               "w_ff2", "b_ff2"]


def run(inputs, n_cores, NS, NPS, DEPTH):
    f = lambda a: np.ascontiguousarray(np.asarray(a, dtype=np.float32))
    inp = {kk: f(v) for kk, v in inputs.items()}
    nb_s = inp["x_sample"].shape[0]
    nc = build_nc(NS, NPS, DEPTH)
    consts = host_consts(NS)
    in_maps = []
    for core in range(n_cores):
        b = core % nb_s
        m = {kk: inp[kk] for kk in WEIGHT_KEYS}
        m.update(consts)
        m["xs"] = inp["x_sample"][b]
        m["xp"] = f(inp["x_prompt"][core * NPS:(core + 1) * NPS].reshape(NPS * 256, D))
        m["cwk"] = f(inp["cache_win_k"][b].reshape(DEPTH, NCTX, 128)); m["cwv"] = f(inp["cache_win_v"][b].reshape(DEPTH, NCTX, 128))
        m["cgk"] = f(inp["cache_grid_k"][b].reshape(DEPTH, NCTX, 128)); m["cgv"] = f(inp["cache_grid_v"][b].reshape(DEPTH, NCTX, 128))
        m["cdk"] = f(inp["cache_diff_k"][b].reshape(DEPTH, NCTX, 512)); m["cdv"] = f(inp["cache_diff_v"][b].reshape(DEPTH, NCTX, 512))
        m["slru"] = f(inp["state_lru"][b])
        m["cond"] = f(np.stack([inp["c"][b], inp["c_ctx"]]))
        in_maps.append(m)
    res = run_bass_kernel_spmd(nc, in_maps, core_ids=list(range(n_cores)))
    rs = res.results
    LAST[0] = rs
    y_p = np.concatenate([r["y_p"].reshape(NPS, 256, D) for r in rs], axis=0)
    y_s = np.stack([rs[b]["y_s"] for b in range(nb_s)], axis=0)
    cat = lambda kk, shp: np.concatenate([r[kk].reshape((NPS, DEPTH, 256) + shp) for r in rs], axis=0)
    outs = (y_p, y_s, cat("o_wk", (2, 64)), cat("o_wv", (2, 64)), cat("o_gk", (2, 64)), cat("o_gv", (2, 64)),
            cat("o_dk", (4, 2, 64)), cat("o_dv", (4, 128)),
            np.concatenate([r["o_lru"].reshape(NPS, DEPTH, 2, 512) for r in rs], axis=0))
    return tuple(np.ascontiguousarray(o, dtype=np.float32) for o in outs)


def kernel(**inputs):
    return run(inputs, 8, 4096, 4, 4)
```

```python
import contextlib
import math
import numpy as np
import ml_dtypes
import concourse.bass as bass
import concourse.mybir as mybir
from concourse.bass_utils import run_bass_kernel_spmd

F32 = mybir.dt.float32
BF16 = mybir.dt.bfloat16
AF = mybir.ActivationFunctionType
ALU = mybir.AluOpType
AX = mybir.AxisListType

D = 2048
KC = 16
TC = 512
NCTX = 512
EPS = 1e-6
GRID_W = 64
WMAP = {"ada": (2048, 12288), "in": (2048, 4096), "gate": (2048, 8192), "br": (2048, 2048),
        "o": (2048, 2048), "ff1": (2048, 8192), "ff2": (8192, 2048)}
WNAME = {"ada": "w_ada", "in": "w_in", "gate": "w_gate", "br": "w_branch", "o": "w_o", "ff1": "w_ff1",
         "ff2": "w_ff2"}


class Res:
    __slots__ = ("name", "w", "r")

    def __init__(self, name):
        self.name = name
        self.w = None
        self.r = {}


class KB:
    def __init__(self, nc, stack):
        self.nc = nc
        self.stack = stack
        self.eng = {"pe": nc.tensor, "act": nc.scalar, "dve": nc.vector, "pool": nc.gpsimd, "sp": nc.sync}
        self.esem = {e: stack.enter_context(nc.semaphore("es_" + e)) for e in self.eng}
        self.cnt = {e: 0 for e in self.eng}
        self.seen = {e: {} for e in self.eng}
        self.res = {}
        self.dsems = {}
        self.n_inst = 0

    def R(self, *key):
        r = self.res.get(key)
        if r is None:
            r = Res(key)
            self.res[key] = r
        return r

    def dsem(self, name):
        d = self.dsems.get(name)
        if d is None:
            d = [self.stack.enter_context(self.nc.semaphore("ds_" + name)), 0]
            self.dsems[name] = d
        return d

    def _collect(self, e, reads, writes):
        waits = {}

        def need(tok, war=False):
            if tok is None:
                return
            kind, key, val = tok
            if kind == "eng":
                if key == e and (war or e == "pe"):
                    return
                waits[("eng", key)] = max(waits.get(("eng", key), 0), val)
            else:
                waits[("dma", key)] = 1
        for r in reads:
            need(r.w)
        for w in writes:
            need(w.w)
            for t in w.r.values():
                need(t, war=True)
        return waits

    def _emit_waits(self, e, waits):
        eng = self.eng[e]
        for (kind, key) in waits:
            if kind == "eng":
                val = waits[(kind, key)]
                sem = self.esem[key]
            else:
                d = self.dsems[key]
                val = d[1]
                sem = d[0]
            if self.seen[e].get((kind, key), 0) >= val:
                continue
            eng.wait_ge(sem, val)
            self.seen[e][(kind, key)] = val

    def op(self, e, fn, reads=(), writes=()):
        if DEAD[0]:
            return None
        waits = self._collect(e, reads, writes)
        self._emit_waits(e, waits)
        ins = fn(self.eng[e])
        self.cnt[e] += 1
        ins.then_inc(self.esem[e], 1)
        tok = ("eng", e, self.cnt[e])
        for r in reads:
            r.r[("eng", e)] = tok
        for w in writes:
            w.w = tok
            w.r = {}
        self.n_inst += 1
        return tok

    def dma(self, q, out, in_, reads=(), writes=(), sem=None, **kw):
        if DEAD[0]:
            return None
        waits = self._collect(q, reads, writes)
        self._emit_waits(q, waits)
        d = self.dsem(sem)
        ins = self.eng[q].dma_start(out=out, in_=in_, **kw)
        d[1] += 16
        ins.then_inc(d[0], 16)
        tok = ("dma", sem, d[1])
        for r in reads:
            r.r[("dma", sem)] = tok
        for w in writes:
            w.w = tok
            w.r = {}
        self.n_inst += 1
        return tok

    def barrier(self):
        if DEAD[0]:
            return
        for e in self.eng:
            eng = self.eng[e]
            for o in self.eng:
                if o == e or self.cnt[o] == 0:
                    continue
                if self.seen[e].get(("eng", o), 0) >= self.cnt[o]:
                    continue
                eng.wait_ge(self.esem[o], self.cnt[o])
                self.seen[e][("eng", o)] = self.cnt[o]
            for name, d in self.dsems.items():
                if name.startswith("cast") or name == "cstv" or d[1] == 0:
                    continue
                if self.seen[e].get(("dma", name), 0) >= d[1]:
                    continue
                eng.wait_ge(d[0], d[1])
                self.seen[e][("dma", name)] = d[1]

    def finish(self):
        sp = self.eng["sp"]
        for name, d in self.dsems.items():
            if d[1] > 0:
                sp.wait_ge(d[0], d[1])
        for e in self.eng:
            if e != "sp" and self.cnt[e] > 0:
                sp.wait_ge(self.esem[e], self.cnt[e])


class _Stop(Exception):
    pass


STOP = [99]


DEAD = [False]
DBG = [False]
LAST = [None]


def chk(n):
    if STOP[0] == n:
        DEAD[0] = True


def build_nc(NS, NPS, DEPTH):
    NP = NPS * 256
    NT = NS + NP
    NSCH = NS // TC
    NCH = NT // TC
    KOFFP = NS + NCTX
    TK = NS + NCTX + NP

    nc = bass.Bass("TRN2", target_bir_lowering=False)
    DEAD[0] = False

    def din(name, shape, dt=F32):
        return nc.dram_tensor(name, list(shape), dt, kind="ExternalInput").ap()

    def dout(name, shape, dt=F32):
        return nc.dram_tensor(name, list(shape), dt, kind="ExternalOutput").ap()

    def dscr(name, shape, dt=BF16):
        return nc.dram_tensor(name, list(shape), dt, kind="Internal").ap()

    xs = din("xs", [NS, D]); xp = din("xp", [NP, D])
    cwk = din("cwk", [DEPTH, NCTX, 128]); cwv = din("cwv", [DEPTH, NCTX, 128])
    cgk = din("cgk", [DEPTH, NCTX, 128]); cgv = din("cgv", [DEPTH, NCTX, 128])
    cdk = din("cdk", [DEPTH, NCTX, 512]); cdv = din("cdv", [DEPTH, NCTX, 512])
    slru = din("slru", [DEPTH, 2, 512])
    cond = din("cond", [2, D])
    W = {}
    for kk, (K_, N_) in WMAP.items():
        if kk == "br":
            W[kk] = din("w_branch", [DEPTH, 4, 512, 2048])
        else:
            W[kk] = din(WNAME[kk], [DEPTH, K_, N_])
    b_ada = din("b_ada", [DEPTH, 12288]); norm1_g = din("norm1_g", [DEPTH, D]); norm2_g = din("norm2_g", [DEPTH, D])
    conv_w = din("conv_w", [DEPTH, 4, 512]); conv_b = din("conv_b", [DEPTH, 512])
    lru_wr = din("lru_wr", [DEPTH, 2, 8, 64, 64]); lru_br = din("lru_br", [DEPTH, 2, 512])
    lru_wi = din("lru_wi", [DEPTH, 2, 8, 64, 64]); lru_bi = din("lru_bi", [DEPTH, 2, 512])
    lru_lambda = din("lru_lambda", [DEPTH, 2, 512])
    gains = {n: din(n, [DEPTH, 64]) for n in ("win_qn", "win_kn", "grid_qn", "grid_kn", "diff_qn", "diff_kn",
                                           "diff_lq1", "diff_lk1", "diff_lq2", "diff_lk2")}
    win_sink = din("win_sink", [DEPTH, 8]); diff_out_g = din("diff_out_g", [DEPTH, 128])
    b_gate = din("b_gate", [DEPTH, 8192]); b_ff1 = din("b_ff1", [DEPTH, 8192]); b_ff2 = din("b_ff2", [DEPTH, D])
    ropeC = din("ropeC", [128, NS]); ropeS = din("ropeS", [128, NS])
    cmat = din("cmat", [5, 128, 128])
    cmask = din("cmask", [2, 128, 512], BF16)
    cidb = din("cidb", [128, 128], BF16)

    y_s = dout("y_s", [NS, D]); y_p = dout("y_p", [NP, D])
    o_wk = dout("o_wk", [NPS, DEPTH, 256, 128]); o_wv = dout("o_wv", [NPS, DEPTH, 256, 128])
    o_gk = dout("o_gk", [NPS, DEPTH, 256, 128]); o_gv = dout("o_gv", [NPS, DEPTH, 256, 128])
    o_dk = dout("o_dk", [NPS, DEPTH, 256, 512]); o_dv = dout("o_dv", [NPS, DEPTH, 256, 512])
    o_lru = dout("o_lru", [NPS, DEPTH, 2, 512])

    wbf = {kk: dscr("wbf_" + kk, [DEPTH, K_, N_]) for kk, (K_, N_) in WMAP.items()}
    hT_d = dout("hT_d", [D, NT], BF16) if DBG[0] else dscr("hT_d", [D, NT])
    xaT_d = dscr("xaT_d", [512, NT], F32); yaT_d = dscr("yaT_d", [512, NT], F32)
    dd = (lambda n, s_: dout(n, s_, BF16)) if DBG[0] else dscr
    qT_d = {b: dd("qT_d%d" % b, [512, NT]) for b in (1, 2, 3)}
    kT_d = {1: dd("kT_d1", [128, TK]), 2: dd("kT_d2", [128, TK]), 3: dd("kT_d3", [512, TK])}
    v_d = {1: dd("v_d1", [TK, 128]), 2: dd("v_d2", [TK, 128]), 3: dd("v_d3", [TK, 512])}
    brT_d = dout("brT_d", [D, NT], BF16) if DBG[0] else dscr("brT_d", [D, NT])
    gbc_d = dout("gbc_d", [2, 2, 128, D], F32) if DBG[0] else dscr("gbc_d", [2, 2, 128, D], F32)

    def xsrc(l, c):
        if c < NSCH:
            src = xs if l == 0 else y_s
            dst = y_s
            r0 = c * TC
        else:
            src = xp if l == 0 else y_p
            dst = y_p
            r0 = (c - NSCH) * TC
        f = lambda t: t[r0:r0 + TC, :].rearrange("(tt p) d -> p tt d", p=128)
        return f(src), f(dst)

    with contextlib.ExitStack() as st:
        k = KB(nc, st)
        R = k.R

        def sbt(stack, name, shape, dt=F32):
            return stack.enter_context(nc.sbuf_tensor(name, list(shape), dt))

        PB = [st.enter_context(nc.psum_tensor("pb%d" % i, [128, 512], F32)) for i in range(8)]
        pb_i = [0]

        def pbank(lo=0, hi=8):
            i = lo + (pb_i[0] % (hi - lo))
            pb_i[0] += 1
            return PB[i], R("pb", i)

        cm = sbt(st, "cm", [128, 5, 128])
        idb = sbt(st, "idb", [128, 128], BF16)
        onesb = sbt(st, "onesb", [128, 128], BF16)
        masks = sbt(st, "masks", [128, 2, 512], BF16)
        epsc = sbt(st, "epsc", [128, 1]); onec = sbt(st, "onec", [128, 1])
        NW = 3
        wring = [sbt(st, "wring%d" % i, [128, KC, 512], BF16) for i in range(NW)]
        k.dma("sp", cm[:], cmat.rearrange("c p n -> p c n"), writes=[R("cm")], sem="c0")
        k.dma("sp", idb[:], cidb[:, :], writes=[R("idb")], sem="c0")
        k.dma("sp", masks[:], cmask.rearrange("c p n -> p c n"), writes=[R("masks")], sem="c0")
        k.op("dve", lambda e: e.memset(epsc[:], EPS), writes=[R("epsc")])
        k.op("dve", lambda e: e.memset(onec[:], 1.0), writes=[R("onec")])
        k.op("dve", lambda e: e.memset(onesb[:], 1.0), writes=[R("onesb")])
        identf = cm[:, 0, :]; permf = cm[:, 1, :]; bd64 = cm[:, 2, :]; avg128 = cm[:, 3, :]; onesf = cm[:, 4, :]
        RC = R("cm")

        def cast_layer(l):
            for kk, (K_, N_) in WMAP.items():
                src = W[kk][l] if kk != "br" else W[kk][l].rearrange("b c n -> (b c) n")
                npieces = K_ // 512
                for i in range(npieces):
                    k.dma("pool", wbf[kk][l, i * 512:(i + 1) * 512, :], src[i * 512:(i + 1) * 512, :],
                          writes=[R("wbf", kk, l, i)], sem="cast%d" % l, max_dma_last_dim=4096)

        def wres(kk, l):
            return [R("wbf", kk, l, i) for i in range(WMAP[kk][0] // 512)]

        class WStream:
            cur = [0]

            def __init__(self, items):
                self.items = items
                self.issued = 0
                self.taken = 0
                self.slots = {}

            def _issue(self):
                kk, l, r0, c0 = self.items[self.issued]
                s = WStream.cur[0] % NW
                WStream.cur[0] += 1
                src = wbf[kk][l, r0:r0 + 2048, c0:c0 + 512].rearrange("(kc p) n -> p kc n", p=128)
                k.dma("sp", wring[s][:], src, reads=wres(kk, l), writes=[R("wring", s)], sem="w%d" % s)
                self.slots[self.issued] = s
                self.issued += 1

            def get(self):
                while self.issued < len(self.items) and self.issued < self.taken + NW - 1:
                    self._issue()
                if self.issued <= self.taken:
                    self._issue()
                s = self.slots.pop(self.taken)
                self.taken += 1
                return wring[s], R("wring", s)

        def mm(out_ap, pairs, reads, wr, start=True, stop=True):
            def fn(e):
                n = len(pairs)
                ins = None
                for i, (l_, r_) in enumerate(pairs):
                    ins = e.matmul(out_ap, l_, r_, start=(start and i == 0), stop=(stop and i == n - 1))
                return ins
            k.op("pe", fn, reads=reads, writes=[wr])

        def act(out, in_, func, reads, writes, scale=1.0, bias=None, eng="act"):
            kw = {}
            if bias is not None:
                kw["bias"] = bias
            k.op("act", lambda e: e.activation(out=out, in_=in_, func=func, scale=scale, **kw), reads=reads, writes=writes)

        def tt(eng, out, in0, in1, op, reads, writes):
            k.op(eng, lambda e: e.tensor_tensor(out=out, in0=in0, in1=in1, op=op), reads=reads, writes=writes)

        def ts(eng, out, in0, s1, s2, op0, op1, reads, writes):
            if s2 is None:
                k.op(eng, lambda e: e.tensor_scalar(out=out, in0=in0, scalar1=s1, scalar2=None, op0=op0), reads=reads, writes=writes)
            else:
                k.op(eng, lambda e: e.tensor_scalar(out=out, in0=in0, scalar1=s1, scalar2=s2, op0=op0, op1=op1), reads=reads, writes=writes)

        def stt(eng, out, in0, scalar, in1, op0, op1, reads, writes, accum_out=None):
            kw = {} if accum_out is None else {"accum_out": accum_out}
            k.op(eng, lambda e: e.scalar_tensor_tensor(out=out, in0=in0, scalar=scalar, in1=in1, op0=op0, op1=op1, **kw),
                 reads=reads, writes=writes)

        def copy(eng, out, in_, reads, writes):
            if eng == "act":
                act(out, in_, AF.Identity, reads, writes)
            else:
                k.op(eng, lambda e: e.tensor_copy(out=out, in_=in_), reads=reads, writes=writes)

        def recip(out, in_, reads, writes):
            k.op("dve", lambda e: e.reciprocal(out=out, in_=in_), reads=reads, writes=writes)

        uid = [0]

        def U(prefix):
            uid[0] += 1
            return "%s_%d" % (prefix, uid[0])

        def _layers():
            for l in range(DEPTH):
                k.barrier()
                lam_init = 0.8 - 0.6 * math.exp(-0.3 * l)
                with contextlib.ExitStack() as ls:
                    T1 = sbt(ls, U("T1"), [128, 128]); T2 = sbt(ls, U("T2"), [128, 128]); T3 = sbt(ls, U("T3"), [128, 64])
                    modT = sbt(ls, U("modT"), [128, 96, 2])
                    A1 = sbt(ls, U("A1"), [128, KC, 2]); A2 = sbt(ls, U("A2"), [128, KC, 2])
                    bdw = sbt(ls, U("bdw"), [128, 16, 128], BF16)
                    clam = sbt(ls, U("clam"), [128, 8])
                    lamt = sbt(ls, U("lamt"), [128, 4])
                    sinkT = sbt(ls, U("sinkT"), [65, 8, 128])
                    b2bc = sbt(ls, U("b2bc"), [128, D])
                    knbc = sbt(ls, U("knbc"), [128, 3, 64])
                    RL = R("layerparams", l)
                    with contextlib.ExitStack() as ps_:
                        stg = sbt(ps_, U("stg"), [128, 128]); stg2 = sbt(ps_, U("stg2"), [128, 128]); stg3 = sbt(ps_, U("stg3"), [64, 128])
                        k.dma("sp", stg[0:96, :], b_ada[l].rearrange("(r p) -> r p", p=128), writes=[R("stg")], sem="p_stg")
                        k.dma("sp", stg[96:112, :], norm1_g[l].rearrange("(r p) -> r p", p=128), writes=[R("stg")], sem="p_stg")
                        k.dma("sp", stg[112:128, :], norm2_g[l].rearrange("(r p) -> r p", p=128), writes=[R("stg")], sem="p_stg")
                        pbk, pr = pbank()
                        k.op("pe", lambda e: e.transpose(pbk[:, 0:128], stg[:], identf), reads=[R("stg"), RC], writes=[pr])
                        copy("dve", T1[:], pbk[:, 0:128], [pr], [RL])
                        k.dma("sp", stg2[0:64, :], b_gate[l].rearrange("(r p) -> r p", p=128), writes=[R("stg2")], sem="p_stg2")
                        k.dma("sp", stg2[64:128, :], b_ff1[l].rearrange("(r p) -> r p", p=128), writes=[R("stg2")], sem="p_stg2")
                        pbk2, pr2 = pbank()
                        k.op("pe", lambda e: e.transpose(pbk2[:, 0:128], stg2[:], identf), reads=[R("stg2"), RC], writes=[pr2])
                        copy("dve", T2[:], pbk2[:, 0:128], [pr2], [RL])
                        S3 = R("stg3")
                        k.op("dve", lambda e: e.memset(stg3[:], 0.0), writes=[S3])
                        k.dma("sp", stg3[0:16, :], conv_w[l].rearrange("j (fc p) -> (j fc) p", p=128), writes=[S3], sem="p_stg3")
                        k.dma("sp", stg3[16:20, :], conv_b[l].rearrange("(fc p) -> fc p", p=128), writes=[S3], sem="p_stg3")
                        k.dma("sp", stg3[20:28, :], lru_br[l].rearrange("k (fc p) -> (k fc) p", p=128), writes=[S3], sem="p_stg3")
                        k.dma("sp", stg3[28:36, :], lru_bi[l].rearrange("k (fc p) -> (k fc) p", p=128), writes=[S3], sem="p_stg3")
                        k.dma("sp", stg3[36:44, :], lru_lambda[l].rearrange("k (fc p) -> (k fc) p", p=128), writes=[S3], sem="p_stg3")
                        k.dma("sp", stg3[44:45, :], diff_out_g[l:l + 1, :], writes=[S3], sem="p_stg3")
                        for gi, gn in enumerate(("win_qn", "win_kn", "grid_qn", "grid_kn", "diff_qn", "diff_kn")):
                            for hh in range(2):
                                k.dma("sp", stg3[45 + gi:46 + gi, hh * 64:(hh + 1) * 64], gains[gn][l:l + 1, :], writes=[S3], sem="p_stg3")
                        pbk3, pr3 = pbank()
                        k.op("pe", lambda e: e.transpose(pbk3[:, 0:64], stg3[:], identf[0:64, 0:64]), reads=[S3, RC], writes=[pr3])
                        copy("dve", T3[:], pbk3[:, 0:64], [pr3], [RL])
                        for gi in (45, 47, 49):
                            ts("dve", T3[:, gi:gi + 1], T3[:, gi:gi + 1], 0.125, None, ALU.mult, None, [RL], [RL])
                        act(clam[:], T3[:, 36:44], AF.Exp, [RL], [RL], scale=-1.0)
                        act(clam[:], clam[:], AF.Ln, [RL], [RL], bias=onec[:])
                        ts("dve", clam[:], clam[:], -8.0, None, ALU.mult, None, [RL], [RL])
                        stg4 = sbt(ps_, U("stg4"), [32, 128])
                        k.dma("sp", stg4[:], cond.rearrange("j (kc p) -> (j kc) p", p=128), writes=[R("stg4")], sem="p_stg4")
                        pbk4, pr4 = pbank()
                        k.op("pe", lambda e: e.transpose(pbk4[:, 0:32], stg4[:], identf[0:32, 0:32]), reads=[R("stg4"), RC], writes=[pr4])
                        scT = sbt(ps_, U("scT"), [128, KC, 2], BF16)
                        act(scT[:].rearrange("p kc j -> p j kc"), pbk4[:, 0:32].rearrange("p (j kc) -> p j kc", j=2), AF.Silu, [pr4], [R("scT")])
                        pm, pmr = pbank()
                        ws = WStream([("ada", l, 0, t * 512) for t in range(24)])
                        for t in range(24):
                            wt, wr_ = ws.get()
                            for mb in range(4):
                                col = (t * 4 + mb) * 2
                                mm(pm[:, col:col + 2], [(wt[:, kc, mb * 128:(mb + 1) * 128], scT[:, kc, :]) for kc in range(KC)],
                                   [wr_, R("scT")], pmr)
                        for j in range(2):
                            tt("dve", modT[:, :, j], pm[:, 0:192].rearrange("p (c j) -> p c j", j=2)[:, :, j], T1[:, 0:96], ALU.add, [pmr, RL], [RL])
                        for j in range(2):
                            stt("dve", A1[:, :, j], modT[:, 16:32, j], 1.0, T1[:, 96:112], ALU.add, ALU.mult, [RL], [RL])
                            stt("dve", A2[:, :, j], modT[:, 64:80, j], 1.0, T1[:, 112:128], ALU.add, ALU.mult, [RL], [RL])
                        gst = sbt(ps_, U("gst"), [128, D]); gm = sbt(ps_, U("gm"), [128, 128])
                        for i, base in enumerate((32, 80)):
                            for j in range(2):
                                for q4 in range(4):
                                    pg, pgr = pbank()
                                    for kk4 in range(4):
                                        kc = q4 * 4 + kk4
                                        ts("dve", gm[:], onesf, modT[:, base + kc, j:j + 1], None, ALU.mult, None, [RL, RC], [R("gm")])
                                        mm(pg[:, kk4 * 128:(kk4 + 1) * 128], [(gm[:], identf)], [R("gm"), RC], pgr)
                                    copy("act", gst[:, q4 * 512:(q4 + 1) * 512], pg[:], [pgr], [R("gst")])
                                k.dma("sp", gbc_d[i, j], gst[:], reads=[R("gst")], writes=[R("gbc", i, j)], sem="gst")
                        bst = sbt(ps_, U("bst"), [128, 16, 128])
                        k.op("dve", lambda e: e.memset(bst[:], 0.0), writes=[R("bst")])
                        for ri, wsrc in enumerate((lru_wr, lru_wi)):
                            for kd in range(2):
                                for par in range(2):
                                    for fc in range(4):
                                        idx = (fc * 2 + kd) * 2 + ri
                                        k.dma("sp", bst[par * 64:(par + 1) * 64, idx, par * 64:(par + 1) * 64], wsrc[l, kd, 2 * fc + par],
                                              writes=[R("bst")], sem="p_bst")
                        copy("dve", bdw[:], bst[:], [R("bst")], [RL])
                        lq = sbt(ps_, U("lq"), [128, 4, 64])
                        for i, gn in enumerate(("diff_lq1", "diff_lk1", "diff_lq2", "diff_lk2")):
                            k.dma("sp", lq[:, i, :], gains[gn][l:l + 1, :].partition_broadcast(128), writes=[R("lq")], sem="p_lq")
                        lj = sbt(ps_, U("lj"), [128, 64]); l2 = sbt(ps_, U("l2"), [128, 2])
                        for i_ in range(2):
                            tt("dve", lj[:], lq[:, 2 * i_, :], lq[:, 2 * i_ + 1, :], ALU.mult, [R("lq")], [R("lj")])
                            k.op("dve", lambda e, i_=i_: e.tensor_reduce(out=l2[:, i_:i_ + 1], in_=lj[:], axis=AX.X, op=ALU.add), reads=[R("lj")], writes=[R("l2")])
                        act(l2[:], l2[:], AF.Exp, [R("l2")], [R("l2")])
                        tt("dve", lamt[:, 0:1], l2[:, 0:1], l2[:, 1:2], ALU.subtract, [R("l2")], [RL])
                        ts("dve", lamt[:, 0:1], lamt[:, 0:1], lam_init, None, ALU.add, None, [RL], [RL])
                        ts("dve", lamt[:, 1:2], lamt[:, 0:1], -1.0, None, ALU.mult, None, [RL], [RL])
                        ts("dve", lamt[:, 2:3], T3[:, 44:45], 1.0 - lam_init, None, ALU.mult, None, [RL], [RL])
                        sraw = sbt(ps_, U("sraw"), [65, 8])
                        k.dma("sp", sraw[64:65, :], win_sink[l:l + 1, :], writes=[R("sraw")], sem="p_sraw")
                        act(sraw[64:65, :], sraw[64:65, :], AF.Exp, [R("sraw")], [R("sraw")])
                        for h in range(8):
                            ts("dve", sinkT[64:65, h, :], onesf[64:65, :], sraw[64:65, h:h + 1], None, ALU.mult, None, [R("sraw"), RC], [RL])
                        k.dma("sp", b2bc[:], b_ff2[l:l + 1, :].partition_broadcast(128), writes=[RL], sem="p_b2bc")
                        for i, gn in enumerate(("win_kn", "grid_kn", "diff_kn")):
                            k.dma("sp", knbc[:, i, :], gains[gn][l:l + 1, :].partition_broadcast(128), writes=[RL], sem="p_knbc")
                        cst = sbt(ps_, U("cst"), [128, 4, 512]); cko = sbt(ps_, U("cko"), [128, 4, 512], BF16)
                        for b, (ck, cv, nf) in {1: (cwk, cwv, 128), 2: (cgk, cgv, 128), 3: (cdk, cdv, 512)}.items():
                            k.dma("pool", v_d[b][NS:NS + NCTX, :], cv[l], writes=[R("v_d", b, "ctx")], sem="cstv")
                            k.dma("sp", cst[:, :, 0:nf], ck[l].rearrange("(tt p) f -> p tt f", p=128), writes=[R("cst")], sem="cst")
                            for fc in range(nf // 128):
                                pk, pkr = pbank()
                                def fnT(e, pk=pk, fc=fc):
                                    ins = None
                                    for tt_ in range(4):
                                        ins = e.transpose(pk[:, tt_ * 128:(tt_ + 1) * 128], cst[:, tt_, fc * 128:(fc + 1) * 128], identf)
                                    return ins
                                k.op("pe", fnT, reads=[R("cst"), RC], writes=[pkr])
                                copy("act", cko[:, fc, :], pk[:], [pkr], [R("cko")])
                            k.dma("sp", kT_d[b][:, NS:NS + NCTX].rearrange("(fc p) t -> p fc t", p=128), cko[:, 0:nf // 128, :],
                                  reads=[R("cko")], writes=[R("kT_d", b, "ctx")], sem="cstk")

                    chk(1)
                    k.barrier()
                    if l + 1 < DEPTH:
                        cast_layer(l + 1)
                    with contextlib.ExitStack() as p1:
                        xt = sbt(p1, U("xt"), [128, 4, D]); xn = sbt(p1, U("xn"), [128, 4, D], BF16)
                        hT = sbt(p1, U("hT"), [128, KC, TC], BF16)
                        ssq = sbt(p1, U("ssq"), [128, 8])
                        rC = sbt(p1, U("rC"), [128, TC]); rS = sbt(p1, U("rS"), [128, TC])
                        zsA = [sbt(p1, U("zs"), [128, TC]) for _ in range(2)]; sqA = [sbt(p1, U("sq"), [128, TC]) for _ in range(2)]
                        sdA = [sbt(p1, U("sd"), [128, TC]) for _ in range(2)]
                        qnA = [sbt(p1, U("qn"), [128, TC]) for _ in range(2)]; t1A = [sbt(p1, U("t1"), [128, TC]) for _ in range(2)]
                        qk_i = [0]
                        qo = sbt(p1, U("qo"), [128, 4, TC], BF16)
                        xo = sbt(p1, U("xo"), [128, 4, TC])
                        vo = sbt(p1, U("vo"), [128, 4, 768], BF16)
                        xtf_ = xt[:].rearrange("p a b -> p (a b)")
                        vof = xtf_[:, 0:3072].rearrange("p (t f) -> p t f", f=768)
                        kof = xtf_[:, 3072:6144].rearrange("p (t f) -> p t f", f=768)
                        ksq = sbt(p1, U("ksq"), [128, 768]); kss = sbt(p1, U("kss"), [128, 12])
                        items = [("in", l, 0, t * 512) for c in range(NCH) for t in range(8)]
                        ws = WStream(items)
                        for c in range(NCH):
                            j = 0 if c < NSCH else 1
                            is_s = c < NSCH
                            t0 = c * TC
                            xsrc_, _ = xsrc(l, c)
                            Rx = R("xres", c)
                            k.dma("sp", xt[:], xsrc_, reads=[Rx], writes=[R("xt")], sem="xt")
                            if is_s:
                                k.dma("sp", rC[:], ropeC[:, t0:t0 + TC], writes=[R("rC")], sem="rope")
                                k.dma("sp", rS[:], ropeS[:, t0:t0 + TC], writes=[R("rS")], sem="rope")
                            for tt_ in range(4):
                                xof = xo[:].rearrange("p a b -> p (a b)")
                                tt("dve", xof, xt[:, tt_, :], xt[:, tt_, :], ALU.mult, [R("xt")], [R("xo")])
                                k.op("dve", lambda e, tt_=tt_, xof=xof: e.tensor_reduce(out=ssq[:, tt_:tt_ + 1], in_=xof, axis=AX.X, op=ALU.add),
                                     reads=[R("xo")], writes=[R("ssq")])
                            act(ssq[:, 4:8], ssq[:, 0:4], AF.Sqrt, [R("ssq")], [R("ssq")], scale=1.0 / D, bias=epsc[:])
                            recip(ssq[:, 4:8], ssq[:, 4:8], [R("ssq")], [R("ssq")])
                            for tt_ in range(4):
                                ts("dve", xn[:, tt_, :], xt[:, tt_, :], ssq[:, 4 + tt_:5 + tt_], None, ALU.mult, None, [R("xt"), R("ssq")], [R("xn")])
                            for kc in range(KC):
                                pk, pkr = pbank()
                                pkb = pk[:].bitcast(BF16)
                                def fnT(e, pkb=pkb, kc=kc):
                                    ins = None
                                    for tt_ in range(4):
                                        ins = e.transpose(pkb[:, tt_ * 128:(tt_ + 1) * 128], xn[:, tt_, kc * 128:(kc + 1) * 128], idb[:])
                                    return ins
                                k.op("pe", fnT, reads=[R("xn"), R("idb")], writes=[pkr])
                                act(hT[:, kc, :], pkb[:, 0:TC], AF.Identity, [pkr, RL], [R("hT")], scale=A1[:, kc, j:j + 1], bias=modT[:, kc, j:j + 1])
                            k.dma("sp", hT_d[:, t0:t0 + TC].rearrange("(kc p) t -> p kc t", p=128), hT[:], reads=[R("hT")], writes=[R("hT_d", c)], sem="hTst")

                            chk(11)

                            def fm_block(wt, wr_, mb):
                                pz, pzr = pbank()
                                mm(pz[:], [(wt[:, kc, mb * 128:(mb + 1) * 128], hT[:, kc, :]) for kc in range(KC)], [wr_, R("hT")], pzr)
                                return pz, pzr

                            def qk_block(pz, pzr, gcol, dst, slot, rope):
                                qi = qk_i[0] % 2
                                qk_i[0] += 1
                                zs, sq, sd, qn, t1 = zsA[qi], sqA[qi], sdA[qi], qnA[qi], t1A[qi]
                                Rz, Rsq, Rsd, Rqn, Rt1 = R("zs", qi), R("sq", qi), R("sd", qi), R("qn", qi), R("t1", qi)
                                copy("act", zs[:], pz[:], [pzr], [Rz])
                                tt("pool", sq[:], zs[:], zs[:], ALU.mult, [Rz], [Rsq])
                                pm_, pmr_ = pbank()
                                mm(pm_[:], [(bd64, sq[:])], [RC, Rsq], pmr_)
                                act(sd[:], pm_[:], AF.Sqrt, [pmr_], [Rsd], bias=epsc[:])
                                recip(sd[:], sd[:], [Rsd], [Rsd])
                                if not rope:
                                    stt("dve", dst[:, slot, :], zs[:], T3[:, gcol:gcol + 1], sd[:], ALU.mult, ALU.mult, [Rz, Rsd, RL], [R("qo")])
                                    return
                                stt("dve", qn[:], zs[:], T3[:, gcol:gcol + 1], sd[:], ALU.mult, ALU.mult, [Rz, Rsd, RL], [Rqn])
                                ps_w, psr = pbank()
                                mm(ps_w[:], [(permf, qn[:])], [RC, Rqn], psr)
                                tt("pool", t1[:], qn[:], rC[:], ALU.mult, [Rqn, R("rC")], [Rt1])
                                tt("dve", qn[:], ps_w[:], rS[:], ALU.mult, [psr, R("rS")], [Rqn])
                                tt("dve", dst[:, slot, :], t1[:], qn[:], ALU.add, [Rt1, Rqn], [R("qo")])

                            def tm_block(wt, wr_, tt_, c0, c1, dst_off):
                                pz, pzr = pbank()
                                mm(pz[:, 0:c1 - c0], [(hT[:, kc, tt_ * 128:(tt_ + 1) * 128], wt[:, kc, c0:c1]) for kc in range(KC)], [wr_, R("hT")], pzr)
                                return pz, pzr

                            kcol = t0 if is_s else KOFFP + (t0 - NS)
                            for t in range(8):
                                chk(20 + t)
                                wt, wr_ = ws.get()
                                if t in (0, 1):
                                    for mb in range(4):
                                        pz, pzr = fm_block(wt, wr_, mb)
                                        copy("act", xo[:, mb, :], pz[:], [pzr], [R("xo")])
                                    dstd = xaT_d if t == 0 else yaT_d
                                    k.dma("sp", dstd[:, t0:t0 + TC].rearrange("(fc p) t -> p fc t", p=128), xo[:], reads=[R("xo")],
                                          writes=[R("xyT_d", t, c)], sem="xost")
                                elif t in (2, 5):
                                    b = 1 if t == 2 else 3
                                    for mb in range(4):
                                        pz, pzr = fm_block(wt, wr_, mb)
                                        qk_block(pz, pzr, 45 if b == 1 else 49, qo, mb, is_s)
                                    k.dma("sp", qT_d[b][:, t0:t0 + TC].rearrange("(fc p) t -> p fc t", p=128), qo[:], reads=[R("qo")],
                                          writes=[R("qT_d", b, c)], sem="qost")
                                elif t == 3:
                                    pz, pzr = fm_block(wt, wr_, 0)
                                    qk_block(pz, pzr, 46, qo, 0, is_s)
                                    k.dma("sp", kT_d[1][:, kcol:kcol + TC], qo[:, 0, :], reads=[R("qo")], writes=[R("kT_d", 1, c)], sem="qost")
                                    for mb in (2, 3):
                                        pz, pzr = fm_block(wt, wr_, mb)
                                        qk_block(pz, pzr, 47, qo, mb, is_s)
                                    for tt_ in range(4):
                                        pz, pzr = tm_block(wt, wr_, tt_, 128, 256, 0)
                                        if is_s:
                                            copy("act", vo[:, tt_, 0:128], pz[:, 0:128], [pzr], [R("vo")])
                                        else:
                                            copy("act", vof[:, tt_, 0:128], pz[:, 0:128], [pzr], [R("xt")])
                                            copy("pool", vo[:, tt_, 0:128], vof[:, tt_, 0:128], [R("xt")], [R("vo")])
                                            pz2, pzr2 = tm_block(wt, wr_, tt_, 0, 128, 0)
                                            copy("act", kof[:, tt_, 0:128], pz2[:, 0:128], [pzr2], [R("xt")])
                                elif t == 4:
                                    for mb in (0, 1):
                                        pz, pzr = fm_block(wt, wr_, mb)
                                        qk_block(pz, pzr, 47, qo, mb, is_s)
                                    k.dma("sp", qT_d[2][0:256, t0:t0 + TC].rearrange("(fc p) t -> p fc t", p=128), qo[:, 2:4, :], reads=[R("qo")],
                                          writes=[R("qT_d", 2, c, 0)], sem="qost")
                                    k.dma("sp", qT_d[2][256:512, t0:t0 + TC].rearrange("(fc p) t -> p fc t", p=128), qo[:, 0:2, :], reads=[R("qo")],
                                          writes=[R("qT_d", 2, c, 1)], sem="qost")
                                    pz, pzr = fm_block(wt, wr_, 2)
                                    qk_block(pz, pzr, 48, qo, 2, is_s)
                                    k.dma("sp", kT_d[2][:, kcol:kcol + TC], qo[:, 2, :], reads=[R("qo")], writes=[R("kT_d", 2, c)], sem="qost")
                                    for tt_ in range(4):
                                        pz, pzr = tm_block(wt, wr_, tt_, 384, 512, 0)
                                        if is_s:
                                            copy("act", vo[:, tt_, 128:256], pz[:, 0:128], [pzr], [R("vo")])
                                        else:
                                            copy("act", vof[:, tt_, 128:256], pz[:, 0:128], [pzr], [R("xt")])
                                            copy("pool", vo[:, tt_, 128:256], vof[:, tt_, 128:256], [R("xt")], [R("vo")])
                                            pz2, pzr2 = tm_block(wt, wr_, tt_, 256, 384, 0)
                                            copy("act", kof[:, tt_, 128:256], pz2[:, 0:128], [pzr2], [R("xt")])
                                elif t == 6:
                                    for mb in range(4):
                                        pz, pzr = fm_block(wt, wr_, mb)
                                        qk_block(pz, pzr, 50, qo, mb, is_s)
                                    k.dma("sp", kT_d[3][:, kcol:kcol + TC].rearrange("(fc p) t -> p fc t", p=128), qo[:], reads=[R("qo")],
                                          writes=[R("kT_d", 3, c)], sem="qost")
                                    if not is_s:
                                        for tt_ in range(4):
                                            pz2, pzr2 = tm_block(wt, wr_, tt_, 0, 512, 0)
                                            copy("act", kof[:, tt_, 256:768], pz2[:, 0:512], [pzr2], [R("xt")])
                                else:
                                    for tt_ in range(4):
                                        pz, pzr = tm_block(wt, wr_, tt_, 0, 512, 0)
                                        if is_s:
                                            copy("act", vo[:, tt_, 256:768], pz[:, 0:512], [pzr], [R("vo")])
                                        else:
                                            copy("act", vof[:, tt_, 256:768], pz[:, 0:512], [pzr], [R("xt")])
                                            copy("pool", vo[:, tt_, 256:768], vof[:, tt_, 256:768], [R("xt")], [R("vo")])
                            chk(12)
                            for b, (c0, c1) in {1: (0, 128), 2: (128, 256), 3: (256, 768)}.items():
                                k.dma("sp", v_d[b][kcol:kcol + TC, :].rearrange("(tt p) f -> p tt f", p=128), vo[:, :, c0:c1], reads=[R("vo")],
                                      writes=[R("v_d", b, c)], sem="vost")
                            chk(13)
                            if not is_s:
                                s0 = (t0 - NS) // 256
                                for (b, odst, c0, c1) in ((1, o_wv, 0, 128), (2, o_gv, 128, 256), (3, o_dv, 256, 768)):
                                    for sq_ in range(2):
                                        k.dma("sp", odst[s0 + sq_, l].rearrange("(tt p) f -> p tt f", p=128), vof[:, 2 * sq_:2 * sq_ + 2, c0:c1],
                                              reads=[R("xt")], sem="ovst")
                                for tt_ in range(4):
                                    tt("pool", ksq[:], kof[:, tt_, :], kof[:, tt_, :], ALU.mult, [R("xt")], [R("ksq")])
                                    k.op("dve", lambda e: e.tensor_reduce(out=kss[:], in_=ksq[:].rearrange("p (h d) -> p h d", d=64), axis=AX.X, op=ALU.add),
                                         reads=[R("ksq")], writes=[R("kss")])
                                    act(kss[:], kss[:], AF.Sqrt, [R("kss")], [R("kss")], scale=1.0 / 64, bias=epsc[:])
                                    recip(kss[:], kss[:], [R("kss")], [R("kss")])
                                    for h in range(12):
                                        gi = 0 if h < 2 else (1 if h < 4 else 2)
                                        stt("dve", kof[:, tt_, h * 64:(h + 1) * 64], kof[:, tt_, h * 64:(h + 1) * 64], kss[:, h:h + 1], knbc[:, gi, :],
                                            ALU.mult, ALU.mult, [R("xt"), R("kss"), RL], [R("xt")])
                                for (b, odst, c0, c1) in ((1, o_wk, 0, 128), (2, o_gk, 128, 256), (3, o_dk, 256, 768)):
                                    for sq_ in range(2):
                                        k.dma("sp", odst[s0 + sq_, l].rearrange("(tt p) f -> p tt f", p=128), kof[:, 2 * sq_:2 * sq_ + 2, c0:c1],
                                              reads=[R("xt")], sem="ovst")

                    chk(2)
                    k.barrier()
                    seqs = [(0, NS, True, None)] + [(NS + s * 256, 256, False, s) for s in range(NPS)]
                    with contextlib.ExitStack() as p2:
                        LM = NS
                        Tm = [sbt(p2, U("lt%d" % i), [128, LM]) for i in range(6)]
                        ub = sbt(p2, U("ub"), [128, LM], BF16); ao = sbt(p2, U("ao"), [128, LM], BF16)
                        h0t = sbt(p2, U("h0t"), [128, 8])
                        h0s = sbt(p2, U("h0s"), [8, 128])
                        k.dma("sp", h0s[:], slru[l].rearrange("k (fc p) -> (k fc) p", p=128), writes=[R("h0s")], sem="p_h0s")
                        pbk, pr = pbank()
                        k.op("pe", lambda e: e.transpose(pbk[:, 0:8], h0s[:], identf[0:8, 0:8]), reads=[R("h0s"), RC], writes=[pr])
                        copy("dve", h0t[:], pbk[:, 0:8], [pr], [R("h0t")])
                        for (t0, L, is_s, sidx) in seqs:
                            all_c = range(t0 // TC, (t0 + L + TC - 1) // TC)
                            for fc in range(4):
                                xa, u, ra, ig, m4, ya = [t[:, 0:L] for t in Tm]
                                rd = [R("xyT_d", 0, c) for c in all_c]
                                k.dma("sp", xa, xaT_d[fc * 128:(fc + 1) * 128, t0:t0 + L], reads=rd, writes=[R("lt", 0)], sem="lru_in")
                                k.dma("sp", ya, yaT_d[fc * 128:(fc + 1) * 128, t0:t0 + L], reads=[R("xyT_d", 1, c) for c in all_c], writes=[R("lt", 5)], sem="lru_in2")
                                act(u, xa, AF.Identity, [R("lt", 0), RL], [R("lt", 1)], scale=T3[:, 4 + fc:5 + fc], bias=T3[:, 16 + fc:17 + fc])
                                stt("dve", u[:, 1:L], xa[:, 0:L - 1], T3[:, 0 + fc:1 + fc], u[:, 1:L], ALU.mult, ALU.add, [R("lt", 0), R("lt", 1), RL], [R("lt", 1)])
                                stt("dve", u[:, 0:L - 1], xa[:, 1:L], T3[:, 8 + fc:9 + fc], u[:, 0:L - 1], ALU.mult, ALU.add, [R("lt", 0), R("lt", 1), RL], [R("lt", 1)])
                                stt("dve", u[:, 0:L - 2], xa[:, 2:L], T3[:, 12 + fc:13 + fc], u[:, 0:L - 2], ALU.mult, ALU.add, [R("lt", 0), R("lt", 1), RL], [R("lt", 1)])
                                copy("pool", ub[:, 0:L], u, [R("lt", 1)], [R("ub")])
                                for kd in range(2):
                                    for n0 in range(0, L, TC):
                                        n1 = min(L, n0 + TC)
                                        for ri, dstt in ((0, ra), (1, ig)):
                                            idx = (fc * 2 + kd) * 2 + ri
                                            pz, pzr = pbank()
                                            mm(pz[:, 0:n1 - n0], [(bdw[:, idx, :], ub[:, n0:n1])], [RL, R("ub")], pzr)
                                            bcol = (20 if ri == 0 else 28) + kd * 4 + fc
                                            act(dstt[:, n0:n1], pz[:, 0:n1 - n0], AF.Sigmoid, [pzr, RL], [R("lt", 2 + ri)], bias=T3[:, bcol:bcol + 1])
                                    act(ra, ra, AF.Exp, [R("lt", 2), RL], [R("lt", 2)], scale=clam[:, kd * 4 + fc:kd * 4 + fc + 1])
                                    tt("pool", m4, ra, ra, ALU.mult, [R("lt", 2)], [R("lt", 4)])
                                    act(m4, m4, AF.Sqrt, [R("lt", 4)], [R("lt", 4)], scale=-1.0, bias=onec[:])
                                    tt("dve", ig, ig, m4, ALU.mult, [R("lt", 3), R("lt", 4)], [R("lt", 3)])
                                    tt("dve", ig, ig, u, ALU.mult, [R("lt", 3), R("lt", 1)], [R("lt", 3)])
                                    if kd == 0:
                                        init = h0t[:, fc:fc + 1] if is_s else 0.0
                                        k.op("dve", lambda e, init=init: e.tensor_tensor_scan(out=xa, data0=ra, data1=ig, initial=init, op0=ALU.mult, op1=ALU.add),
                                             reads=[R("lt", 2), R("lt", 3), R("h0t")], writes=[R("lt", 0)])
                                    else:
                                        init = h0t[:, 4 + fc:5 + fc] if is_s else 0.0
                                        k.op("dve", lambda e, init=init: e.tensor_tensor_scan(out=m4[:, ::-1], data0=ra[:, ::-1], data1=ig[:, ::-1], initial=init,
                                                                                              op0=ALU.mult, op1=ALU.add),
                                             reads=[R("lt", 2), R("lt", 3), R("h0t")], writes=[R("lt", 4)])
                                if not is_s:
                                    k.dma("sp", o_lru[sidx, l, 0, fc * 128:(fc + 1) * 128].rearrange("(p o) -> p o", o=1), xa[:, L - 1:L], reads=[R("lt", 0)], sem="olru")
                                    k.dma("sp", o_lru[sidx, l, 1, fc * 128:(fc + 1) * 128].rearrange("(p o) -> p o", o=1), m4[:, 0:1], reads=[R("lt", 4)], sem="olru")
                                tt("dve", xa, xa, m4, ALU.add, [R("lt", 0), R("lt", 4)], [R("lt", 0)])
                                tt("pool", ra, ya, ya, ALU.mult, [R("lt", 5)], [R("lt", 2)])
                                ts("dve", ra, ra, 0.044715, 1.0, ALU.mult, ALU.add, [R("lt", 2)], [R("lt", 2)])
                                tt("dve", ra, ra, ya, ALU.mult, [R("lt", 2), R("lt", 5)], [R("lt", 2)])
                                act(ra, ra, AF.Sigmoid, [R("lt", 2)], [R("lt", 2)], scale=2.0 * math.sqrt(2.0 / math.pi))
                                tt("pool", ra, ra, ya, ALU.mult, [R("lt", 2), R("lt", 5)], [R("lt", 2)])
                                tt("dve", ao[:, 0:L], xa, ra, ALU.mult, [R("lt", 0), R("lt", 2)], [R("ao")])
                                k.dma("sp", brT_d[fc * 128:(fc + 1) * 128, t0:t0 + L], ao[:, 0:L], reads=[R("ao")], writes=[R("brT_d", 0, fc, t0)], sem="aost")

                    chk(3)
                    k.barrier()
                    def key_reads(b, k0, n):
                        out_k, out_v = [], []
                        if k0 < NS:
                            for c in range(k0 // TC, (k0 + n + TC - 1) // TC):
                                out_k.append(R("kT_d", b, c)); out_v.append(R("v_d", b, c))
                            if k0 + n > NS:
                                out_k.append(R("kT_d", b, "ctx")); out_v.append(R("v_d", b, "ctx"))
                        else:
                            tok0 = NS + (k0 - KOFFP)
                            for c in range(tok0 // TC, (tok0 + n + TC - 1) // TC):
                                out_k.append(R("kT_d", b, c)); out_v.append(R("v_d", b, c))
                        return out_k, out_v

                    with contextlib.ExitStack() as p3:
                        PTn = 8
                        PT = [sbt(p3, U("pt%d" % i), [128, 512], BF16) for i in range(PTn)]
                        pt_i = [0]
                        osb = sbt(p3, U("osb"), [128, 512]); osb2 = sbt(p3, U("osb2"), [128, 512]); rz = sbt(p3, U("rz"), [128, 512])
                        ostt = [sbt(p3, U("ost%d" % i), [128, 2, 512], BF16) for i in range(2)]
                        LOOK = [2]
                        SR = [3]
                        pend = []

                        def _pop():
                            pv_, post_ = pend.pop(0)
                            pv_()
                            if post_ is not None:
                                post_()

                        def push(pv_, post_=None):
                            pend.append((pv_, post_))
                            while len(pend) > LOOK[0]:
                                _pop()

                        def flush():
                            while pend:
                                _pop()

                        def score_exp(lhsT_k, rhs_q, n, kread, qread, mask=None):
                            pS, pSr = pbank(0, SR[0])
                            mm(pS[:, 0:n], [(lhsT_k, rhs_q)], [kread, qread], pSr)
                            i = pt_i[0] % PTn
                            pt_i[0] += 1
                            act(PT[i][:, 0:n], pS[:, 0:n], AF.Exp, [pSr], [R("pt", i)])
                            if mask is not None:
                                tt("pool", PT[i][:, 0:n], PT[i][:, 0:n], mask, ALU.mult, [R("pt", i), R("masks")], [R("pt", i)])
                            return PT[i], R("pt", i)

                        for (t0, L, is_s, sidx) in seqs:
                            k0 = 0 if is_s else KOFFP + (t0 - NS)
                            nk = (L + NCTX) if is_s else L
                            nkb = nk // 128
                            for b in (1, 2):
                                LOOK[0] = 3
                                SR[0] = 4
                                with contextlib.ExitStack() as pa:
                                    Kt = sbt(pa, U("Kt"), [128, nk], BF16)
                                    Va = sbt(pa, U("Va"), [128, nkb, 2, 65], BF16)
                                    Qts = [sbt(pa, U("Qt%d" % i), [128, 4, 128], BF16) for i in range(2)]
                                    kr, vr = key_reads(b, k0, nk)
                                    k.dma("sp", Kt[:], kT_d[b][:, k0:k0 + nk], reads=kr, writes=[R("Kt")], sem="kld")
                                    k.op("pool", lambda e: e.memset(Va[:, :, :, 64:65], 1.0), writes=[R("Va")])
                                    for hv in range(2):
                                        k.dma("sp", Va[:, :, hv, 0:64], v_d[b][k0:k0 + nk, hv * 64:(hv + 1) * 64].rearrange("(kb p) d -> p kb d", p=128), reads=vr, writes=[R("Va")], sem="vld")
                                    nqb = L // 128

                                    def load_q(qb):
                                        q0_ = t0 + qb * 128
                                        qc = q0_ // TC
                                        qrd = [R("qT_d", b, qc)] if b != 2 else [R("qT_d", 2, qc, 0), R("qT_d", 2, qc, 1)]
                                        for kvh_ in range(2):
                                            k.dma("sp", Qts[qb % 2][kvh_ * 64:(kvh_ + 1) * 64, :, :],
                                                  qT_d[b][kvh_ * 256:(kvh_ + 1) * 256, q0_:q0_ + 128].rearrange("(g d) q -> d g q", d=64),
                                                  reads=qrd, writes=[R("Qt", qb % 2)], sem="qld%d" % (qb % 2))
                                    load_q(0)
                                    for qb in range(nqb):
                                        if qb + 1 < nqb:
                                            load_q(qb + 1)
                                        q0 = t0 + qb * 128
                                        Qt = Qts[qb % 2]; RQ = R("Qt", qb % 2)
                                        ost = ostt[qb % 2]; Rost = R("ost", qb % 2)
                                        for kvh in range(2):
                                            if b == 1 and is_s:
                                                kbs = [(kb, (0 if kb == qb - 1 else (1 if kb == qb + 1 else None))) for kb in (qb - 1, qb, qb + 1) if 0 <= kb < L // 128]
                                                kbs += [(L // 128 + i, None) for i in range(NCTX // 128)]
                                            else:
                                                kbs = [(kb, None) for kb in range(nkb)]
                                            pO, pOr = PB[4 + kvh], R("pb", 4 + kvh)
                                            for i, (kb, mi) in enumerate(kbs):
                                                ptile, ptr = score_exp(Kt[kvh * 64:(kvh + 1) * 64, kb * 128:(kb + 1) * 128],
                                                                       Qt[kvh * 64:(kvh + 1) * 64, :, :].rearrange("p g q -> p (g q)"), 512, R("Kt"), RQ,
                                                                       mask=None if mi is None else masks[:, mi, :])
                                                last = (i == len(kbs) - 1)

                                                def pv_(pO=pO, pOr=pOr, kb=kb, kvh=kvh, ptile=ptile, ptr=ptr, i=i, last=last):
                                                    mm(pO[0:65, :], [(Va[:, kb, kvh, :], ptile[:])], [R("Va"), ptr], pOr, start=(i == 0), stop=last)
                                                post_ = None
                                                if last:
                                                    def post_(pO=pO, pOr=pOr, kvh=kvh, ost=ost, Rost=Rost, q0=q0, b=b, par=qb % 2):
                                                        copy("act", osb[0:65, :], pO[0:65, :], [pOr], [R("osb")])
                                                        if b == 1:
                                                            tt("dve", osb[64:65, :], osb[64:65, :], sinkT[64:65, kvh * 4:(kvh + 1) * 4, :].rearrange("p g q -> p (g q)"), ALU.add,
                                                               [R("osb"), RL], [R("osb")])
                                                        recip(rz[64:65, :], osb[64:65, :], [R("osb")], [R("rz")])
                                                        pZ, pZr = pbank(6, 8)
                                                        mm(pZ[0:64, :], [(onesf[64:65, 0:64], rz[64:65, :])], [RC, R("rz")], pZr)
                                                        tt("dve", ost[0:64, kvh, :], osb[0:64, :], pZ[0:64, :], ALU.mult, [R("osb"), pZr], [Rost])
                                                        if kvh == 1:
                                                            for kv2 in range(2):
                                                                k.dma("sp", brT_d[b * 512 + kv2 * 256:b * 512 + (kv2 + 1) * 256, q0:q0 + 128].rearrange("(g d) q -> d g q", d=64),
                                                                      ost[0:64, kv2, :].rearrange("p (g q) -> p g q", g=4), reads=[Rost], writes=[R("brT_d", b, q0, kv2)], sem="ost%d" % par)
                                                push(pv_, post_)
                                    flush()
                                k.barrier()
                            LOOK[0] = 2
                            SR[0] = 3
                            with contextlib.ExitStack() as pa:
                                Kt = sbt(pa, U("Ktd"), [128, 4, nk], BF16)
                                Vd = sbt(pa, U("Vd"), [128, nkb, 512], BF16)
                                Qds = [sbt(pa, U("Qd%d" % i), [128, 4, 512], BF16) for i in range(2)]
                                o32 = sbt(pa, U("o32"), [128, 512])
                                dsts = [sbt(pa, U("dst%d" % i), [128, 512], BF16) for i in range(2)]
                                kr, vr = key_reads(3, k0, nk)
                                k.dma("sp", Kt[:], kT_d[3][:, k0:k0 + nk].rearrange("(h p) t -> p h t", p=128), reads=kr, writes=[R("Ktd")], sem="kld")
                                k.dma("sp", Vd[:], v_d[3][k0:k0 + nk, :].rearrange("(kb p) f -> p kb f", p=128), reads=vr, writes=[R("Vd")], sem="vld")
                                QN = min(512, L)
                                nqq = L // QN

                                def load_qd(qq):
                                    q0_ = t0 + qq * QN
                                    k.dma("sp", Qds[qq % 2][:, :, 0:QN], qT_d[3][:, q0_:q0_ + QN].rearrange("(h p) t -> p h t", p=128), reads=[R("qT_d", 3, q0_ // TC)],
                                          writes=[R("Qd", qq % 2)], sem="qdl%d" % (qq % 2))
                                load_qd(0)
                                hcnt = [0]
                                for qq in range(nqq):
                                    if qq + 1 < nqq:
                                        load_qd(qq + 1)
                                    q0 = t0 + qq * QN
                                    Qd = Qds[qq % 2]; RQd = R("Qd", qq % 2)
                                    for h in range(4):
                                        accs = [(PB[3], R("pb", 3), PB[4], R("pb", 4)), (PB[5], R("pb", 5), PB[6], R("pb", 6))]
                                        for kb in range(nkb):
                                            for jm in range(2):
                                                pO, pOr, pZ, pZr = accs[jm]
                                                ptile, ptr = score_exp(Kt[jm * 64:(jm + 1) * 64, h, kb * 128:(kb + 1) * 128], Qd[jm * 64:(jm + 1) * 64, h, 0:QN], QN,
                                                                       R("Ktd"), RQd)

                                                def pv_(pO=pO, pOr=pOr, pZ=pZ, pZr=pZr, kb=kb, h=h, ptile=ptile, ptr=ptr):
                                                    mm(pO[:, 0:QN], [(Vd[:, kb, h * 128:(h + 1) * 128], ptile[:, 0:QN])], [R("Vd"), ptr], pOr, start=(kb == 0), stop=(kb == nkb - 1))
                                                    mm(pZ[:, 0:QN], [(onesb[:], ptile[:, 0:QN])], [R("onesb"), ptr], pZr, start=(kb == 0), stop=(kb == nkb - 1))
                                                post_ = None
                                                if kb == nkb - 1 and jm == 1:
                                                    def post_(accs=accs, h=h, q0=q0):
                                                        di = hcnt[0] % 2
                                                        hcnt[0] += 1
                                                        dst_ = dsts[di]
                                                        recip(rz[:, 0:QN], accs[0][2][:, 0:QN], [accs[0][3]], [R("rz")])
                                                        tt("dve", osb[:, 0:QN], accs[0][0][:, 0:QN], rz[:, 0:QN], ALU.mult, [accs[0][1], R("rz")], [R("osb")])
                                                        recip(rz[:, 0:QN], accs[1][2][:, 0:QN], [accs[1][3]], [R("rz")])
                                                        tt("dve", osb2[:, 0:QN], accs[1][0][:, 0:QN], rz[:, 0:QN], ALU.mult, [accs[1][1], R("rz")], [R("osb2")])
                                                        stt("dve", o32[:, 0:QN], osb2[:, 0:QN], lamt[:, 1:2], osb[:, 0:QN], ALU.mult, ALU.add, [R("osb"), R("osb2"), RL], [R("o32")])
                                                        tt("pool", osb[:, 0:QN], o32[:, 0:QN], o32[:, 0:QN], ALU.mult, [R("o32")], [R("osb")])
                                                        pM, pMr = PB[7], R("pb", 7)
                                                        mm(pM[:, 0:QN], [(avg128, osb[:, 0:QN])], [RC, R("osb")], pMr)
                                                        act(rz[:, 0:QN], pM[:, 0:QN], AF.Ln, [pMr], [R("rz")], bias=epsc[:])
                                                        act(rz[:, 0:QN], rz[:, 0:QN], AF.Exp, [R("rz")], [R("rz")], scale=-0.5)
                                                        stt("dve", dst_[:, 0:QN], o32[:, 0:QN], lamt[:, 2:3], rz[:, 0:QN], ALU.mult, ALU.mult, [R("o32"), R("rz"), RL], [R("dst", di)])
                                                        k.dma("sp", brT_d[3 * 512 + h * 128:3 * 512 + (h + 1) * 128, q0:q0 + QN], dst_[:, 0:QN], reads=[R("dst", di)],
                                                              writes=[R("brT_d", 3, q0, h)], sem="dst%d" % di)
                                                push(pv_, post_)
                                flush()
                            k.barrier()

                    chk(4)
                    k.barrier()
                    def br_reads(c):
                        t0 = c * TC
                        rr = []
                        if c < NSCH:
                            rr += [R("brT_d", 0, fc, 0) for fc in range(4)]
                        else:
                            for s in range(NPS):
                                if NS + s * 256 >= t0 and NS + s * 256 < t0 + TC:
                                    rr += [R("brT_d", 0, fc, NS + s * 256) for fc in range(4)]
                        for q0 in range(t0, t0 + TC, 128):
                            rr += [R("brT_d", b, q0, kvh) for b in (1, 2) for kvh in range(2)]
                        for h in range(4):
                            if c < NSCH:
                                rr.append(R("brT_d", 3, t0, h))
                            else:
                                rr += [R("brT_d", 3, t0, h), R("brT_d", 3, t0 + 256, h)]
                        return rr

                    with contextlib.ExitStack() as p4:
                        hT = sbt(p4, U("hT3"), [128, KC, TC], BF16); bT = sbt(p4, U("bT"), [128, KC, TC], BF16)
                        mT = sbt(p4, U("mT"), [128, KC, TC], BF16)
                        xt = sbt(p4, U("xt3"), [128, 4, D]); xn = bT[:].rearrange("p (a e) b -> p a (e b)", a=4)
                        gbc = sbt(p4, U("gbc"), [128, D])
                        acc = sbt(p4, U("acc"), [128, 4, TC]); sg = sbt(p4, U("sg"), [128, TC]); tmp = sbt(p4, U("tmp"), [128, TC])
                        ssq = sbt(p4, U("ssq3"), [128, 8])
                        wbrr = [sbt(p4, U("wbrr%d" % i_), [128, 4, 512], BF16) for i_ in range(2)]
                        wbr_i = [0]
                        items = []
                        for c in range(NCH):
                            for mg in range(4):
                                for b in range(4):
                                    items.append(("gate", l, 0, (b * 4 + mg) * 512))
                            for ct in range(4):
                                items.append(("o", l, 0, ct * 512))
                        ws = WStream(items)
                        cur_j = -1
                        for c in range(NCH):
                            j = 0 if c < NSCH else 1
                            t0 = c * TC
                            if j != cur_j:
                                k.dma("sp", gbc[:], gbc_d[0, j], reads=[R("gbc", 0, j)], writes=[R("gbcs")], sem="gbl")
                                cur_j = j
                            k.dma("sp", hT[:], hT_d[:, t0:t0 + TC].rearrange("(kc p) t -> p kc t", p=128), reads=[R("hT_d", c)], writes=[R("hT3")], sem="hTld")
                            k.dma("sp", bT[:], brT_d[:, t0:t0 + TC].rearrange("(kc p) t -> p kc t", p=128), reads=br_reads(c), writes=[R("bT")], sem="bTld")
                            xsrc_, xdst_ = xsrc(l, c)
                            Rx = R("xres", c)
                            k.dma("sp", xt[:], xsrc_, reads=[Rx], writes=[R("xt3")], sem="xt3")
                            for mg in range(4):
                                for b in range(4):
                                    wg_, wgr_ = ws.get()
                                    si_ = wbr_i[0] % 2
                                    wbr_i[0] += 1
                                    wb_, wbr_ = wbrr[si_], R("wbrr", si_)
                                    k.dma("sp", wb_[:], wbf["br"][l, b * 512:(b + 1) * 512, mg * 512:(mg + 1) * 512].rearrange("(kc p) n -> p kc n", p=128),
                                          reads=wres("br", l), writes=[wbr_], sem="wbr%d" % si_)
                                    for mm_ in range(4):
                                        m = mg * 4 + mm_
                                        pg, pgr = pbank()
                                        mm(pg[:], [(wg_[:, kc, mm_ * 128:(mm_ + 1) * 128], hT[:, kc, :]) for kc in range(KC)], [wgr_, R("hT3")], pgr)
                                        pp, ppr = pbank()
                                        mm(pp[:], [(wb_[:, k4, mm_ * 128:(mm_ + 1) * 128], bT[:, b * 4 + k4, :]) for k4 in range(4)], [wbr_, R("bT")], ppr)
                                        act(sg[:], pg[:], AF.Sigmoid, [pgr, RL], [R("sg")], bias=T2[:, b * 16 + m:b * 16 + m + 1])
                                        if b == 0:
                                            tt("dve", acc[:, mm_, :], sg[:], pp[:], ALU.mult, [R("sg"), ppr], [R("acc", mm_)])
                                        else:
                                            tt("dve", tmp[:], sg[:], pp[:], ALU.mult, [R("sg"), ppr], [R("tmp")])
                                            if b < 3:
                                                tt("pool", acc[:, mm_, :], acc[:, mm_, :], tmp[:], ALU.add, [R("acc", mm_), R("tmp")], [R("acc", mm_)])
                                            else:
                                                tt("pool", mT[:, m, :], acc[:, mm_, :], tmp[:], ALU.add, [R("acc", mm_), R("tmp")], [R("mT")])
                            for ct in range(4):
                                wo_, wor_ = ws.get()
                                for tt_ in range(4):
                                    po, por = pbank()
                                    mm(po[:], [(mT[:, kc, tt_ * 128:(tt_ + 1) * 128], wo_[:, kc, :]) for kc in range(KC)], [wor_, R("mT")], por)
                                    tt("dve", tmp[:], po[:], gbc[:, ct * 512:(ct + 1) * 512], ALU.mult, [por, R("gbcs")], [R("tmp")])
                                    tt("pool", xt[:, tt_, ct * 512:(ct + 1) * 512], xt[:, tt_, ct * 512:(ct + 1) * 512], tmp[:], ALU.add, [R("xt3"), R("tmp")], [R("xt3")])
                            k.dma("sp", xdst_, xt[:], reads=[R("xt3")], writes=[Rx], sem="x1st")
                            for tt_ in range(4):
                                accf = acc[:].rearrange("p a b -> p (a b)")
                                tt("dve", accf, xt[:, tt_, :], xt[:, tt_, :], ALU.mult, [R("xt3")], [R("acc", i_) for i_ in range(4)])
                                k.op("dve", lambda e, tt_=tt_, accf=accf: e.tensor_reduce(out=ssq[:, tt_:tt_ + 1], in_=accf, axis=AX.X, op=ALU.add),
                                     reads=[R("acc", i_) for i_ in range(4)], writes=[R("ssq3")])
                            act(ssq[:, 4:8], ssq[:, 0:4], AF.Sqrt, [R("ssq3")], [R("ssq3")], scale=1.0 / D, bias=epsc[:])
                            recip(ssq[:, 4:8], ssq[:, 4:8], [R("ssq3")], [R("ssq3")])
                            for tt_ in range(4):
                                ts("dve", xn[:, tt_, :], xt[:, tt_, :], ssq[:, 4 + tt_:5 + tt_], None, ALU.mult, None, [R("xt3"), R("ssq3")], [R("bT")])
                            for kc in range(KC):
                                pk, pkr = pbank()
                                pkb = pk[:].bitcast(BF16)
                                def fnT(e, pkb=pkb, kc=kc):
                                    ins = None
                                    for tt_ in range(4):
                                        ins = e.transpose(pkb[:, tt_ * 128:(tt_ + 1) * 128], xn[:, tt_, kc * 128:(kc + 1) * 128], idb[:])
                                    return ins
                                k.op("pe", fnT, reads=[R("bT"), R("idb")], writes=[pkr])
                                act(hT[:, kc, :], pkb[:, 0:TC], AF.Identity, [pkr, RL], [R("hT3")], scale=A2[:, kc, j:j + 1], bias=modT[:, 48 + kc, j:j + 1])
                            k.dma("sp", hT_d[:, t0:t0 + TC].rearrange("(kc p) t -> p kc t", p=128), hT[:], reads=[R("hT3")], writes=[R("hT_d", c)], sem="hTst")

                    chk(5)
                    k.barrier()
                    with contextlib.ExitStack() as p5:
                        hT = sbt(p5, U("hT5"), [128, KC, TC], BF16)
                        aT = sbt(p5, U("aT"), [128, 64, TC], BF16)
                        rl = sbt(p5, U("rl"), [128, TC])
                        gbc = sbt(p5, U("gbc5"), [128, D])
                        xc = sbt(p5, U("xc"), [128, 4, 512]); tmp = sbt(p5, U("tmp5"), [128, 512])
                        items = []
                        for c in range(NCH):
                            for ft in range(16):
                                items.append(("ff1", l, 0, ft * 512))
                            for ct in range(4):
                                for g in range(4):
                                    items.append(("ff2", l, g * 2048, ct * 512))
                        ws = WStream(items)
                        cur_j = -1
                        for c in range(NCH):
                            j = 0 if c < NSCH else 1
                            t0 = c * TC
                            if j != cur_j:
                                k.dma("sp", gbc[:], gbc_d[1, j], reads=[R("gbc", 1, j)], writes=[R("gbc5")], sem="gbl")
                                cur_j = j
                            k.dma("sp", hT[:], hT_d[:, t0:t0 + TC].rearrange("(kc p) t -> p kc t", p=128), reads=[R("hT_d", c)], writes=[R("hT5")], sem="hTld")
                            _, xdst_ = xsrc(l, c)
                            Rx = R("xres", c)
                            for ft in range(16):
                                w1, w1r = ws.get()
                                for mm_ in range(4):
                                    f = ft * 4 + mm_
                                    pf, pfr = pbank()
                                    mm(pf[:], [(w1[:, kc, mm_ * 128:(mm_ + 1) * 128], hT[:, kc, :]) for kc in range(KC)], [w1r, R("hT5")], pfr)
                                    act(rl[:], pf[:], AF.Relu, [pfr, RL], [R("rl")], bias=T2[:, 64 + f:65 + f])
                                    tt("dve" if mm_ % 2 == 0 else "pool", aT[:, f, :], rl[:], rl[:], ALU.mult, [R("rl")], [R("aT", f)])
                            for ct in range(4):
                                k.dma("sp", xc[:], xdst_[:, :, ct * 512:(ct + 1) * 512], reads=[Rx], writes=[R("xc")], sem="xc")
                                banks = [pbank() for _ in range(4)]
                                for g in range(4):
                                    w2, w2r = ws.get()
                                    for tt_ in range(4):
                                        po, por = banks[tt_]
                                        mm(po[:], [(aT[:, g * 16 + kc, tt_ * 128:(tt_ + 1) * 128], w2[:, kc, :]) for kc in range(KC)],
                                           [w2r] + [R("aT", g * 16 + kc) for kc in range(KC)], por, start=(g == 0), stop=(g == 3))
                                for tt_ in range(4):
                                    po, por = banks[tt_]
                                    tt("dve", tmp[:], po[:], b2bc[:, ct * 512:(ct + 1) * 512], ALU.add, [por, RL], [R("tmp5")])
                                    tt("pool", tmp[:], tmp[:], gbc[:, ct * 512:(ct + 1) * 512], ALU.mult, [R("tmp5"), R("gbc5")], [R("tmp5")])
                                    tt("dve", xc[:, tt_, :], xc[:, tt_, :], tmp[:], ALU.add, [R("xc"), R("tmp5")], [R("xc")])
                                k.dma("sp", xdst_[:, :, ct * 512:(ct + 1) * 512], xc[:], reads=[R("xc")], writes=[Rx], sem="x2st")
        cast_layer(0)
        try:
            _layers()
        except _Stop:
            pass
        k.finish()
    nc._n_inst = k.n_inst
    return nc


def host_consts(NS):
    n_freq = 16
    t = np.arange(NS)
    row = (t // GRID_W).astype(np.float32)
    col = (t % GRID_W).astype(np.float32)
    inv = (10000.0 ** (-np.arange(n_freq, dtype=np.float32) / n_freq)).astype(np.float32)
    C = np.zeros((128, NS), np.float32)
    S = np.zeros((128, NS), np.float32)
    for p in range(128):
        d = p % 64
        axis, half, fr = d // 32, (d % 32) // 16, d % 16
        pos = row if axis == 0 else col
        ang = (pos * inv[fr]).astype(np.float32)
        C[p] = np.cos(ang)
        S[p] = np.sin(ang) * (-1.0 if half == 0 else 1.0)
    cm = np.zeros((5, 128, 128), np.float32)
    cm[0] = np.eye(128)
    for p in range(128):
        d = p % 64
        half = (d % 32) // 16
        cm[1][p, p + 16 if half == 0 else p - 16] = 1.0
    for hh in range(2):
        cm[2][hh * 64:(hh + 1) * 64, hh * 64:(hh + 1) * 64] = 1.0 / 64
    cm[3][:] = 1.0 / 128
    cm[4][:] = 1.0
    kk = np.arange(128)[:, None]
    qq = np.arange(128)[None, :]
    mA = (kk >= qq).astype(np.float32)
    mB = (kk <= qq).astype(np.float32)
    cmask = np.stack([np.tile(mA, (1, 4)), np.tile(mB, (1, 4))]).astype(ml_dtypes.bfloat16)
    cidb = np.eye(128, dtype=np.float32).astype(ml_dtypes.bfloat16)
    return {"ropeC": C, "ropeS": S, "cmat": cm, "cmask": cmask, "cidb": cidb}


WEIGHT_KEYS = ["w_ada", "b_ada", "norm1_g", "norm2_g", "w_in", "conv_w", "conv_b", "lru_wr", "lru_br", "lru_wi", "lru_bi",
               "lru_lambda", "win_qn", "win_kn", "win_sink", "grid_qn", "grid_kn", "diff_qn", "diff_kn", "diff_lq1",
               "diff_lk1", "diff_lq2", "diff_lk2", "diff_out_g", "w_branch", "w_gate", "b_gate", "w_o", "w_ff1", "b_ff1",
               "w_ff2", "b_ff2"]


def run(inputs, n_cores, NS, NPS, DEPTH):
    f = lambda a: np.ascontiguousarray(np.asarray(a, dtype=np.float32))
    inp = {kk: f(v) for kk, v in inputs.items()}
    nb_s = inp["x_sample"].shape[0]
    nc = build_nc(NS, NPS, DEPTH)
    consts = host_consts(NS)
    in_maps = []
    for core in range(n_cores):
        b = core % nb_s
        m = {kk: inp[kk] for kk in WEIGHT_KEYS}
        m.update(consts)
        m["xs"] = inp["x_sample"][b]
        m["xp"] = f(inp["x_prompt"][core * NPS:(core + 1) * NPS].reshape(NPS * 256, D))
        m["cwk"] = f(inp["cache_win_k"][b].reshape(DEPTH, NCTX, 128)); m["cwv"] = f(inp["cache_win_v"][b].reshape(DEPTH, NCTX, 128))
        m["cgk"] = f(inp["cache_grid_k"][b].reshape(DEPTH, NCTX, 128)); m["cgv"] = f(inp["cache_grid_v"][b].reshape(DEPTH, NCTX, 128))
        m["cdk"] = f(inp["cache_diff_k"][b].reshape(DEPTH, NCTX, 512)); m["cdv"] = f(inp["cache_diff_v"][b].reshape(DEPTH, NCTX, 512))
        m["slru"] = f(inp["state_lru"][b])
        m["cond"] = f(np.stack([inp["c"][b], inp["c_ctx"]]))
        in_maps.append(m)
    res = run_bass_kernel_spmd(nc, in_maps, core_ids=list(range(n_cores)))
    rs = res.results
    LAST[0] = rs
    y_p = np.concatenate([r["y_p"].reshape(NPS, 256, D) for r in rs], axis=0)
    y_s = np.stack([rs[b]["y_s"] for b in range(nb_s)], axis=0)
    cat = lambda kk, shp: np.concatenate([r[kk].reshape((NPS, DEPTH, 256) + shp) for r in rs], axis=0)
    outs = (y_p, y_s, cat("o_wk", (2, 64)), cat("o_wv", (2, 64)), cat("o_gk", (2, 64)), cat("o_gv", (2, 64)),
            cat("o_dk", (4, 2, 64)), cat("o_dv", (4, 128)),
            np.concatenate([r["o_lru"].reshape(NPS, DEPTH, 2, 512) for r in rs], axis=0))
    return tuple(np.ascontiguousarray(o, dtype=np.float32) for o in outs)


def kernel(**inputs):
    return run(inputs, 8, 4096, 4, 4)
```

```python
import contextlib
import math
import numpy as np
import ml_dtypes
import concourse.bass as bass
import concourse.mybir as mybir
from concourse.bass_utils import run_bass_kernel_spmd

F32 = mybir.dt.float32
BF16 = mybir.dt.bfloat16
AF = mybir.ActivationFunctionType
ALU = mybir.AluOpType
AX = mybir.AxisListType

D = 2048
KC = 16
TC = 512
NCTX = 512
EPS = 1e-6
GRID_W = 64
WMAP = {"ada": (2048, 12288), "in": (2048, 4096), "gate": (2048, 8192), "br": (2048, 2048),
        "o": (2048, 2048), "ff1": (2048, 8192), "ff2": (8192, 2048)}
WNAME = {"ada": "w_ada", "in": "w_in", "gate": "w_gate", "br": "w_branch", "o": "w_o", "ff1": "w_ff1",
         "ff2": "w_ff2"}


class Res:
    __slots__ = ("name", "w", "r")

    def __init__(self, name):
        self.name = name
        self.w = None
        self.r = {}


class KB:
    def __init__(self, nc, stack):
        self.nc = nc
        self.stack = stack
        self.eng = {"pe": nc.tensor, "act": nc.scalar, "dve": nc.vector, "pool": nc.gpsimd, "sp": nc.sync}
        self.esem = {e: stack.enter_context(nc.semaphore("es_" + e)) for e in self.eng}
        self.cnt = {e: 0 for e in self.eng}
        self.seen = {e: {} for e in self.eng}
        self.res = {}
        self.dsems = {}
        self.n_inst = 0

    def R(self, *key):
        r = self.res.get(key)
        if r is None:
            r = Res(key)
            self.res[key] = r
        return r

    def dsem(self, name):
        d = self.dsems.get(name)
        if d is None:
            d = [self.stack.enter_context(self.nc.semaphore("ds_" + name)), 0]
            self.dsems[name] = d
        return d

    def _collect(self, e, reads, writes):
        waits = {}

        def need(tok, war=False):
            if tok is None:
                return
            kind, key, val = tok
            if kind == "eng":
                if key == e and (war or e == "pe"):
                    return
                waits[("eng", key)] = max(waits.get(("eng", key), 0), val)
            else:
                waits[("dma", key)] = 1
        for r in reads:
            need(r.w)
        for w in writes:
            need(w.w)
            for t in w.r.values():
                need(t, war=True)
        return waits

    def _emit_waits(self, e, waits):
        eng = self.eng[e]
        for (kind, key) in waits:
            if kind == "eng":
                val = waits[(kind, key)]
                sem = self.esem[key]
            else:
                d = self.dsems[key]
                val = d[1]
                sem = d[0]
            if self.seen[e].get((kind, key), 0) >= val:
                continue
            eng.wait_ge(sem, val)
            self.seen[e][(kind, key)] = val

    def op(self, e, fn, reads=(), writes=()):
        if DEAD[0]:
            return None
        waits = self._collect(e, reads, writes)
        self._emit_waits(e, waits)
        ins = fn(self.eng[e])
        self.cnt[e] += 1
        ins.then_inc(self.esem[e], 1)
        tok = ("eng", e, self.cnt[e])
        for r in reads:
            r.r[("eng", e)] = tok
        for w in writes:
            w.w = tok
            w.r = {}
        self.n_inst += 1
        return tok

    def dma(self, q, out, in_, reads=(), writes=(), sem=None, **kw):
        if DEAD[0]:
            return None
        waits = self._collect(q, reads, writes)
        self._emit_waits(q, waits)
        d = self.dsem(sem)
        ins = self.eng[q].dma_start(out=out, in_=in_, **kw)
        d[1] += 16
        ins.then_inc(d[0], 16)
        tok = ("dma", sem, d[1])
        for r in reads:
            r.r[("dma", sem)] = tok
        for w in writes:
            w.w = tok
            w.r = {}
        self.n_inst += 1
        return tok

    def barrier(self):
        if DEAD[0]:
            return
        for e in self.eng:
            eng = self.eng[e]
            for o in self.eng:
                if o == e or self.cnt[o] == 0:
                    continue
                if self.seen[e].get(("eng", o), 0) >= self.cnt[o]:
                    continue
                eng.wait_ge(self.esem[o], self.cnt[o])
                self.seen[e][("eng", o)] = self.cnt[o]
            for name, d in self.dsems.items():
                if name.startswith("cast") or name == "cstv" or d[1] == 0:
                    continue
                if self.seen[e].get(("dma", name), 0) >= d[1]:
                    continue
                eng.wait_ge(d[0], d[1])
                self.seen[e][("dma", name)] = d[1]

    def finish(self):
        sp = self.eng["sp"]
        for name, d in self.dsems.items():
            if d[1] > 0:
                sp.wait_ge(d[0], d[1])
        for e in self.eng:
            if e != "sp" and self.cnt[e] > 0:
                sp.wait_ge(self.esem[e], self.cnt[e])


class _Stop(Exception):
    pass


STOP = [99]


DEAD = [False]
DBG = [False]
LAST = [None]


def chk(n):
    if STOP[0] == n:
        DEAD[0] = True


def build_nc(NS, NPS, DEPTH):
    NP = NPS * 256
    NT = NS + NP
    NSCH = NS // TC
    NCH = NT // TC
    KOFFP = NS + NCTX
    TK = NS + NCTX + NP

    nc = bass.Bass("TRN2", target_bir_lowering=False)
    DEAD[0] = False

    def din(name, shape, dt=F32):
        return nc.dram_tensor(name, list(shape), dt, kind="ExternalInput").ap()

    def dout(name, shape, dt=F32):
        return nc.dram_tensor(name, list(shape), dt, kind="ExternalOutput").ap()

    def dscr(name, shape, dt=BF16):
        return nc.dram_tensor(name, list(shape), dt, kind="Internal").ap()

    xs = din("xs", [NS, D]); xp = din("xp", [NP, D])
    cwk = din("cwk", [DEPTH, NCTX, 128]); cwv = din("cwv", [DEPTH, NCTX, 128])
    cgk = din("cgk", [DEPTH, NCTX, 128]); cgv = din("cgv", [DEPTH, NCTX, 128])
    cdk = din("cdk", [DEPTH, NCTX, 512]); cdv = din("cdv", [DEPTH, NCTX, 512])
    slru = din("slru", [DEPTH, 2, 512])
    cond = din("cond", [2, D])
    W = {}
    for kk, (K_, N_) in WMAP.items():
        if kk == "br":
            W[kk] = din("w_branch", [DEPTH, 4, 512, 2048])
        else:
            W[kk] = din(WNAME[kk], [DEPTH, K_, N_])
    b_ada = din("b_ada", [DEPTH, 12288]); norm1_g = din("norm1_g", [DEPTH, D]); norm2_g = din("norm2_g", [DEPTH, D])
    conv_w = din("conv_w", [DEPTH, 4, 512]); conv_b = din("conv_b", [DEPTH, 512])
    lru_wr = din("lru_wr", [DEPTH, 2, 8, 64, 64]); lru_br = din("lru_br", [DEPTH, 2, 512])
    lru_wi = din("lru_wi", [DEPTH, 2, 8, 64, 64]); lru_bi = din("lru_bi", [DEPTH, 2, 512])
    lru_lambda = din("lru_lambda", [DEPTH, 2, 512])
    gains = {n: din(n, [DEPTH, 64]) for n in ("win_qn", "win_kn", "grid_qn", "grid_kn", "diff_qn", "diff_kn",
                                           "diff_lq1", "diff_lk1", "diff_lq2", "diff_lk2")}
    win_sink = din("win_sink", [DEPTH, 8]); diff_out_g = din("diff_out_g", [DEPTH, 128])
    b_gate = din("b_gate", [DEPTH, 8192]); b_ff1 = din("b_ff1", [DEPTH, 8192]); b_ff2 = din("b_ff2", [DEPTH, D])
    ropeC = din("ropeC", [128, NS]); ropeS = din("ropeS", [128, NS])
    cmat = din("cmat", [5, 128, 128])
    cmask = din("cmask", [2, 128, 512], BF16)
    cidb = din("cidb", [128, 128], BF16)

    y_s = dout("y_s", [NS, D]); y_p = dout("y_p", [NP, D])
    o_wk = dout("o_wk", [NPS, DEPTH, 256, 128]); o_wv = dout("o_wv", [NPS, DEPTH, 256, 128])
    o_gk = dout("o_gk", [NPS, DEPTH, 256, 128]); o_gv = dout("o_gv", [NPS, DEPTH, 256, 128])
    o_dk = dout("o_dk", [NPS, DEPTH, 256, 512]); o_dv = dout("o_dv", [NPS, DEPTH, 256, 512])
    o_lru = dout("o_lru", [NPS, DEPTH, 2, 512])

    wbf = {kk: dscr("wbf_" + kk, [DEPTH, K_, N_]) for kk, (K_, N_) in WMAP.items()}
    hT_d = dout("hT_d", [D, NT], BF16) if DBG[0] else dscr("hT_d", [D, NT])
    xaT_d = dscr("xaT_d", [512, NT], F32); yaT_d = dscr("yaT_d", [512, NT], F32)
    dd = (lambda n, s_: dout(n, s_, BF16)) if DBG[0] else dscr
    qT_d = {b: dd("qT_d%d" % b, [512, NT]) for b in (1, 2, 3)}
    kT_d = {1: dd("kT_d1", [128, TK]), 2: dd("kT_d2", [128, TK]), 3: dd("kT_d3", [512, TK])}
    v_d = {1: dd("v_d1", [TK, 128]), 2: dd("v_d2", [TK, 128]), 3: dd("v_d3", [TK, 512])}
    brT_d = dout("brT_d", [D, NT], BF16) if DBG[0] else dscr("brT_d", [D, NT])
    gbc_d = dout("gbc_d", [2, 2, 128, D], F32) if DBG[0] else dscr("gbc_d", [2, 2, 128, D], F32)

    def xsrc(l, c):
        if c < NSCH:
            src = xs if l == 0 else y_s
            dst = y_s
            r0 = c * TC
        else:
            src = xp if l == 0 else y_p
            dst = y_p
            r0 = (c - NSCH) * TC
        f = lambda t: t[r0:r0 + TC, :].rearrange("(tt p) d -> p tt d", p=128)
        return f(src), f(dst)

    with contextlib.ExitStack() as st:
        k = KB(nc, st)
        R = k.R

        def sbt(stack, name, shape, dt=F32):
            return stack.enter_context(nc.sbuf_tensor(name, list(shape), dt))

        PBall = st.enter_context(nc.psum_tensor("pball", [128, 8, 512], F32))
        PB = [PBall[:, i, :] for i in range(8)]
        pb_i = [0]

        def pbank(lo=0, hi=8):
            i = lo + (pb_i[0] % (hi - lo))
            pb_i[0] += 1
            return PB[i], R("pb", i)

        cm = sbt(st, "cm", [128, 5, 128])
        idb = sbt(st, "idb", [128, 128], BF16)
        onesb = sbt(st, "onesb", [128, 128], BF16)
        masks = sbt(st, "masks", [128, 2, 512], BF16)
        epsc = sbt(st, "epsc", [128, 1]); onec = sbt(st, "onec", [128, 1])
        NW = 3
        wring = [sbt(st, "wring%d" % i, [128, KC, 512], BF16) for i in range(NW)]
        k.dma("sp", cm[:], cmat.rearrange("c p n -> p c n"), writes=[R("cm")], sem="c0")
        k.dma("sp", idb[:], cidb[:, :], writes=[R("idb")], sem="c0")
        k.dma("sp", masks[:], cmask.rearrange("c p n -> p c n"), writes=[R("masks")], sem="c0")
        k.op("dve", lambda e: e.memset(epsc[:], EPS), writes=[R("epsc")])
        k.op("dve", lambda e: e.memset(onec[:], 1.0), writes=[R("onec")])
        k.op("dve", lambda e: e.memset(onesb[:], 1.0), writes=[R("onesb")])
        identf = cm[:, 0, :]; permf = cm[:, 1, :]; bd64 = cm[:, 2, :]; avg128 = cm[:, 3, :]; onesf = cm[:, 4, :]
        RC = R("cm")

        def cast_layer(l):
            for kk, (K_, N_) in WMAP.items():
                src = W[kk][l] if kk != "br" else W[kk][l].rearrange("b c n -> (b c) n")
                npieces = K_ // 512
                for i in range(npieces):
                    k.dma("pool", wbf[kk][l, i * 512:(i + 1) * 512, :], src[i * 512:(i + 1) * 512, :],
                          writes=[R("wbf", kk, l, i)], sem="cast%d" % l, max_dma_last_dim=4096)

        def wres(kk, l):
            return [R("wbf", kk, l, i) for i in range(WMAP[kk][0] // 512)]

        class WStream:
            cur = [0]

            def __init__(self, items):
                self.items = items
                self.issued = 0
                self.taken = 0
                self.slots = {}

            def _issue(self):
                kk, l, r0, c0 = self.items[self.issued]
                s = WStream.cur[0] % NW
                WStream.cur[0] += 1
                src = wbf[kk][l, r0:r0 + 2048, c0:c0 + 512].rearrange("(kc p) n -> p kc n", p=128)
                k.dma("sp", wring[s][:], src, reads=wres(kk, l), writes=[R("wring", s)], sem="w%d" % s)
                self.slots[self.issued] = s
                self.issued += 1

            def get(self):
                while self.issued < len(self.items) and self.issued < self.taken + NW - 1:
                    self._issue()
                if self.issued <= self.taken:
                    self._issue()
                s = self.slots.pop(self.taken)
                self.taken += 1
                return wring[s], R("wring", s)

        def mm(out_ap, pairs, reads, wr, start=True, stop=True):
            def fn(e):
                n = len(pairs)
                ins = None
                for i, (l_, r_) in enumerate(pairs):
                    ins = e.matmul(out_ap, l_, r_, start=(start and i == 0), stop=(stop and i == n - 1))
                return ins
            k.op("pe", fn, reads=reads, writes=[wr])

        def act(out, in_, func, reads, writes, scale=1.0, bias=None, eng="act"):
            kw = {}
            if bias is not None:
                kw["bias"] = bias
            k.op("act", lambda e: e.activation(out=out, in_=in_, func=func, scale=scale, **kw), reads=reads, writes=writes)

        def tt(eng, out, in0, in1, op, reads, writes):
            k.op(eng, lambda e: e.tensor_tensor(out=out, in0=in0, in1=in1, op=op), reads=reads, writes=writes)

        def ts(eng, out, in0, s1, s2, op0, op1, reads, writes):
            if s2 is None:
                k.op(eng, lambda e: e.tensor_scalar(out=out, in0=in0, scalar1=s1, scalar2=None, op0=op0), reads=reads, writes=writes)
            else:
                k.op(eng, lambda e: e.tensor_scalar(out=out, in0=in0, scalar1=s1, scalar2=s2, op0=op0, op1=op1), reads=reads, writes=writes)

        def stt(eng, out, in0, scalar, in1, op0, op1, reads, writes, accum_out=None):
            kw = {} if accum_out is None else {"accum_out": accum_out}
            k.op(eng, lambda e: e.scalar_tensor_tensor(out=out, in0=in0, scalar=scalar, in1=in1, op0=op0, op1=op1, **kw),
                 reads=reads, writes=writes)

        def copy(eng, out, in_, reads, writes):
            if eng == "act":
                act(out, in_, AF.Identity, reads, writes)
            else:
                k.op(eng, lambda e: e.tensor_copy(out=out, in_=in_), reads=reads, writes=writes)

        def recip(out, in_, reads, writes):
            k.op("dve", lambda e: e.reciprocal(out=out, in_=in_), reads=reads, writes=writes)

        uid = [0]

        def U(prefix):
            uid[0] += 1
            return "%s_%d" % (prefix, uid[0])

        def _layers():
            for l in range(DEPTH):
                k.barrier()
                lam_init = 0.8 - 0.6 * math.exp(-0.3 * l)
                with contextlib.ExitStack() as ls:
                    T1 = sbt(ls, U("T1"), [128, 128]); T2 = sbt(ls, U("T2"), [128, 128]); T3 = sbt(ls, U("T3"), [128, 64])
                    modT = sbt(ls, U("modT"), [128, 96, 2])
                    A1 = sbt(ls, U("A1"), [128, KC, 2]); A2 = sbt(ls, U("A2"), [128, KC, 2])
                    bdw = sbt(ls, U("bdw"), [128, 16, 128], BF16)
                    clam = sbt(ls, U("clam"), [128, 8])
                    lamt = sbt(ls, U("lamt"), [128, 4])
                    sinkT = sbt(ls, U("sinkT"), [65, 8, 128])
                    b2bc = sbt(ls, U("b2bc"), [128, D])
                    knbc = sbt(ls, U("knbc"), [128, 3, 64])
                    RL = R("layerparams", l)
                    with contextlib.ExitStack() as ps_:
                        stg = sbt(ps_, U("stg"), [128, 128]); stg2 = sbt(ps_, U("stg2"), [128, 128]); stg3 = sbt(ps_, U("stg3"), [64, 128])
                        k.dma("sp", stg[0:96, :], b_ada[l].rearrange("(r p) -> r p", p=128), writes=[R("stg")], sem="p_stg")
                        k.dma("sp", stg[96:112, :], norm1_g[l].rearrange("(r p) -> r p", p=128), writes=[R("stg")], sem="p_stg")
                        k.dma("sp", stg[112:128, :], norm2_g[l].rearrange("(r p) -> r p", p=128), writes=[R("stg")], sem="p_stg")
                        pbk, pr = pbank()
                        k.op("pe", lambda e: e.transpose(pbk[:, 0:128], stg[:], identf), reads=[R("stg"), RC], writes=[pr])
                        copy("dve", T1[:], pbk[:, 0:128], [pr], [RL])
                        k.dma("sp", stg2[0:64, :], b_gate[l].rearrange("(r p) -> r p", p=128), writes=[R("stg2")], sem="p_stg2")
                        k.dma("sp", stg2[64:128, :], b_ff1[l].rearrange("(r p) -> r p", p=128), writes=[R("stg2")], sem="p_stg2")
                        pbk2, pr2 = pbank()
                        k.op("pe", lambda e: e.transpose(pbk2[:, 0:128], stg2[:], identf), reads=[R("stg2"), RC], writes=[pr2])
                        copy("dve", T2[:], pbk2[:, 0:128], [pr2], [RL])
                        S3 = R("stg3")
                        k.op("dve", lambda e: e.memset(stg3[:], 0.0), writes=[S3])
                        k.dma("sp", stg3[0:16, :], conv_w[l].rearrange("j (fc p) -> (j fc) p", p=128), writes=[S3], sem="p_stg3")
                        k.dma("sp", stg3[16:20, :], conv_b[l].rearrange("(fc p) -> fc p", p=128), writes=[S3], sem="p_stg3")
                        k.dma("sp", stg3[20:28, :], lru_br[l].rearrange("k (fc p) -> (k fc) p", p=128), writes=[S3], sem="p_stg3")
                        k.dma("sp", stg3[28:36, :], lru_bi[l].rearrange("k (fc p) -> (k fc) p", p=128), writes=[S3], sem="p_stg3")
                        k.dma("sp", stg3[36:44, :], lru_lambda[l].rearrange("k (fc p) -> (k fc) p", p=128), writes=[S3], sem="p_stg3")
                        k.dma("sp", stg3[44:45, :], diff_out_g[l:l + 1, :], writes=[S3], sem="p_stg3")
                        for gi, gn in enumerate(("win_qn", "win_kn", "grid_qn", "grid_kn", "diff_qn", "diff_kn")):
                            for hh in range(2):
                                k.dma("sp", stg3[45 + gi:46 + gi, hh * 64:(hh + 1) * 64], gains[gn][l:l + 1, :], writes=[S3], sem="p_stg3")
                        pbk3, pr3 = pbank()
                        k.op("pe", lambda e: e.transpose(pbk3[:, 0:64], stg3[:], identf[0:64, 0:64]), reads=[S3, RC], writes=[pr3])
                        copy("dve", T3[:], pbk3[:, 0:64], [pr3], [RL])
                        for gi in (45, 47, 49):
                            ts("dve", T3[:, gi:gi + 1], T3[:, gi:gi + 1], 0.125, None, ALU.mult, None, [RL], [RL])
                        act(clam[:], T3[:, 36:44], AF.Exp, [RL], [RL], scale=-1.0)
                        act(clam[:], clam[:], AF.Ln, [RL], [RL], bias=onec[:])
                        ts("dve", clam[:], clam[:], -8.0, None, ALU.mult, None, [RL], [RL])
                        stg4 = sbt(ps_, U("stg4"), [32, 128])
                        k.dma("sp", stg4[:], cond.rearrange("j (kc p) -> (j kc) p", p=128), writes=[R("stg4")], sem="p_stg4")
                        pbk4, pr4 = pbank()
                        k.op("pe", lambda e: e.transpose(pbk4[:, 0:32], stg4[:], identf[0:32, 0:32]), reads=[R("stg4"), RC], writes=[pr4])
                        scT = sbt(ps_, U("scT"), [128, KC, 2], BF16)
                        act(scT[:].rearrange("p kc j -> p j kc"), pbk4[:, 0:32].rearrange("p (j kc) -> p j kc", j=2), AF.Silu, [pr4], [R("scT")])
                        pm, pmr = pbank()
                        ws = WStream([("ada", l, 0, t * 512) for t in range(24)])
                        for t in range(24):
                            wt, wr_ = ws.get()
                            for mb in range(4):
                                col = (t * 4 + mb) * 2
                                mm(pm[:, col:col + 2], [(wt[:, kc, mb * 128:(mb + 1) * 128], scT[:, kc, :]) for kc in range(KC)],
                                   [wr_, R("scT")], pmr)
                        for j in range(2):
                            tt("dve", modT[:, :, j], pm[:, 0:192].rearrange("p (c j) -> p c j", j=2)[:, :, j], T1[:, 0:96], ALU.add, [pmr, RL], [RL])
                        for j in range(2):
                            stt("dve", A1[:, :, j], modT[:, 16:32, j], 1.0, T1[:, 96:112], ALU.add, ALU.mult, [RL], [RL])
                            stt("dve", A2[:, :, j], modT[:, 64:80, j], 1.0, T1[:, 112:128], ALU.add, ALU.mult, [RL], [RL])
                        gst = sbt(ps_, U("gst"), [128, D]); gm = sbt(ps_, U("gm"), [128, 128])
                        for i, base in enumerate((32, 80)):
                            for j in range(2):
                                for q4 in range(4):
                                    pg, pgr = pbank()
                                    for kk4 in range(4):
                                        kc = q4 * 4 + kk4
                                        ts("dve", gm[:], onesf, modT[:, base + kc, j:j + 1], None, ALU.mult, None, [RL, RC], [R("gm")])
                                        mm(pg[:, kk4 * 128:(kk4 + 1) * 128], [(gm[:], identf)], [R("gm"), RC], pgr)
                                    copy("act", gst[:, q4 * 512:(q4 + 1) * 512], pg[:], [pgr], [R("gst")])
                                k.dma("sp", gbc_d[i, j], gst[:], reads=[R("gst")], writes=[R("gbc", i, j)], sem="gst")
                        bst = sbt(ps_, U("bst"), [128, 16, 128])
                        k.op("dve", lambda e: e.memset(bst[:], 0.0), writes=[R("bst")])
                        for ri, wsrc in enumerate((lru_wr, lru_wi)):
                            for kd in range(2):
                                for par in range(2):
                                    for fc in range(4):
                                        idx = (fc * 2 + kd) * 2 + ri
                                        k.dma("sp", bst[par * 64:(par + 1) * 64, idx, par * 64:(par + 1) * 64], wsrc[l, kd, 2 * fc + par],
                                              writes=[R("bst")], sem="p_bst")
                        copy("dve", bdw[:], bst[:], [R("bst")], [RL])
                        lq = sbt(ps_, U("lq"), [128, 4, 64])
                        for i, gn in enumerate(("diff_lq1", "diff_lk1", "diff_lq2", "diff_lk2")):
                            k.dma("sp", lq[:, i, :], gains[gn][l:l + 1, :].partition_broadcast(128), writes=[R("lq")], sem="p_lq")
                        lj = sbt(ps_, U("lj"), [128, 64]); l2 = sbt(ps_, U("l2"), [128, 2])
                        for i_ in range(2):
                            tt("dve", lj[:], lq[:, 2 * i_, :], lq[:, 2 * i_ + 1, :], ALU.mult, [R("lq")], [R("lj")])
                            k.op("dve", lambda e, i_=i_: e.tensor_reduce(out=l2[:, i_:i_ + 1], in_=lj[:], axis=AX.X, op=ALU.add), reads=[R("lj")], writes=[R("l2")])
                        act(l2[:], l2[:], AF.Exp, [R("l2")], [R("l2")])
                        tt("dve", lamt[:, 0:1], l2[:, 0:1], l2[:, 1:2], ALU.subtract, [R("l2")], [RL])
                        ts("dve", lamt[:, 0:1], lamt[:, 0:1], lam_init, None, ALU.add, None, [RL], [RL])
                        ts("dve", lamt[:, 1:2], lamt[:, 0:1], -1.0, None, ALU.mult, None, [RL], [RL])
                        ts("dve", lamt[:, 2:3], T3[:, 44:45], 1.0 - lam_init, None, ALU.mult, None, [RL], [RL])
                        sraw = sbt(ps_, U("sraw"), [65, 8])
                        k.dma("sp", sraw[64:65, :], win_sink[l:l + 1, :], writes=[R("sraw")], sem="p_sraw")
                        act(sraw[64:65, :], sraw[64:65, :], AF.Exp, [R("sraw")], [R("sraw")])
                        for h in range(8):
                            ts("dve", sinkT[64:65, h, :], onesf[64:65, :], sraw[64:65, h:h + 1], None, ALU.mult, None, [R("sraw"), RC], [RL])
                        k.dma("sp", b2bc[:], b_ff2[l:l + 1, :].partition_broadcast(128), writes=[RL], sem="p_b2bc")
                        for i, gn in enumerate(("win_kn", "grid_kn", "diff_kn")):
                            k.dma("sp", knbc[:, i, :], gains[gn][l:l + 1, :].partition_broadcast(128), writes=[RL], sem="p_knbc")
                        cst = sbt(ps_, U("cst"), [128, 4, 512]); cko = sbt(ps_, U("cko"), [128, 4, 512], BF16)
                        for b, (ck, cv, nf) in {1: (cwk, cwv, 128), 2: (cgk, cgv, 128), 3: (cdk, cdv, 512)}.items():
                            k.dma("pool", v_d[b][NS:NS + NCTX, :], cv[l], writes=[R("v_d", b, "ctx")], sem="cstv")
                            k.dma("sp", cst[:, :, 0:nf], ck[l].rearrange("(tt p) f -> p tt f", p=128), writes=[R("cst")], sem="cst")
                            for fc in range(nf // 128):
                                pk, pkr = pbank()
                                def fnT(e, pk=pk, fc=fc):
                                    ins = None
                                    for tt_ in range(4):
                                        ins = e.transpose(pk[:, tt_ * 128:(tt_ + 1) * 128], cst[:, tt_, fc * 128:(fc + 1) * 128], identf)
                                    return ins
                                k.op("pe", fnT, reads=[R("cst"), RC], writes=[pkr])
                                copy("act", cko[:, fc, :], pk[:], [pkr], [R("cko")])
                            k.dma("sp", kT_d[b][:, NS:NS + NCTX].rearrange("(fc p) t -> p fc t", p=128), cko[:, 0:nf // 128, :],
                                  reads=[R("cko")], writes=[R("kT_d", b, "ctx")], sem="cstk")

                    chk(1)
                    k.barrier()
                    if l + 1 < DEPTH:
                        cast_layer(l + 1)
                    with contextlib.ExitStack() as p1:
                        xt = sbt(p1, U("xt"), [128, 4, D]); xn = sbt(p1, U("xn"), [128, 4, D], BF16)
                        hT = sbt(p1, U("hT"), [128, KC, TC], BF16)
                        ssq = sbt(p1, U("ssq"), [128, 8])
                        rC = sbt(p1, U("rC"), [128, TC]); rS = sbt(p1, U("rS"), [128, TC])
                        zsA = [sbt(p1, U("zs"), [128, TC]) for _ in range(2)]; sqA = [sbt(p1, U("sq"), [128, TC]) for _ in range(2)]
                        sdA = [sbt(p1, U("sd"), [128, TC]) for _ in range(2)]
                        qnA = [sbt(p1, U("qn"), [128, TC]) for _ in range(2)]; t1A = [sbt(p1, U("t1"), [128, TC]) for _ in range(2)]
                        qk_i = [0]
                        qo = sbt(p1, U("qo"), [128, 4, TC], BF16)
                        xo = sbt(p1, U("xo"), [128, 4, TC])
                        vo = sbt(p1, U("vo"), [128, 4, 768], BF16)
                        xtf_ = xt[:].rearrange("p a b -> p (a b)")
                        vof = xtf_[:, 0:3072].rearrange("p (t f) -> p t f", f=768)
                        kof = xtf_[:, 3072:6144].rearrange("p (t f) -> p t f", f=768)
                        ksq = sbt(p1, U("ksq"), [128, 768]); kss = sbt(p1, U("kss"), [128, 12])
                        items = [("in", l, 0, t * 512) for c in range(NCH) for t in range(8)]
                        ws = WStream(items)
                        for c in range(NCH):
                            j = 0 if c < NSCH else 1
                            is_s = c < NSCH
                            t0 = c * TC
                            xsrc_, _ = xsrc(l, c)
                            Rx = R("xres", c)
                            k.dma("sp", xt[:], xsrc_, reads=[Rx], writes=[R("xt")], sem="xt")
                            if is_s:
                                k.dma("sp", rC[:], ropeC[:, t0:t0 + TC], writes=[R("rC")], sem="rope")
                                k.dma("sp", rS[:], ropeS[:, t0:t0 + TC], writes=[R("rS")], sem="rope")
                            for tt_ in range(4):
                                xof = xo[:].rearrange("p a b -> p (a b)")
                                tt("dve", xof, xt[:, tt_, :], xt[:, tt_, :], ALU.mult, [R("xt")], [R("xo")])
                                k.op("dve", lambda e, tt_=tt_, xof=xof: e.tensor_reduce(out=ssq[:, tt_:tt_ + 1], in_=xof, axis=AX.X, op=ALU.add),
                                     reads=[R("xo")], writes=[R("ssq")])
                            act(ssq[:, 4:8], ssq[:, 0:4], AF.Sqrt, [R("ssq")], [R("ssq")], scale=1.0 / D, bias=epsc[:])
                            recip(ssq[:, 4:8], ssq[:, 4:8], [R("ssq")], [R("ssq")])
                            for tt_ in range(4):
                                ts("dve", xn[:, tt_, :], xt[:, tt_, :], ssq[:, 4 + tt_:5 + tt_], None, ALU.mult, None, [R("xt"), R("ssq")], [R("xn")])
                            for kc in range(KC):
                                pk, pkr = pbank()
                                pkb = pk[:].bitcast(BF16)
                                def fnT(e, pkb=pkb, kc=kc):
                                    ins = None
                                    for tt_ in range(4):
                                        ins = e.transpose(pkb[:, tt_ * 128:(tt_ + 1) * 128], xn[:, tt_, kc * 128:(kc + 1) * 128], idb[:])
                                    return ins
                                k.op("pe", fnT, reads=[R("xn"), R("idb")], writes=[pkr])
                                act(hT[:, kc, :], pkb[:, 0:TC], AF.Identity, [pkr, RL], [R("hT")], scale=A1[:, kc, j:j + 1], bias=modT[:, kc, j:j + 1])
                            k.dma("sp", hT_d[:, t0:t0 + TC].rearrange("(kc p) t -> p kc t", p=128), hT[:], reads=[R("hT")], writes=[R("hT_d", c)], sem="hTst")

                            chk(11)

                            def fm_block(wt, wr_, mb):
                                pz, pzr = pbank()
                                mm(pz[:], [(wt[:, kc, mb * 128:(mb + 1) * 128], hT[:, kc, :]) for kc in range(KC)], [wr_, R("hT")], pzr)
                                return pz, pzr

                            def qk_block(pz, pzr, gcol, dst, slot, rope):
                                qi = qk_i[0] % 2
                                qk_i[0] += 1
                                zs, sq, sd, qn, t1 = zsA[qi], sqA[qi], sdA[qi], qnA[qi], t1A[qi]
                                Rz, Rsq, Rsd, Rqn, Rt1 = R("zs", qi), R("sq", qi), R("sd", qi), R("qn", qi), R("t1", qi)
                                copy("act", zs[:], pz[:], [pzr], [Rz])
                                tt("pool", sq[:], zs[:], zs[:], ALU.mult, [Rz], [Rsq])
                                pm_, pmr_ = pbank()
                                mm(pm_[:], [(bd64, sq[:])], [RC, Rsq], pmr_)
                                act(sd[:], pm_[:], AF.Sqrt, [pmr_], [Rsd], bias=epsc[:])
                                recip(sd[:], sd[:], [Rsd], [Rsd])
                                if not rope:
                                    stt("dve", dst[:, slot, :], zs[:], T3[:, gcol:gcol + 1], sd[:], ALU.mult, ALU.mult, [Rz, Rsd, RL], [R("qo")])
                                    return
                                stt("dve", qn[:], zs[:], T3[:, gcol:gcol + 1], sd[:], ALU.mult, ALU.mult, [Rz, Rsd, RL], [Rqn])
                                ps_w, psr = pbank()
                                mm(ps_w[:], [(permf, qn[:])], [RC, Rqn], psr)
                                tt("pool", t1[:], qn[:], rC[:], ALU.mult, [Rqn, R("rC")], [Rt1])
                                tt("dve", qn[:], ps_w[:], rS[:], ALU.mult, [psr, R("rS")], [Rqn])
                                tt("dve", dst[:, slot, :], t1[:], qn[:], ALU.add, [Rt1, Rqn], [R("qo")])

                            def tm_block(wt, wr_, tt_, c0, c1, dst_off):
                                pz, pzr = pbank()
                                mm(pz[:, 0:c1 - c0], [(hT[:, kc, tt_ * 128:(tt_ + 1) * 128], wt[:, kc, c0:c1]) for kc in range(KC)], [wr_, R("hT")], pzr)
                                return pz, pzr

                            kcol = t0 if is_s else KOFFP + (t0 - NS)
                            for t in range(8):
                                chk(20 + t)
                                wt, wr_ = ws.get()
                                if t in (0, 1):
                                    for mb in range(4):
                                        pz, pzr = fm_block(wt, wr_, mb)
                                        copy("act", xo[:, mb, :], pz[:], [pzr], [R("xo")])
                                    dstd = xaT_d if t == 0 else yaT_d
                                    k.dma("sp", dstd[:, t0:t0 + TC].rearrange("(fc p) t -> p fc t", p=128), xo[:], reads=[R("xo")],
                                          writes=[R("xyT_d", t, c)], sem="xost")
                                elif t in (2, 5):
                                    b = 1 if t == 2 else 3
                                    for mb in range(4):
                                        pz, pzr = fm_block(wt, wr_, mb)
                                        qk_block(pz, pzr, 45 if b == 1 else 49, qo, mb, is_s)
                                    k.dma("sp", qT_d[b][:, t0:t0 + TC].rearrange("(fc p) t -> p fc t", p=128), qo[:], reads=[R("qo")],
                                          writes=[R("qT_d", b, c)], sem="qost")
                                elif t == 3:
                                    pz, pzr = fm_block(wt, wr_, 0)
                                    qk_block(pz, pzr, 46, qo, 0, is_s)
                                    k.dma("sp", kT_d[1][:, kcol:kcol + TC], qo[:, 0, :], reads=[R("qo")], writes=[R("kT_d", 1, c)], sem="qost")
                                    for mb in (2, 3):
                                        pz, pzr = fm_block(wt, wr_, mb)
                                        qk_block(pz, pzr, 47, qo, mb, is_s)
                                    for tt_ in range(4):
                                        pz, pzr = tm_block(wt, wr_, tt_, 128, 256, 0)
                                        if is_s:
                                            copy("act", vo[:, tt_, 0:128], pz[:, 0:128], [pzr], [R("vo")])
                                        else:
                                            copy("act", vof[:, tt_, 0:128], pz[:, 0:128], [pzr], [R("xt")])
                                            copy("pool", vo[:, tt_, 0:128], vof[:, tt_, 0:128], [R("xt")], [R("vo")])
                                            pz2, pzr2 = tm_block(wt, wr_, tt_, 0, 128, 0)
                                            copy("act", kof[:, tt_, 0:128], pz2[:, 0:128], [pzr2], [R("xt")])
                                elif t == 4:
                                    for mb in (0, 1):
                                        pz, pzr = fm_block(wt, wr_, mb)
                                        qk_block(pz, pzr, 47, qo, mb, is_s)
                                    k.dma("sp", qT_d[2][0:256, t0:t0 + TC].rearrange("(fc p) t -> p fc t", p=128), qo[:, 2:4, :], reads=[R("qo")],
                                          writes=[R("qT_d", 2, c, 0)], sem="qost")
                                    k.dma("sp", qT_d[2][256:512, t0:t0 + TC].rearrange("(fc p) t -> p fc t", p=128), qo[:, 0:2, :], reads=[R("qo")],
                                          writes=[R("qT_d", 2, c, 1)], sem="qost")
                                    pz, pzr = fm_block(wt, wr_, 2)
                                    qk_block(pz, pzr, 48, qo, 2, is_s)
                                    k.dma("sp", kT_d[2][:, kcol:kcol + TC], qo[:, 2, :], reads=[R("qo")], writes=[R("kT_d", 2, c)], sem="qost")
                                    for tt_ in range(4):
                                        pz, pzr = tm_block(wt, wr_, tt_, 384, 512, 0)
                                        if is_s:
                                            copy("act", vo[:, tt_, 128:256], pz[:, 0:128], [pzr], [R("vo")])
                                        else:
                                            copy("act", vof[:, tt_, 128:256], pz[:, 0:128], [pzr], [R("xt")])
                                            copy("pool", vo[:, tt_, 128:256], vof[:, tt_, 128:256], [R("xt")], [R("vo")])
                                            pz2, pzr2 = tm_block(wt, wr_, tt_, 256, 384, 0)
                                            copy("act", kof[:, tt_, 128:256], pz2[:, 0:128], [pzr2], [R("xt")])
                                elif t == 6:
                                    for mb in range(4):
                                        pz, pzr = fm_block(wt, wr_, mb)
                                        qk_block(pz, pzr, 50, qo, mb, is_s)
                                    k.dma("sp", kT_d[3][:, kcol:kcol + TC].rearrange("(fc p) t -> p fc t", p=128), qo[:], reads=[R("qo")],
                                          writes=[R("kT_d", 3, c)], sem="qost")
                                    if not is_s:
                                        for tt_ in range(4):
                                            pz2, pzr2 = tm_block(wt, wr_, tt_, 0, 512, 0)
                                            copy("act", kof[:, tt_, 256:768], pz2[:, 0:512], [pzr2], [R("xt")])
                                else:
                                    for tt_ in range(4):
                                        pz, pzr = tm_block(wt, wr_, tt_, 0, 512, 0)
                                        if is_s:
                                            copy("act", vo[:, tt_, 256:768], pz[:, 0:512], [pzr], [R("vo")])
                                        else:
                                            copy("act", vof[:, tt_, 256:768], pz[:, 0:512], [pzr], [R("xt")])
                                            copy("pool", vo[:, tt_, 256:768], vof[:, tt_, 256:768], [R("xt")], [R("vo")])
                            chk(12)
                            for b, (c0, c1) in {1: (0, 128), 2: (128, 256), 3: (256, 768)}.items():
                                k.dma("sp", v_d[b][kcol:kcol + TC, :].rearrange("(tt p) f -> p tt f", p=128), vo[:, :, c0:c1], reads=[R("vo")],
                                      writes=[R("v_d", b, c)], sem="vost")
                            chk(13)
                            if not is_s:
                                s0 = (t0 - NS) // 256
                                for (b, odst, c0, c1) in ((1, o_wv, 0, 128), (2, o_gv, 128, 256), (3, o_dv, 256, 768)):
                                    for sq_ in range(2):
                                        k.dma("sp", odst[s0 + sq_, l].rearrange("(tt p) f -> p tt f", p=128), vof[:, 2 * sq_:2 * sq_ + 2, c0:c1],
                                              reads=[R("xt")], sem="ovst")
                                for tt_ in range(4):
                                    tt("pool", ksq[:], kof[:, tt_, :], kof[:, tt_, :], ALU.mult, [R("xt")], [R("ksq")])
                                    k.op("dve", lambda e: e.tensor_reduce(out=kss[:], in_=ksq[:].rearrange("p (h d) -> p h d", d=64), axis=AX.X, op=ALU.add),
                                         reads=[R("ksq")], writes=[R("kss")])
                                    act(kss[:], kss[:], AF.Sqrt, [R("kss")], [R("kss")], scale=1.0 / 64, bias=epsc[:])
                                    recip(kss[:], kss[:], [R("kss")], [R("kss")])
                                    for h in range(12):
                                        gi = 0 if h < 2 else (1 if h < 4 else 2)
                                        stt("dve", kof[:, tt_, h * 64:(h + 1) * 64], kof[:, tt_, h * 64:(h + 1) * 64], kss[:, h:h + 1], knbc[:, gi, :],
                                            ALU.mult, ALU.mult, [R("xt"), R("kss"), RL], [R("xt")])
                                for (b, odst, c0, c1) in ((1, o_wk, 0, 128), (2, o_gk, 128, 256), (3, o_dk, 256, 768)):
                                    for sq_ in range(2):
                                        k.dma("sp", odst[s0 + sq_, l].rearrange("(tt p) f -> p tt f", p=128), kof[:, 2 * sq_:2 * sq_ + 2, c0:c1],
                                              reads=[R("xt")], sem="ovst")

                    chk(2)
                    k.barrier()
                    seqs = [(0, NS, True, None)] + [(NS + s * 256, 256, False, s) for s in range(NPS)]
                    with contextlib.ExitStack() as p2:
                        LM = NS
                        Tm = [sbt(p2, U("lt%d" % i), [128, LM]) for i in range(6)]
                        ub = sbt(p2, U("ub"), [128, LM], BF16); ao = sbt(p2, U("ao"), [128, LM], BF16)
                        h0t = sbt(p2, U("h0t"), [128, 8])
                        h0s = sbt(p2, U("h0s"), [8, 128])
                        k.dma("sp", h0s[:], slru[l].rearrange("k (fc p) -> (k fc) p", p=128), writes=[R("h0s")], sem="p_h0s")
                        pbk, pr = pbank()
                        k.op("pe", lambda e: e.transpose(pbk[:, 0:8], h0s[:], identf[0:8, 0:8]), reads=[R("h0s"), RC], writes=[pr])
                        copy("dve", h0t[:], pbk[:, 0:8], [pr], [R("h0t")])
                        for (t0, L, is_s, sidx) in seqs:
                            all_c = range(t0 // TC, (t0 + L + TC - 1) // TC)
                            for fc in range(4):
                                xa, u, ra, ig, m4, ya = [t[:, 0:L] for t in Tm]
                                rd = [R("xyT_d", 0, c) for c in all_c]
                                k.dma("sp", xa, xaT_d[fc * 128:(fc + 1) * 128, t0:t0 + L], reads=rd, writes=[R("lt", 0)], sem="lru_in")
                                k.dma("sp", ya, yaT_d[fc * 128:(fc + 1) * 128, t0:t0 + L], reads=[R("xyT_d", 1, c) for c in all_c], writes=[R("lt", 5)], sem="lru_in2")
                                act(u, xa, AF.Identity, [R("lt", 0), RL], [R("lt", 1)], scale=T3[:, 4 + fc:5 + fc], bias=T3[:, 16 + fc:17 + fc])
                                stt("dve", u[:, 1:L], xa[:, 0:L - 1], T3[:, 0 + fc:1 + fc], u[:, 1:L], ALU.mult, ALU.add, [R("lt", 0), R("lt", 1), RL], [R("lt", 1)])
                                stt("dve", u[:, 0:L - 1], xa[:, 1:L], T3[:, 8 + fc:9 + fc], u[:, 0:L - 1], ALU.mult, ALU.add, [R("lt", 0), R("lt", 1), RL], [R("lt", 1)])
                                stt("dve", u[:, 0:L - 2], xa[:, 2:L], T3[:, 12 + fc:13 + fc], u[:, 0:L - 2], ALU.mult, ALU.add, [R("lt", 0), R("lt", 1), RL], [R("lt", 1)])
                                copy("pool", ub[:, 0:L], u, [R("lt", 1)], [R("ub")])
                                for kd in range(2):
                                    for n0 in range(0, L, TC):
                                        n1 = min(L, n0 + TC)
                                        for ri, dstt in ((0, ra), (1, ig)):
                                            idx = (fc * 2 + kd) * 2 + ri
                                            pz, pzr = pbank()
                                            mm(pz[:, 0:n1 - n0], [(bdw[:, idx, :], ub[:, n0:n1])], [RL, R("ub")], pzr)
                                            bcol = (20 if ri == 0 else 28) + kd * 4 + fc
                                            act(dstt[:, n0:n1], pz[:, 0:n1 - n0], AF.Sigmoid, [pzr, RL], [R("lt", 2 + ri)], bias=T3[:, bcol:bcol + 1])
                                    act(ra, ra, AF.Exp, [R("lt", 2), RL], [R("lt", 2)], scale=clam[:, kd * 4 + fc:kd * 4 + fc + 1])
                                    tt("pool", m4, ra, ra, ALU.mult, [R("lt", 2)], [R("lt", 4)])
                                    act(m4, m4, AF.Sqrt, [R("lt", 4)], [R("lt", 4)], scale=-1.0, bias=onec[:])
                                    tt("dve", ig, ig, m4, ALU.mult, [R("lt", 3), R("lt", 4)], [R("lt", 3)])
                                    tt("dve", ig, ig, u, ALU.mult, [R("lt", 3), R("lt", 1)], [R("lt", 3)])
                                    if kd == 0:
                                        init = h0t[:, fc:fc + 1] if is_s else 0.0
                                        k.op("dve", lambda e, init=init: e.tensor_tensor_scan(out=xa, data0=ra, data1=ig, initial=init, op0=ALU.mult, op1=ALU.add),
                                             reads=[R("lt", 2), R("lt", 3), R("h0t")], writes=[R("lt", 0)])
                                    else:
                                        init = h0t[:, 4 + fc:5 + fc] if is_s else 0.0
                                        k.op("dve", lambda e, init=init: e.tensor_tensor_scan(out=m4[:, ::-1], data0=ra[:, ::-1], data1=ig[:, ::-1], initial=init,
                                                                                              op0=ALU.mult, op1=ALU.add),
                                             reads=[R("lt", 2), R("lt", 3), R("h0t")], writes=[R("lt", 4)])
                                if not is_s:
                                    k.dma("sp", o_lru[sidx, l, 0, fc * 128:(fc + 1) * 128].rearrange("(p o) -> p o", o=1), xa[:, L - 1:L], reads=[R("lt", 0)], sem="olru")
                                    k.dma("sp", o_lru[sidx, l, 1, fc * 128:(fc + 1) * 128].rearrange("(p o) -> p o", o=1), m4[:, 0:1], reads=[R("lt", 4)], sem="olru")
                                tt("dve", xa, xa, m4, ALU.add, [R("lt", 0), R("lt", 4)], [R("lt", 0)])
                                tt("pool", ra, ya, ya, ALU.mult, [R("lt", 5)], [R("lt", 2)])
                                ts("dve", ra, ra, 0.044715, 1.0, ALU.mult, ALU.add, [R("lt", 2)], [R("lt", 2)])
                                tt("dve", ra, ra, ya, ALU.mult, [R("lt", 2), R("lt", 5)], [R("lt", 2)])
                                act(ra, ra, AF.Sigmoid, [R("lt", 2)], [R("lt", 2)], scale=2.0 * math.sqrt(2.0 / math.pi))
                                tt("pool", ra, ra, ya, ALU.mult, [R("lt", 2), R("lt", 5)], [R("lt", 2)])
                                tt("dve", ao[:, 0:L], xa, ra, ALU.mult, [R("lt", 0), R("lt", 2)], [R("ao")])
                                k.dma("sp", brT_d[fc * 128:(fc + 1) * 128, t0:t0 + L], ao[:, 0:L], reads=[R("ao")], writes=[R("brT_d", 0, fc, t0)], sem="aost")

                    chk(3)
                    k.barrier()
                    def key_reads(b, k0, n):
                        out_k, out_v = [], []
                        if k0 < NS:
                            for c in range(k0 // TC, (k0 + n + TC - 1) // TC):
                                out_k.append(R("kT_d", b, c)); out_v.append(R("v_d", b, c))
                            if k0 + n > NS:
                                out_k.append(R("kT_d", b, "ctx")); out_v.append(R("v_d", b, "ctx"))
                        else:
                            tok0 = NS + (k0 - KOFFP)
                            for c in range(tok0 // TC, (tok0 + n + TC - 1) // TC):
                                out_k.append(R("kT_d", b, c)); out_v.append(R("v_d", b, c))
                        return out_k, out_v

                    with contextlib.ExitStack() as p3:
                        PTn = 8
                        PTall = sbt(p3, U("ptall"), [128, PTn, 512], BF16)
                        PT = [PTall[:, i, :] for i in range(PTn)]
                        pp_i = [0]
                        pt_i = [0]
                        osb = sbt(p3, U("osb"), [128, 512]); osb2 = sbt(p3, U("osb2"), [128, 512]); rz = sbt(p3, U("rz"), [128, 512])
                        ostt = [sbt(p3, U("ost%d" % i), [128, 2, 512], BF16) for i in range(2)]
                        LOOK = [2]
                        SR = [0, 4]
                        pend = []

                        def _pop():
                            pv_, post_ = pend.pop(0)
                            pv_()
                            if post_ is not None:
                                post_()

                        def push(pv_, post_=None):
                            pend.append((pv_, post_))
                            while len(pend) > LOOK[0]:
                                _pop()

                        def flush():
                            while pend:
                                _pop()

                        def s_mm2(lk0, rq0, lk1, rq1, n, kread, qread):
                            npair = (SR[1] - SR[0]) // 2
                            b0 = SR[0] + 2 * (pp_i[0] % npair)
                            pp_i[0] += 1
                            mm(PB[b0][:, 0:n], [(lk0, rq0)], [kread, qread], R("pb", b0))
                            mm(PB[b0 + 1][:, 0:n], [(lk1, rq1)], [kread, qread], R("pb", b0 + 1))
                            return b0

                        def s_exp2(b0, n, masks2=(None, None)):
                            j = (pt_i[0] % (PTn // 2)) * 2
                            pt_i[0] += 1
                            act(PTall[:, j:j + 2, 0:n], PBall[:, b0:b0 + 2, 0:n], AF.Exp, [R("pb", b0), R("pb", b0 + 1)], [R("pt", j), R("pt", j + 1)])
                            for u in range(2):
                                if masks2[u] is not None:
                                    tt("pool", PT[j + u][:, 0:n], PT[j + u][:, 0:n], masks2[u], ALU.mult, [R("pt", j + u), R("masks")], [R("pt", j + u)])
                            return [(PT[j], R("pt", j)), (PT[j + 1], R("pt", j + 1))]

                        for (t0, L, is_s, sidx) in seqs:
                            k0 = 0 if is_s else KOFFP + (t0 - NS)
                            nk = (L + NCTX) if is_s else L
                            nkb = nk // 128
                            for b in (1, 2):
                                LOOK[0] = 4
                                SR[0], SR[1] = 0, 4
                                with contextlib.ExitStack() as pa:
                                    Kt = sbt(pa, U("Kt"), [128, nk], BF16)
                                    Va = sbt(pa, U("Va"), [128, nkb, 2, 65], BF16)
                                    Qts = [sbt(pa, U("Qt%d" % i), [128, 4, 128], BF16) for i in range(2)]
                                    kr, vr = key_reads(b, k0, nk)
                                    k.dma("sp", Kt[:], kT_d[b][:, k0:k0 + nk], reads=kr, writes=[R("Kt")], sem="kld")
                                    k.op("pool", lambda e: e.memset(Va[:, :, :, 64:65], 1.0), writes=[R("Va")])
                                    for hv in range(2):
                                        k.dma("sp", Va[:, :, hv, 0:64], v_d[b][k0:k0 + nk, hv * 64:(hv + 1) * 64].rearrange("(kb p) d -> p kb d", p=128), reads=vr, writes=[R("Va")], sem="vld")
                                    nqb = L // 128

                                    def load_q(qb):
                                        q0_ = t0 + qb * 128
                                        qc = q0_ // TC
                                        qrd = [R("qT_d", b, qc)] if b != 2 else [R("qT_d", 2, qc, 0), R("qT_d", 2, qc, 1)]
                                        for kvh_ in range(2):
                                            k.dma("sp", Qts[qb % 2][kvh_ * 64:(kvh_ + 1) * 64, :, :],
                                                  qT_d[b][kvh_ * 256:(kvh_ + 1) * 256, q0_:q0_ + 128].rearrange("(g d) q -> d g q", d=64),
                                                  reads=qrd, writes=[R("Qt", qb % 2)], sem="qld%d" % (qb % 2))
                                    load_q(0)
                                    for qb in range(nqb):
                                        if qb + 1 < nqb:
                                            load_q(qb + 1)
                                        q0 = t0 + qb * 128
                                        Qt = Qts[qb % 2]; RQ = R("Qt", qb % 2)
                                        ost = ostt[qb % 2]; Rost = R("ost", qb % 2)
                                        if b == 1 and is_s:
                                            kbs = [(kb, (0 if kb == qb - 1 else (1 if kb == qb + 1 else None))) for kb in (qb - 1, qb, qb + 1) if 0 <= kb < L // 128]
                                            kbs += [(L // 128 + i, None) for i in range(NCTX // 128)]
                                        else:
                                            kbs = [(kb, None) for kb in range(nkb)]
                                        for i, (kb, mi) in enumerate(kbs):
                                            b0_ = s_mm2(Kt[0:64, kb * 128:(kb + 1) * 128], Qt[0:64, :, :].rearrange("p g q -> p (g q)"),
                                                        Kt[64:128, kb * 128:(kb + 1) * 128], Qt[64:128, :, :].rearrange("p g q -> p (g q)"), 512, R("Kt"), RQ)
                                            mk_ = None if mi is None else masks[:, mi, :]
                                            pts_ = s_exp2(b0_, 512, (mk_, mk_))
                                            for kvh in range(2):
                                                pO, pOr = PB[4 + kvh], R("pb", 4 + kvh)
                                                ptile, ptr = pts_[kvh]
                                                last = (i == len(kbs) - 1)

                                                def pv_(pO=pO, pOr=pOr, kb=kb, kvh=kvh, ptile=ptile, ptr=ptr, i=i, last=last):
                                                    mm(pO[0:65, :], [(Va[:, kb, kvh, :], ptile[:])], [R("Va"), ptr], pOr, start=(i == 0), stop=last)
                                                post_ = None
                                                if last:
                                                    def post_(pO=pO, pOr=pOr, kvh=kvh, ost=ost, Rost=Rost, q0=q0, b=b, par=qb % 2):
                                                        copy("act", osb[0:65, :], pO[0:65, :], [pOr], [R("osb")])
                                                        if b == 1:
                                                            tt("dve", osb[64:65, :], osb[64:65, :], sinkT[64:65, kvh * 4:(kvh + 1) * 4, :].rearrange("p g q -> p (g q)"), ALU.add,
                                                               [R("osb"), RL], [R("osb")])
                                                        recip(rz[64:65, :], osb[64:65, :], [R("osb")], [R("rz")])
                                                        pZ, pZr = pbank(6, 8)
                                                        mm(pZ[0:64, :], [(onesf[64:65, 0:64], rz[64:65, :])], [RC, R("rz")], pZr)
                                                        tt("dve", ost[0:64, kvh, :], osb[0:64, :], pZ[0:64, :], ALU.mult, [R("osb"), pZr], [Rost])
                                                        if kvh == 1:
                                                            for kv2 in range(2):
                                                                k.dma("sp", brT_d[b * 512 + kv2 * 256:b * 512 + (kv2 + 1) * 256, q0:q0 + 128].rearrange("(g d) q -> d g q", d=64),
                                                                      ost[0:64, kv2, :].rearrange("p (g q) -> p g q", g=4), reads=[Rost], writes=[R("brT_d", b, q0, kv2)], sem="ost%d" % par)
                                                push(pv_, post_)
                                    flush()
                                k.barrier()
                            LOOK[0] = 2
                            SR[0], SR[1] = 4, 8
                            with contextlib.ExitStack() as pa:
                                Kt = sbt(pa, U("Ktd"), [128, 4, nk], BF16)
                                Vd = sbt(pa, U("Vd"), [128, nkb, 512], BF16)
                                Qds = [sbt(pa, U("Qd%d" % i), [128, 4, 512], BF16) for i in range(2)]
                                o32 = sbt(pa, U("o32"), [128, 512])
                                dsts = [sbt(pa, U("dst%d" % i), [128, 512], BF16) for i in range(2)]
                                kr, vr = key_reads(3, k0, nk)
                                k.dma("sp", Kt[:], kT_d[3][:, k0:k0 + nk].rearrange("(h p) t -> p h t", p=128), reads=kr, writes=[R("Ktd")], sem="kld")
                                k.dma("sp", Vd[:], v_d[3][k0:k0 + nk, :].rearrange("(kb p) f -> p kb f", p=128), reads=vr, writes=[R("Vd")], sem="vld")
                                QN = min(512, L)
                                nqq = L // QN

                                def load_qd(qq):
                                    q0_ = t0 + qq * QN
                                    k.dma("sp", Qds[qq % 2][:, :, 0:QN], qT_d[3][:, q0_:q0_ + QN].rearrange("(h p) t -> p h t", p=128), reads=[R("qT_d", 3, q0_ // TC)],
                                          writes=[R("Qd", qq % 2)], sem="qdl%d" % (qq % 2))
                                load_qd(0)
                                hcnt = [0]
                                for qq in range(nqq):
                                    if qq + 1 < nqq:
                                        load_qd(qq + 1)
                                    q0 = t0 + qq * QN
                                    Qd = Qds[qq % 2]; RQd = R("Qd", qq % 2)
                                    for h in range(4):
                                        accs = [(PB[0], R("pb", 0), PB[1], R("pb", 1)), (PB[2], R("pb", 2), PB[3], R("pb", 3))]
                                        for kb in range(nkb):
                                            b0_ = s_mm2(Kt[0:64, h, kb * 128:(kb + 1) * 128], Qd[0:64, h, 0:QN], Kt[64:128, h, kb * 128:(kb + 1) * 128], Qd[64:128, h, 0:QN],
                                                        QN, R("Ktd"), RQd)
                                            pts_ = s_exp2(b0_, QN)
                                            for jm in range(2):
                                                pO, pOr, pZ, pZr = accs[jm]
                                                ptile, ptr = pts_[jm]

                                                def pv_(pO=pO, pOr=pOr, pZ=pZ, pZr=pZr, kb=kb, h=h, ptile=ptile, ptr=ptr):
                                                    mm(pO[:, 0:QN], [(Vd[:, kb, h * 128:(h + 1) * 128], ptile[:, 0:QN])], [R("Vd"), ptr], pOr, start=(kb == 0), stop=(kb == nkb - 1))
                                                    mm(pZ[:, 0:QN], [(onesb[:], ptile[:, 0:QN])], [R("onesb"), ptr], pZr, start=(kb == 0), stop=(kb == nkb - 1))
                                                post_ = None
                                                if kb == nkb - 1 and jm == 1:
                                                    def post_(accs=accs, h=h, q0=q0):
                                                        di = hcnt[0] % 2
                                                        hcnt[0] += 1
                                                        dst_ = dsts[di]
                                                        recip(rz[:, 0:QN], accs[0][2][:, 0:QN], [accs[0][3]], [R("rz")])
                                                        tt("dve", osb[:, 0:QN], accs[0][0][:, 0:QN], rz[:, 0:QN], ALU.mult, [accs[0][1], R("rz")], [R("osb")])
                                                        recip(rz[:, 0:QN], accs[1][2][:, 0:QN], [accs[1][3]], [R("rz")])
                                                        tt("dve", osb2[:, 0:QN], accs[1][0][:, 0:QN], rz[:, 0:QN], ALU.mult, [accs[1][1], R("rz")], [R("osb2")])
                                                        stt("dve", o32[:, 0:QN], osb2[:, 0:QN], lamt[:, 1:2], osb[:, 0:QN], ALU.mult, ALU.add, [R("osb"), R("osb2"), RL], [R("o32")])
                                                        tt("pool", osb[:, 0:QN], o32[:, 0:QN], o32[:, 0:QN], ALU.mult, [R("o32")], [R("osb")])
                                                        pM, pMr = pbank(4, 8)
                                                        mm(pM[:, 0:QN], [(avg128, osb[:, 0:QN])], [RC, R("osb")], pMr)
                                                        act(rz[:, 0:QN], pM[:, 0:QN], AF.Ln, [pMr], [R("rz")], bias=epsc[:])
                                                        act(rz[:, 0:QN], rz[:, 0:QN], AF.Exp, [R("rz")], [R("rz")], scale=-0.5)
                                                        stt("dve", dst_[:, 0:QN], o32[:, 0:QN], lamt[:, 2:3], rz[:, 0:QN], ALU.mult, ALU.mult, [R("o32"), R("rz"), RL], [R("dst", di)])
                                                        k.dma("sp", brT_d[3 * 512 + h * 128:3 * 512 + (h + 1) * 128, q0:q0 + QN], dst_[:, 0:QN], reads=[R("dst", di)],
                                                              writes=[R("brT_d", 3, q0, h)], sem="dst%d" % di)
                                                push(pv_, post_)
                                flush()
                            k.barrier()

                    chk(4)
                    k.barrier()
                    def br_reads(c):
                        t0 = c * TC
                        rr = []
                        if c < NSCH:
                            rr += [R("brT_d", 0, fc, 0) for fc in range(4)]
                        else:
                            for s in range(NPS):
                                if NS + s * 256 >= t0 and NS + s * 256 < t0 + TC:
                                    rr += [R("brT_d", 0, fc, NS + s * 256) for fc in range(4)]
                        for q0 in range(t0, t0 + TC, 128):
                            rr += [R("brT_d", b, q0, kvh) for b in (1, 2) for kvh in range(2)]
                        for h in range(4):
                            if c < NSCH:
                                rr.append(R("brT_d", 3, t0, h))
                            else:
                                rr += [R("brT_d", 3, t0, h), R("brT_d", 3, t0 + 256, h)]
                        return rr

                    with contextlib.ExitStack() as p4:
                        hT = sbt(p4, U("hT3"), [128, KC, TC], BF16); bT = sbt(p4, U("bT"), [128, KC, TC], BF16)
                        mT = sbt(p4, U("mT"), [128, KC, TC], BF16)
                        xt = sbt(p4, U("xt3"), [128, 4, D]); xn = bT[:].rearrange("p (a e) b -> p a (e b)", a=4)
                        gbc = sbt(p4, U("gbc"), [128, D])
                        acc = sbt(p4, U("acc"), [128, 4, TC]); sg = sbt(p4, U("sg"), [128, TC]); tmp = sbt(p4, U("tmp"), [128, TC])
                        ssq = sbt(p4, U("ssq3"), [128, 8])
                        wbrr = [sbt(p4, U("wbrr%d" % i_), [128, 4, 512], BF16) for i_ in range(2)]
                        wbr_i = [0]
                        items = []
                        for c in range(NCH):
                            for mg in range(4):
                                for b in range(4):
                                    items.append(("gate", l, 0, (b * 4 + mg) * 512))
                            for ct in range(4):
                                items.append(("o", l, 0, ct * 512))
                        ws = WStream(items)
                        cur_j = -1
                        for c in range(NCH):
                            j = 0 if c < NSCH else 1
                            t0 = c * TC
                            if j != cur_j:
                                k.dma("sp", gbc[:], gbc_d[0, j], reads=[R("gbc", 0, j)], writes=[R("gbcs")], sem="gbl")
                                cur_j = j
                            k.dma("sp", hT[:], hT_d[:, t0:t0 + TC].rearrange("(kc p) t -> p kc t", p=128), reads=[R("hT_d", c)], writes=[R("hT3")], sem="hTld")
                            k.dma("sp", bT[:], brT_d[:, t0:t0 + TC].rearrange("(kc p) t -> p kc t", p=128), reads=br_reads(c), writes=[R("bT")], sem="bTld")
                            xsrc_, xdst_ = xsrc(l, c)
                            Rx = R("xres", c)
                            k.dma("sp", xt[:], xsrc_, reads=[Rx], writes=[R("xt3")], sem="xt3")
                            for mg in range(4):
                                for b in range(4):
                                    wg_, wgr_ = ws.get()
                                    si_ = wbr_i[0] % 2
                                    wbr_i[0] += 1
                                    wb_, wbr_ = wbrr[si_], R("wbrr", si_)
                                    k.dma("sp", wb_[:], wbf["br"][l, b * 512:(b + 1) * 512, mg * 512:(mg + 1) * 512].rearrange("(kc p) n -> p kc n", p=128),
                                          reads=wres("br", l), writes=[wbr_], sem="wbr%d" % si_)
                                    for mm_ in range(4):
                                        m = mg * 4 + mm_
                                        pg, pgr = pbank()
                                        mm(pg[:], [(wg_[:, kc, mm_ * 128:(mm_ + 1) * 128], hT[:, kc, :]) for kc in range(KC)], [wgr_, R("hT3")], pgr)
                                        pp, ppr = pbank()
                                        mm(pp[:], [(wb_[:, k4, mm_ * 128:(mm_ + 1) * 128], bT[:, b * 4 + k4, :]) for k4 in range(4)], [wbr_, R("bT")], ppr)
                                        act(sg[:], pg[:], AF.Sigmoid, [pgr, RL], [R("sg")], bias=T2[:, b * 16 + m:b * 16 + m + 1])
                                        if b == 0:
                                            tt("dve", acc[:, mm_, :], sg[:], pp[:], ALU.mult, [R("sg"), ppr], [R("acc", mm_)])
                                        else:
                                            tt("dve", tmp[:], sg[:], pp[:], ALU.mult, [R("sg"), ppr], [R("tmp")])
                                            if b < 3:
                                                tt("pool", acc[:, mm_, :], acc[:, mm_, :], tmp[:], ALU.add, [R("acc", mm_), R("tmp")], [R("acc", mm_)])
                                            else:
                                                tt("pool", mT[:, m, :], acc[:, mm_, :], tmp[:], ALU.add, [R("acc", mm_), R("tmp")], [R("mT")])
                            for ct in range(4):
                                wo_, wor_ = ws.get()
                                for tt_ in range(4):
                                    po, por = pbank()
                                    mm(po[:], [(mT[:, kc, tt_ * 128:(tt_ + 1) * 128], wo_[:, kc, :]) for kc in range(KC)], [wor_, R("mT")], por)
                                    tt("dve", tmp[:], po[:], gbc[:, ct * 512:(ct + 1) * 512], ALU.mult, [por, R("gbcs")], [R("tmp")])
                                    tt("pool", xt[:, tt_, ct * 512:(ct + 1) * 512], xt[:, tt_, ct * 512:(ct + 1) * 512], tmp[:], ALU.add, [R("xt3"), R("tmp")], [R("xt3")])
                            k.dma("sp", xdst_, xt[:], reads=[R("xt3")], writes=[Rx], sem="x1st")
                            for tt_ in range(4):
                                accf = acc[:].rearrange("p a b -> p (a b)")
                                tt("dve", accf, xt[:, tt_, :], xt[:, tt_, :], ALU.mult, [R("xt3")], [R("acc", i_) for i_ in range(4)])
                                k.op("dve", lambda e, tt_=tt_, accf=accf: e.tensor_reduce(out=ssq[:, tt_:tt_ + 1], in_=accf, axis=AX.X, op=ALU.add),
                                     reads=[R("acc", i_) for i_ in range(4)], writes=[R("ssq3")])
                            act(ssq[:, 4:8], ssq[:, 0:4], AF.Sqrt, [R("ssq3")], [R("ssq3")], scale=1.0 / D, bias=epsc[:])
                            recip(ssq[:, 4:8], ssq[:, 4:8], [R("ssq3")], [R("ssq3")])
                            for tt_ in range(4):
                                ts("dve", xn[:, tt_, :], xt[:, tt_, :], ssq[:, 4 + tt_:5 + tt_], None, ALU.mult, None, [R("xt3"), R("ssq3")], [R("bT")])
                            for kc in range(KC):
                                pk, pkr = pbank()
                                pkb = pk[:].bitcast(BF16)
                                def fnT(e, pkb=pkb, kc=kc):
                                    ins = None
                                    for tt_ in range(4):
                                        ins = e.transpose(pkb[:, tt_ * 128:(tt_ + 1) * 128], xn[:, tt_, kc * 128:(kc + 1) * 128], idb[:])
                                    return ins
                                k.op("pe", fnT, reads=[R("bT"), R("idb")], writes=[pkr])
                                act(hT[:, kc, :], pkb[:, 0:TC], AF.Identity, [pkr, RL], [R("hT3")], scale=A2[:, kc, j:j + 1], bias=modT[:, 48 + kc, j:j + 1])
                            k.dma("sp", hT_d[:, t0:t0 + TC].rearrange("(kc p) t -> p kc t", p=128), hT[:], reads=[R("hT3")], writes=[R("hT_d", c)], sem="hTst")

                    chk(5)
                    k.barrier()
                    with contextlib.ExitStack() as p5:
                        hT = sbt(p5, U("hT5"), [128, KC, TC], BF16)
                        aT = sbt(p5, U("aT"), [128, 64, TC], BF16)
                        rl = sbt(p5, U("rl"), [128, TC])
                        gbc = sbt(p5, U("gbc5"), [128, D])
                        xc = sbt(p5, U("xc"), [128, 4, 512]); tmp = sbt(p5, U("tmp5"), [128, 512])
                        items = []
                        for c in range(NCH):
                            for ft in range(16):
                                items.append(("ff1", l, 0, ft * 512))
                            for ct in range(4):
                                for g in range(4):
                                    items.append(("ff2", l, g * 2048, ct * 512))
                        ws = WStream(items)
                        cur_j = -1
                        for c in range(NCH):
                            j = 0 if c < NSCH else 1
                            t0 = c * TC
                            if j != cur_j:
                                k.dma("sp", gbc[:], gbc_d[1, j], reads=[R("gbc", 1, j)], writes=[R("gbc5")], sem="gbl")
                                cur_j = j
                            k.dma("sp", hT[:], hT_d[:, t0:t0 + TC].rearrange("(kc p) t -> p kc t", p=128), reads=[R("hT_d", c)], writes=[R("hT5")], sem="hTld")
                            _, xdst_ = xsrc(l, c)
                            Rx = R("xres", c)
                            for ft in range(16):
                                w1, w1r = ws.get()
                                for mm_ in range(4):
                                    f = ft * 4 + mm_
                                    pf, pfr = pbank()
                                    mm(pf[:], [(w1[:, kc, mm_ * 128:(mm_ + 1) * 128], hT[:, kc, :]) for kc in range(KC)], [w1r, R("hT5")], pfr)
                                    act(rl[:], pf[:], AF.Relu, [pfr, RL], [R("rl")], bias=T2[:, 64 + f:65 + f])
                                    tt("dve" if mm_ % 2 == 0 else "pool", aT[:, f, :], rl[:], rl[:], ALU.mult, [R("rl")], [R("aT", f)])
                            for ct in range(4):
                                k.dma("sp", xc[:], xdst_[:, :, ct * 512:(ct + 1) * 512], reads=[Rx], writes=[R("xc")], sem="xc")
                                banks = [pbank() for _ in range(4)]
                                for g in range(4):
                                    w2, w2r = ws.get()
                                    for tt_ in range(4):
                                        po, por = banks[tt_]
                                        mm(po[:], [(aT[:, g * 16 + kc, tt_ * 128:(tt_ + 1) * 128], w2[:, kc, :]) for kc in range(KC)],
                                           [w2r] + [R("aT", g * 16 + kc) for kc in range(KC)], por, start=(g == 0), stop=(g == 3))
                                for tt_ in range(4):
                                    po, por = banks[tt_]
                                    tt("dve", tmp[:], po[:], b2bc[:, ct * 512:(ct + 1) * 512], ALU.add, [por, RL], [R("tmp5")])
                                    tt("pool", tmp[:], tmp[:], gbc[:, ct * 512:(ct + 1) * 512], ALU.mult, [R("tmp5"), R("gbc5")], [R("tmp5")])
                                    tt("dve", xc[:, tt_, :], xc[:, tt_, :], tmp[:], ALU.add, [R("xc"), R("tmp5")], [R("xc")])
                                k.dma("sp", xdst_[:, :, ct * 512:(ct + 1) * 512], xc[:], reads=[R("xc")], writes=[Rx], sem="x2st")
        cast_layer(0)
        try:
            _layers()
        except _Stop:
            pass
        k.finish()
    nc._n_inst = k.n_inst
    return nc


def host_consts(NS):
    n_freq = 16
    t = np.arange(NS)
    row = (t // GRID_W).astype(np.float32)
    col = (t % GRID_W).astype(np.float32)
    inv = (10000.0 ** (-np.arange(n_freq, dtype=np.float32) / n_freq)).astype(np.float32)
    C = np.zeros((128, NS), np.float32)
    S = np.zeros((128, NS), np.float32)
    for p in range(128):
        d = p % 64
        axis, half, fr = d // 32, (d % 32) // 16, d % 16
        pos = row if axis == 0 else col
        ang = (pos * inv[fr]).astype(np.float32)
        C[p] = np.cos(ang)
        S[p] = np.sin(ang) * (-1.0 if half == 0 else 1.0)
    cm = np.zeros((5, 128, 128), np.float32)
    cm[0] = np.eye(128)
    for p in range(128):
        d = p % 64
        half = (d % 32) // 16
        cm[1][p, p + 16 if half == 0 else p - 16] = 1.0
    for hh in range(2):
        cm[2][hh * 64:(hh + 1) * 64, hh * 64:(hh + 1) * 64] = 1.0 / 64
    cm[3][:] = 1.0 / 128
    cm[4][:] = 1.0
    kk = np.arange(128)[:, None]
    qq = np.arange(128)[None, :]
    mA = (kk >= qq).astype(np.float32)
    mB = (kk <= qq).astype(np.float32)
    cmask = np.stack([np.tile(mA, (1, 4)), np.tile(mB, (1, 4))]).astype(ml_dtypes.bfloat16)
    cidb = np.eye(128, dtype=np.float32).astype(ml_dtypes.bfloat16)
    return {"ropeC": C, "ropeS": S, "cmat": cm, "cmask": cmask, "cidb": cidb}


WEIGHT_KEYS = ["w_ada", "b_ada", "norm1_g", "norm2_g", "w_in", "conv_w", "conv_b", "lru_wr", "lru_br", "lru_wi", "lru_bi",
               "lru_lambda", "win_qn", "win_kn", "win_sink", "grid_qn", "grid_kn", "diff_qn", "diff_kn", "diff_lq1",
               "diff_lk1", "diff_lq2", "diff_lk2", "diff_out_g", "w_branch", "w_gate", "b_gate", "w_o", "w_ff1", "b_ff1",
               "w_ff2", "b_ff2"]


def run(inputs, n_cores, NS, NPS, DEPTH):
    f = lambda a: np.ascontiguousarray(np.asarray(a, dtype=np.float32))
    inp = {kk: f(v) for kk, v in inputs.items()}
    nb_s = inp["x_sample"].shape[0]
    nc = build_nc(NS, NPS, DEPTH)
    consts = host_consts(NS)
    in_maps = []
    for core in range(n_cores):
        b = core % nb_s
        m = {kk: inp[kk] for kk in WEIGHT_KEYS}
        m.update(consts)
        m["xs"] = inp["x_sample"][b]
        m["xp"] = f(inp["x_prompt"][core * NPS:(core + 1) * NPS].reshape(NPS * 256, D))
        m["cwk"] = f(inp["cache_win_k"][b].reshape(DEPTH, NCTX, 128)); m["cwv"] = f(inp["cache_win_v"][b].reshape(DEPTH, NCTX, 128))
        m["cgk"] = f(inp["cache_grid_k"][b].reshape(DEPTH, NCTX, 128)); m["cgv"] = f(inp["cache_grid_v"][b].reshape(DEPTH, NCTX, 128))
        m["cdk"] = f(inp["cache_diff_k"][b].reshape(DEPTH, NCTX, 512)); m["cdv"] = f(inp["cache_diff_v"][b].reshape(DEPTH, NCTX, 512))
        m["slru"] = f(inp["state_lru"][b])
        m["cond"] = f(np.stack([inp["c"][b], inp["c_ctx"]]))
        in_maps.append(m)
    res = run_bass_kernel_spmd(nc, in_maps, core_ids=list(range(n_cores)))
    rs = res.results
    LAST[0] = rs
    y_p = np.concatenate([r["y_p"].reshape(NPS, 256, D) for r in rs], axis=0)
    y_s = np.stack([rs[b]["y_s"] for b in range(nb_s)], axis=0)
    cat = lambda kk, shp: np.concatenate([r[kk].reshape((NPS, DEPTH, 256) + shp) for r in rs], axis=0)
    outs = (y_p, y_s, cat("o_wk", (2, 64)), cat("o_wv", (2, 64)), cat("o_gk", (2, 64)), cat("o_gv", (2, 64)),
            cat("o_dk", (4, 2, 64)), cat("o_dv", (4, 128)),
            np.concatenate([r["o_lru"].reshape(NPS, DEPTH, 2, 512) for r in rs], axis=0))
    return tuple(np.ascontiguousarray(o, dtype=np.float32) for o in outs)


def kernel(**inputs):
    return run(inputs, 8, 4096, 4, 4)
```
